# Optimizing a Trainium2 kernel written in Bass

```python
import jax, jax.numpy as jnp
from jax import lax
import numpy as np

D_MODEL = 1024
BATCH = 4
SEQ = 4096
DEPTH = 2

CTX_LEN = 256
GRID_W = 64
EPS = 1e-6
MIX_WIDTH = 2 * D_MODEL
SSD_WIDTH = MIX_WIDTH // 2
SSD_HEAD_DIM = 64
SSD_HEADS = SSD_WIDTH // SSD_HEAD_DIM
SSD_GROUPS = 2
SSD_HPG = SSD_HEADS // SSD_GROUPS
SSD_STATE = 128
SSD_CHUNK = 128
SSD_CONV = 3
SSD_XBC = SSD_WIDTH + 2 * SSD_GROUPS * SSD_STATE
SSD_COLS = SSD_WIDTH + SSD_XBC + 2 * SSD_HEADS
SC_WIDTH = MIX_WIDTH // 4
SC_CONV = 3
SC_COLS = 4 * SC_WIDTH
ATTN_HEAD_DIM = 64
ATTN_WIDTH = MIX_WIDTH // 4
ATTN_HEADS = ATTN_WIDTH // ATTN_HEAD_DIM
ATTN_KV_HEADS = 2
ATTN_REP = ATTN_HEADS // ATTN_KV_HEADS
ATTN_WINDOW = 128
ATTN_BLOCK = 128
ROPE_BASE = 10000.0
ATTN_COLS = 2 * ATTN_WIDTH + 2 * ATTN_KV_HEADS * ATTN_HEAD_DIM
IN_COLS = SSD_COLS + SC_COLS + ATTN_COLS
NEG_INF = -1e30

kernel_name = "hybrid_ssd_shortconv_swa_dit_block"


def rmsnorm(t, w):
    t32 = t.astype(jnp.float32)
    t32 = t32 * lax.rsqrt(jnp.mean(t32 * t32, axis=-1, keepdims=True) + EPS)
    return (t32 * w.astype(jnp.float32)).astype(t.dtype)


def dwconv(u, w):
    k, ch = w.shape
    return lax.conv_general_dilated(
        u, w[:, None, :].astype(u.dtype), window_strides=(1,), padding=[(k // 2, k // 2)],
        dimension_numbers=("NWC", "WIO", "NWC"), feature_group_count=ch)


def segsum(a):
    t = a.shape[-1]
    cs = jnp.cumsum(a, axis=-1)
    diff = cs[..., :, None] - cs[..., None, :]
    mask = jnp.tril(jnp.ones((t, t), dtype=bool))
    return jnp.where(mask, diff, -jnp.inf)


def ssd_scan(xs, da, bm, cm, init, with_y):
    b, l, g, r, p = xs.shape
    n = bm.shape[-1]
    nc = l // SSD_CHUNK
    xs = xs.reshape(b, nc, SSD_CHUNK, g, r, p)
    bm = bm.reshape(b, nc, SSD_CHUNK, g, n)
    cm = cm.reshape(b, nc, SSD_CHUNK, g, n)
    da = da.reshape(b, nc, SSD_CHUNK, g, r).transpose(0, 3, 4, 1, 2)
    a_cum = jnp.cumsum(da, axis=-1)
    decay_states = jnp.exp(a_cum[..., -1:] - a_cum)
    states = jnp.einsum("bclgn,bgrcl,bclgrp->bcgrpn", bm, decay_states, xs)
    chunk_tot = jnp.pad(a_cum[..., -1], ((0, 0), (0, 0), (0, 0), (1, 0)))
    decay_chunk = jnp.exp(segsum(chunk_tot))
    states = jnp.concatenate([init[:, None], states], axis=1)
    new_states = jnp.einsum("bgrzc,bcgrpn->bzgrpn", decay_chunk, states)
    final = new_states[:, -1]
    if not with_y:
        return None, final
    states = new_states[:, :-1]
    lmat = jnp.exp(segsum(da))
    cb = jnp.einsum("bclgn,bcsgn->bgcls", cm, bm)
    y_diag = jnp.einsum("bgcls,bgrcls,bcsgrp->bclgrp", cb, lmat, xs)
    y_off = jnp.einsum("bclgn,bcgrpn,bgrcl->bclgrp", cm, states, jnp.exp(a_cum))
    return (y_diag + y_off).reshape(b, l, g, r, p), final


def _flip(t, direction):
    return t[:, ::-1] if direction == 1 else t


def ssd_branch(pc, pl, conv_w, conv_b, dt_bias, a_log, d_skip, norm_w, with_ctx):
    def prep(p):
        b, l, _ = p.shape
        z, xbc, dt = jnp.split(p, [SSD_WIDTH, SSD_WIDTH + SSD_XBC], axis=-1)
        xbc = jax.nn.silu(dwconv(xbc, conv_w) + conv_b.astype(xbc.dtype))
        xs, bm, cm = jnp.split(xbc, [SSD_WIDTH, SSD_WIDTH + SSD_GROUPS * SSD_STATE], axis=-1)
        return (z, xs.reshape(b, l, SSD_GROUPS, SSD_HPG, SSD_HEAD_DIM),
                bm.reshape(b, l, SSD_GROUPS, SSD_STATE), cm.reshape(b, l, SSD_GROUPS, SSD_STATE), dt)

    zc, xc, bc, cc, dtc = prep(pc)
    zl, xl, bl, cl, dtl = prep(pl)
    b = pl.shape[0]
    dsk = d_skip.astype(jnp.float32).reshape(SSD_GROUPS, SSD_HPG, 1)
    y_l = xl.astype(jnp.float32) * dsk
    y_c = xc.astype(jnp.float32) * dsk
    for d in range(2):
        a = -jnp.exp(a_log[d].astype(jnp.float32)).reshape(SSD_GROUPS, SSD_HPG)

        def disc(dt_raw):
            dt = jax.nn.softplus(dt_raw[..., d * SSD_HEADS:(d + 1) * SSD_HEADS].astype(jnp.float32)
                                 + dt_bias[d].astype(jnp.float32))
            return dt.reshape(dt.shape[0], dt.shape[1], SSD_GROUPS, SSD_HPG)

        dt_c, dt_l = disc(dtc), disc(dtl)
        init = jnp.zeros((b, SSD_GROUPS, SSD_HPG, SSD_HEAD_DIM, SSD_STATE), jnp.float32)
        yc_d, s_ctx = ssd_scan(_flip(xc * dt_c[..., None], d), _flip(dt_c * a, d),
                               _flip(bc, d), _flip(cc, d), init, with_ctx)
        yl_d, _ = ssd_scan(_flip(xl * dt_l[..., None], d), _flip(dt_l * a, d),
                           _flip(bl, d), _flip(cl, d), s_ctx, True)
        y_l = y_l + _flip(yl_d, d)
        if with_ctx:
            y_c = y_c + _flip(yc_d, d)

    def gated_norm(y, z):
        bb, l = y.shape[:2]
        g = y.reshape(bb, l, SSD_GROUPS, -1) * jax.nn.silu(z.astype(jnp.float32)).reshape(bb, l, SSD_GROUPS, -1)
        g = g * lax.rsqrt(jnp.mean(g * g, axis=-1, keepdims=True) + EPS)
        return (g.reshape(bb, l, SSD_WIDTH) * norm_w.astype(jnp.float32)).astype(z.dtype)

    out_c = gated_norm(y_c, zc) if with_ctx else None
    return out_c, gated_norm(y_l, zl)


def shortconv_branch(pc, pl, conv_w, with_ctx):
    def run(p):
        v, cg, bg, z = jnp.split(p, 4, axis=-1)
        return bg * dwconv(cg * v, conv_w) * jax.nn.silu(z)
    return (run(pc) if with_ctx else None), run(pl)


def rope_tables(n_lat):
    n_rows = n_lat // GRID_W
    rows = jnp.repeat(jnp.arange(n_rows, dtype=jnp.int32), GRID_W).astype(jnp.float32)
    cols = jnp.tile(jnp.arange(GRID_W, dtype=jnp.int32), n_rows).astype(jnp.float32)
    axis_dim = ATTN_HEAD_DIM // 2
    inv_freq = ROPE_BASE ** (-jnp.arange(0, axis_dim, 2, dtype=jnp.float32) / axis_dim)
    ang = jnp.concatenate([rows[:, None] * inv_freq, cols[:, None] * inv_freq], axis=-1)
    return jnp.cos(ang), jnp.sin(ang)


def apply_rope(t, cos, sin):
    b, l, h, d = t.shape
    q = d // 4
    t = t.reshape(b, l, h, 2, 2, q)
    x1, x2 = t[..., 0, :], t[..., 1, :]
    c = cos.reshape(l, 1, 2, q).astype(t.dtype)
    s = sin.reshape(l, 1, 2, q).astype(t.dtype)
    return jnp.stack([x1 * c - x2 * s, x1 * s + x2 * c], axis=-2).reshape(b, l, h, d)


def attn_branch(pc, pl, sink, cos, sin, with_ctx):
    scale = ATTN_HEAD_DIM ** -0.5

    def split(p):
        b, l, _ = p.shape
        q, k, v, z = jnp.split(p, [ATTN_WIDTH, ATTN_WIDTH + ATTN_KV_HEADS * ATTN_HEAD_DIM,
                                   ATTN_WIDTH + 2 * ATTN_KV_HEADS * ATTN_HEAD_DIM], axis=-1)
        return (q.reshape(b, l, ATTN_HEADS, ATTN_HEAD_DIM), k.reshape(b, l, ATTN_KV_HEADS, ATTN_HEAD_DIM),
                v.reshape(b, l, ATTN_KV_HEADS, ATTN_HEAD_DIM), z)

    qc, kc, vc, zc = split(pc)
    ql, kl, vl, zl = split(pl)
    ql, kl = apply_rope(ql, cos, sin), apply_rope(kl, cos, sin)
    b, n_lat = pl.shape[:2]
    nb = n_lat // ATTN_BLOCK
    sink32 = sink.astype(jnp.float32).reshape(ATTN_KV_HEADS, ATTN_REP)

    qb = ql.reshape(b, nb, ATTN_BLOCK, ATTN_KV_HEADS, ATTN_REP, ATTN_HEAD_DIM)

    def windows(t):
        tp = jnp.pad(t.reshape(b, nb, ATTN_BLOCK, ATTN_KV_HEADS, ATTN_HEAD_DIM),
                     ((0, 0), (1, 1), (0, 0), (0, 0), (0, 0)))
        return jnp.concatenate([tp[:, :-2], tp[:, 1:-1], tp[:, 2:]], axis=2)

    kw, vw = windows(kl), windows(vl)
    s_band = jnp.einsum("bnqkrd,bnskd->bnkrqs", qb, kw).astype(jnp.float32) * scale
    qpos = jnp.arange(nb)[:, None] * ATTN_BLOCK + jnp.arange(ATTN_BLOCK)[None, :]
    kpos = (jnp.arange(nb)[:, None] - 1) * ATTN_BLOCK + jnp.arange(3 * ATTN_BLOCK)[None, :]
    valid = ((jnp.abs(qpos[:, :, None] - kpos[:, None, :]) <= ATTN_WINDOW)
             & (kpos[:, None, :] >= 0) & (kpos[:, None, :] < n_lat))
    s_band = jnp.where(valid[None, :, None, None], s_band, NEG_INF)
    s_ctx = jnp.einsum("bnqkrd,bskd->bnkrqs", qb, kc).astype(jnp.float32) * scale
    s_sink = jnp.broadcast_to(sink32[None, None, :, :, None, None], s_band.shape[:-1] + (1,))
    probs = jax.nn.softmax(jnp.concatenate([s_band, s_ctx, s_sink], axis=-1), axis=-1)
    p_band = probs[..., :3 * ATTN_BLOCK].astype(vl.dtype)
    p_ctx = probs[..., 3 * ATTN_BLOCK:3 * ATTN_BLOCK + kc.shape[1]].astype(vl.dtype)
    o = (jnp.einsum("bnkrqs,bnskd->bnqkrd", p_band, vw)
         + jnp.einsum("bnkrqs,bskd->bnqkrd", p_ctx, vc))
    out_l = o.reshape(b, n_lat, ATTN_WIDTH) * jax.nn.silu(zl)

    out_c = None
    if with_ctx:
        n_ctx = pc.shape[1]
        qcg = qc.reshape(b, n_ctx, ATTN_KV_HEADS, ATTN_REP, ATTN_HEAD_DIM)
        sc = jnp.einsum("bqkrd,bskd->bkrqs", qcg, kc).astype(jnp.float32) * scale
        ssk = jnp.broadcast_to(sink32[None, :, :, None, None], sc.shape[:-1] + (1,))
        pcp = jax.nn.softmax(jnp.concatenate([sc, ssk], axis=-1), axis=-1)[..., :n_ctx].astype(vc.dtype)
        oc = jnp.einsum("bkrqs,bskd->bqkrd", pcp, vc)
        out_c = oc.reshape(b, n_ctx, ATTN_WIDTH) * jax.nn.silu(zc)
    return out_c, out_l


def hybrid_layer(x, ctx, silu_c, silu_cc, cos, sin, norm_w, w_mod, b_mod, w_in, ssd_conv_w, ssd_conv_b,
                 ssd_dt_bias, ssd_a_log, ssd_d, ssd_norm_w, sc_conv_w, attn_sink, w_out, with_ctx):
    shift, scale, gate = jnp.split(silu_c @ w_mod + b_mod, 3, axis=-1)
    shift_c, scale_c, gate_c = jnp.split(silu_cc @ w_mod + b_mod, 3, axis=-1)
    h = rmsnorm(x, norm_w) * (1 + scale[:, None]) + shift[:, None]
    hc = rmsnorm(ctx, norm_w) * (1 + scale_c) + shift_c
    pl, pc = h @ w_in, hc @ w_in
    ssd_l, sc_l, at_l = jnp.split(pl, [SSD_COLS, SSD_COLS + SC_COLS], axis=-1)
    ssd_c, sc_c, at_c = jnp.split(pc, [SSD_COLS, SSD_COLS + SC_COLS], axis=-1)
    ys_c, ys_l = ssd_branch(ssd_c, ssd_l, ssd_conv_w, ssd_conv_b, ssd_dt_bias, ssd_a_log, ssd_d,
                            ssd_norm_w, with_ctx)
    yc_c, yc_l = shortconv_branch(sc_c, sc_l, sc_conv_w, with_ctx)
    ya_c, ya_l = attn_branch(at_c, at_l, attn_sink, cos, sin, with_ctx)
    x = x + gate[:, None] * (jnp.concatenate([ys_l, yc_l, ya_l], axis=-1) @ w_out)
    if with_ctx:
        ctx = ctx + gate_c * (jnp.concatenate([ys_c, yc_c, ya_c], axis=-1) @ w_out)
    return x, ctx


def setup_inputs(seed: int = 0) -> dict:
    key = jax.random.key(seed)
    ks = jax.random.split(key, 20)
    nrm = jax.random.normal
    f32 = jnp.float32
    dt0 = jnp.exp(jax.random.uniform(ks[9], (DEPTH, 2, SSD_HEADS), f32, np.log(1e-3), np.log(1e-1)))
    return {
        "x": nrm(ks[0], (BATCH, SEQ, D_MODEL), f32),
        "c": nrm(ks[1], (BATCH, D_MODEL), f32),
        "ctx": nrm(ks[2], (BATCH, CTX_LEN, D_MODEL), f32),
        "c_ctx": nrm(ks[3], (D_MODEL,), f32),
        "norm_w": 1.0 + 0.02 * nrm(ks[4], (DEPTH, D_MODEL), f32),
        "w_mod": 0.5 * D_MODEL ** -0.5 * nrm(ks[5], (DEPTH, D_MODEL, 3 * D_MODEL), f32),
        "b_mod": 0.02 * nrm(ks[6], (DEPTH, 3 * D_MODEL), f32),
        "w_in": D_MODEL ** -0.5 * nrm(ks[7], (DEPTH, D_MODEL, IN_COLS), f32),
        "ssd_conv_w": SSD_CONV ** -0.5 * nrm(ks[8], (DEPTH, SSD_CONV, SSD_XBC), f32),
        "ssd_conv_b": 0.02 * nrm(ks[10], (DEPTH, SSD_XBC), f32),
        "ssd_dt_bias": dt0 + jnp.log(-jnp.expm1(-dt0)),
        "ssd_a_log": jnp.log(jax.random.uniform(ks[11], (DEPTH, 2, SSD_HEADS), f32, 1.0, 16.0)),
        "ssd_d": 1.0 + 0.02 * nrm(ks[12], (DEPTH, SSD_HEADS), f32),
        "ssd_norm_w": 1.0 + 0.02 * nrm(ks[13], (DEPTH, SSD_WIDTH), f32),
        "sc_conv_w": SC_CONV ** -0.5 * nrm(ks[14], (DEPTH, SC_CONV, SC_WIDTH), f32),
        "attn_sink": 0.5 * nrm(ks[15], (DEPTH, ATTN_HEADS), f32),
        "w_out": MIX_WIDTH ** -0.5 * nrm(ks[16], (DEPTH, MIX_WIDTH, D_MODEL), f32),
        "final_norm_w": 1.0 + 0.02 * nrm(ks[17], (D_MODEL,), f32),
    }


def reference(x, c, ctx, c_ctx, norm_w, w_mod, b_mod, w_in, ssd_conv_w, ssd_conv_b, ssd_dt_bias,
              ssd_a_log, ssd_d, ssd_norm_w, sc_conv_w, attn_sink, w_out, final_norm_w):
    cos, sin = rope_tables(x.shape[1])
    silu_c, silu_cc = jax.nn.silu(c), jax.nn.silu(c_ctx)
    for l in range(DEPTH):
        x, ctx = hybrid_layer(x, ctx, silu_c, silu_cc, cos, sin, norm_w[l], w_mod[l], b_mod[l], w_in[l],
                              ssd_conv_w[l], ssd_conv_b[l], ssd_dt_bias[l], ssd_a_log[l], ssd_d[l],
                              ssd_norm_w[l], sc_conv_w[l], attn_sink[l], w_out[l],
                              with_ctx=(l < DEPTH - 1))
    return rmsnorm(x, final_norm_w)
```

```python
import numpy as np
import concourse.bass as bass
import concourse.mybir as mybir
from concourse.bass_utils import run_bass_kernel_spmd

F32 = mybir.dt.float32
BF16 = mybir.dt.bfloat16
AF = mybir.ActivationFunctionType
ALU = mybir.AluOpType
AX = mybir.AxisListType

D = 1024
T_LAT = 4096
T_CTX = 256
NCH = 34
T_ALL = NCH * 128
DEPTH = 2
EPS = 1e-6

O_Z = 0
O_XS = 1024
O_B = 2048
O_C = 2304
O_DT = 2560
O_SCV = 2592
O_SCC = 3104
O_SCB = 3616
O_SCZ = 4128
O_Q = 4640
O_K = 5152
O_V = 5280
O_ZA = 5408


def _rope_partner(d):
    a, r = divmod(d, 32)
    j, i = divmod(r, 16)
    return a * 32 + (1 - j) * 16 + i


def _cols_A():
    cols = list(range(O_XS, O_XS + 1536))
    cols += list(range(O_SCV, O_SCV + 512))
    cols += list(range(O_SCC, O_SCC + 512))
    cols += list(range(O_K, O_K + 128))
    cols += [O_K + g * 64 + _rope_partner(d) for g in range(2) for d in range(64)]
    cols += list(range(O_V, O_V + 128))
    cols += list(range(O_DT, O_DT + 32))
    return cols


A_XBC, A_SCV, A_SCC, A_K, A_KSW, A_V, A_DT, NA = 0, 1536, 2048, 2560, 2688, 2816, 2944, 2976


def _cols_C():
    return list(range(O_Z, O_Z + 1024))


def _cols_Q():
    cols = list(range(O_SCB, O_SCB + 512))
    cols += list(range(O_SCZ, O_SCZ + 512))
    qt = []
    qs = []
    for j in range(4):
        for hq in (j, 4 + j):
            qt += [O_Q + hq * 64 + d for d in range(64)]
            qs += [O_Q + hq * 64 + _rope_partner(d) for d in range(64)]
    cols += qt + qs
    cols += list(range(O_ZA, O_ZA + 512))
    return cols


C_Z, NCC = 0, 1024
Q_SCB, Q_SCZ, Q_Q, Q_QSW, Q_ZA, NQ = 0, 512, 1024, 1536, 2048, 2560


class Tl:
    def __init__(self, t, k):
        self.t = t
        self.k = k

    def __getitem__(self, idx):
        return self.t[idx]


class Sched:
    def __init__(self, nc, n_dma_sems=12):
        self.nc = nc
        self.engs = {"pe": nc.tensor, "act": nc.scalar, "dve": nc.vector,
                     "pool": nc.gpsimd, "sp": nc.sync}
        self.sem = {}
        self.cnt = {}
        for e in self.engs:
            self.sem[e] = nc.alloc_semaphore("s_" + e)
            self.cnt[e] = 0
        self.dring = {}
        for q in ("sp", "act", "pool"):
            self.dring[q] = [[nc.alloc_semaphore("d_%s%d" % (q, i)), 0] for i in range(n_dma_sems)]
        self.dpos = {q: 0 for q in self.dring}
        self.seen = {e: {} for e in self.engs}
        self.state = {}
        self.ninst = 0
        self.nwaits = 0
        self.cur = None
        self.stages = {}
        self.itn = 0

    def begin(self, name):
        self.cur = self.stages.setdefault(name, [])

    def end(self):
        self.cur = None

    def emit(self, name):
        assert self.cur is None
        for item in self.stages.pop(name, []):
            self._emit_item(item)

    def emit_scheduled(self, name, overlap=3.0):
        assert self.cur is None
        items = self.stages.pop(name, [])
        n = len(items)
        if n == 0:
            return

        class _Probe:
            def __init__(self):
                self.nm, self.a, self.k = None, (), {}

            def __getattr__(self, nm):
                def f(*a, **k):
                    self.nm, self.a, self.k = nm, a, k
                    return self
                return f

        def cost_of(it):
            if it[0] == "dma":
                return 0.1, 2.2
            pr = _Probe()
            try:
                it[2](pr)
                out = pr.k.get("out", pr.a[0] if pr.a else None)
                shp = tuple(out.shape)
                nel = 1
                for d_ in shp[1:]:
                    nel *= int(d_)
            except Exception:
                nel = 512
            eng = it[1]
            if eng == "pe":
                c = 0.12 if pr.nm == "transpose" else 0.04 + nel / 1400.0
            elif eng == "act":
                c = 0.1 + nel / 1100.0
            elif eng == "dve":
                c = 0.08 + nel / 960.0
                if pr.nm == "reciprocal":
                    c *= 5.0
            else:
                c = 0.1 + nel / 430.0
            return c, c

        last_w = {}
        readers = {}
        preds = [set() for _ in range(n)]
        for i, it in enumerate(items):
            if it[0] == "dma":
                reads, writes = it[4], it[5]
            else:
                reads, writes = it[3], it[4]
            for k in reads:
                w = last_w.get(k)
                if w is not None:
                    preds[i].add(w)
            for k in writes:
                w = last_w.get(k)
                if w is not None:
                    preds[i].add(w)
                for r in readers.get(k, ()):
                    preds[i].add(r)
            for k in reads:
                readers.setdefault(k, []).append(i)
            for k in writes:
                last_w[k] = i
                readers[k] = []
        succs = [[] for _ in range(n)]
        indeg = [0] * n
        for i in range(n):
            preds[i].discard(i)
            indeg[i] = len(preds[i])
            for p_ in preds[i]:
                succs[p_].append(i)
        engs = [it[1] for it in items]
        costs = [cost_of(it) for it in items]
        ready_t = [0.0] * n
        free = {}
        ready = set(i for i in range(n) if indeg[i] == 0)
        window = int(overlap * 1200)
        done = 0
        lowest = 0
        emitted = [False] * n
        while ready:
            best, bkey = None, None
            for i in ready:
                if i - lowest > window:
                    continue
                st = max(ready_t[i], free.get(engs[i], 0.0))
                key = (st, i)
                if bkey is None or key < bkey:
                    best, bkey = i, key
            if best is None:
                best = min(ready)
                bkey = (max(ready_t[best], free.get(engs[best], 0.0)), best)
            i = best
            ready.discard(i)
            st = bkey[0]
            busy, lat = costs[i]
            free[engs[i]] = st + busy
            fin = st + lat
            it = items[i]
            if it[0] == "op":
                self.op(*it[1:5])
            else:
                self.dma(*it[1:6], **it[6])
            emitted[i] = True
            while lowest < n and emitted[lowest]:
                lowest += 1
            done += 1
            for j in succs[i]:
                hop = 0.05 if engs[j] == engs[i] else 0.45
                if fin + hop > ready_t[j]:
                    ready_t[j] = fin + hop
                indeg[j] -= 1
                if indeg[j] == 0:
                    ready.add(j)
        assert done == n, (done, n)
        self.sim_time = getattr(self, "sim_time", 0.0) + max(free.values())
        busy = {}
        for i in range(n):
            busy[engs[i]] = busy.get(engs[i], 0.0) + costs[i][0]
        self.sim_log = getattr(self, "sim_log", [])
        self.sim_log.append((str(name), round(max(free.values()), 1), {k_: round(v_, 1) for k_, v_ in busy.items()}))

    def _emit_item(self, item):
        if item[0] == "op":
            self.op(*item[1:5])
        else:
            self.dma(*item[1:6], **item[6])

    def emit_merged(self, name_a, name_b):
        assert self.cur is None
        la = self.stages.pop(name_a, [])
        lb = self.stages.pop(name_b, [])
        i = j = 0
        while i < len(la) or j < len(lb):
            if j >= len(lb) or (i < len(la) and i * len(lb) <= j * len(la)):
                self._emit_item(la[i])
                i += 1
            else:
                self._emit_item(lb[j])
                j += 1

    @staticmethod
    def _keys(lst):
        return [x.k if isinstance(x, Tl) else x for x in lst]

    def _need(self, eng, ev):
        sem, val, src = ev
        if src == "pe" and eng == "pe":
            return
        cur = self.seen[eng].get(sem.name, 0)
        if cur >= val:
            return
        self.seen[eng][sem.name] = val
        self.engs[eng].wait_ge(sem, val)
        self.nwaits += 1

    def _deps(self, eng, reads, writes):
        for k in reads:
            st = self.state.get(k)
            if st and st[0] is not None:
                self._need(eng, st[0])
        for k in writes:
            st = self.state.get(k)
            if st:
                if st[0] is not None:
                    self._need(eng, st[0])
                for ev in st[1].values():
                    self._need(eng, ev)

    def _record(self, ev, reads, writes):
        for k in reads:
            st = self.state.setdefault(k, [None, {}])
            old = st[1].get(ev[0].name)
            if old is None or old[1] < ev[1]:
                st[1][ev[0].name] = ev
        for k in writes:
            self.state[k] = [ev, {}]

    def op(self, eng, fn, reads=(), writes=()):
        reads = self._keys(reads)
        writes = self._keys(writes)
        if self.cur is not None:
            self.cur.append(("op", eng, fn, reads, writes, self.itn))
            return
        self._deps(eng, reads, writes)
        ins = fn(self.engs[eng])
        self.cnt[eng] += 1
        ins.then_inc(self.sem[eng], 1)
        ev = (self.sem[eng], self.cnt[eng], eng)
        self._record(ev, reads, writes)
        self.ninst += 1

    def dma(self, q, out, in_, reads=(), writes=(), **kw):
        reads = self._keys(reads)
        writes = self._keys(writes)
        if self.cur is not None:
            self.cur.append(("dma", q, out, in_, reads, writes, kw, self.itn))
            return
        ring = self.dring[q]
        slot = ring[self.dpos[q] % len(ring)]
        self.dpos[q] += 1
        sem, tot = slot
        if tot > 0:
            self._need(q, (sem, tot, None))
        self._deps(q, reads, writes)
        ins = self.engs[q].dma_start(out=out, in_=in_, **kw)
        slot[1] = tot + 16
        ins.then_inc(sem, 16)
        ev = (sem, slot[1], None)
        self._record(ev, reads, writes)
        self.ninst += 1

    def barrier(self):
        evs = [(self.sem[e], self.cnt[e], e) for e in self.engs if self.cnt[e] > 0]
        for q in self.dring:
            for sem, tot in self.dring[q]:
                if tot > 0:
                    evs.append((sem, tot, None))
        for e in self.engs:
            for ev in evs:
                self._need(e, ev)

    def wait_all(self, eng="sp"):
        for k, st in list(self.state.items()):
            if st[0] is not None:
                self._need(eng, st[0])
            for ev in st[1].values():
                self._need(eng, ev)


OVL_C = 3.0


def build_program(debug_out=None, n_layers=DEPTH, stop_after=None):
    nc = bass.Bass("TRN2", target_bir_lowering=False)
    S = Sched(nc)

    def din(name, shape, dt=F32):
        return nc.dram_tensor(name, list(shape), dt, kind="ExternalInput").ap()

    dbg = set(debug_out or [])

    def dscr(name, shape, dt=F32):
        kind = "ExternalOutput" if name in dbg else "Internal"
        return nc.dram_tensor(name, list(shape), dt, kind=kind).ap()

    x_in = din("x", [T_LAT, D])
    ctx_in = din("ctx", [T_CTX, D])
    cc_in = din("cc", [128, 8, 2])
    normw_in = din("norm_w", [DEPTH, 1, D])
    wmod_in = din("w_mod", [DEPTH, D, 3 * D])
    bmod_in = din("b_mod", [DEPTH, 1, 3 * D])
    wA_in = din("wA", [DEPTH, D, NA])
    wC_in = din("wC", [DEPTH, D, NCC])
    wQ_in = din("wQ", [DEPTH, D, NQ])
    wom_in = din("wo_all", [DEPTH, 2048, D])
    convw_in = din("convw", [DEPTH, 128, 12, 3])
    convb_in = din("convb", [DEPTH, 128, 12])
    scw_in = din("scw", [DEPTH, 128, 4, 3])
    dtb_in = din("dt_bias", [DEPTH, 1, 32])
    alog_in = din("a_log", [DEPTH, 1, 32])
    dsk_in = din("ssd_d", [DEPTH, 1, 16])
    snw_in = din("ssd_norm_w", [DEPTH, 1, D])
    sink_in = din("sink", [DEPTH, 1, 8])
    fnw_in = din("final_norm_w", [1, D])
    rope_in = din("ropecs", [128, 2, T_LAT])
    out_d = nc.dram_tensor("out", [T_LAT, D], F32, kind="ExternalOutput").ap()

    x1_d = dscr("x1", [T_LAT, D])
    ctx1_d = dscr("ctx1", [T_CTX, D])
    hT_d = dscr("hT_all", [NCH, 128, 8, 128], BF16)
    xs_d = dscr("xs_all", [NCH, 128, 1024], BF16)
    btm_d = dscr("btm_all", [NCH, 128, 256], BF16)
    bt_d = dscr("bt_all", [NCH, 128, 2, 128], BF16)
    ct_d = dscr("ct_all", [NCH, 128, 2, 128], BF16)
    dtda_d = dscr("dtda_all", [NCH, 128, 64])
    sb_d = dscr("sb_all", [NCH, 128, 1024], BF16)
    cvc_d = dscr("cvc_all", [NCH, 128, 4, 128])
    kt_d = dscr("kt_all", [128, T_ALL], BF16)
    v_d = dscr("v_all", [NCH, 128, 128], BF16)
    mod_d = dscr("mod_scr", [2, 3 * D])
    yc_d = dscr("yc_all", [NCH, 128, 12, 128], BF16)
    og_d = dscr("og_all", [NCH, 64, 8, 128], BF16)
    qt_d = dscr("qt_all", [NCH, 128, 4, 128], BF16)
    za_d = dscr("za_all", [NCH, 64, 8, 128])
    dbg_ys = dscr("dbg_ys", [NCH, 128, 1024], BF16) if "dbg_ys" in dbg else None
    dbg_sc = dscr("dbg_sc", [NCH, 128, 4, 128], BF16) if "dbg_sc" in dbg else None
    dbg_og = dscr("dbg_og", [NCH, 64, 2, 4, 128], BF16) if "dbg_og" in dbg else None

    SB_BASE, SB_END = 16640, 229376
    DTB = {F32: 4, BF16: 2}
    ptr = {"persist": SB_BASE}
    lim = {}

    def _alloc(space, name, shape, dt):
        n = 1
        for d_ in shape[1:]:
            n *= d_
        nbytes = (n * DTB[dt] + 63) // 64 * 64
        off = ptr[space]
        ptr[space] = off + nbytes
        assert ptr[space] <= lim.get(space, SB_END), (space, name, ptr[space])
        uname = "%s_%s" % (space, name)
        return Tl(nc.alloc_sbuf_tensor_at(uname, list(shape), dt, offset=off), uname)

    def sb(name, shape, dt=F32):
        return _alloc("persist", name, shape, dt)

    W = sb("W", [128, 49152], BF16)
    ident_b = sb("ident_b", [128, 128], BF16)
    cm_f = {n: sb("cm_" + n, [128, 128], F32) for n in ("UI", "LI", "SU", "SL", "ones")}
    cm_b = {n: sb("cb_" + n, [128, 128], BF16) for n in ("UI", "LI", "SU", "SL", "ones")}
    negm = {n: sb("negm_" + n, [128, 4, 128], BF16) for n in ("UI", "LI")}
    g_l = sb("g_l", [128, D])
    snw_b = sb("snw_b", [128, D])
    aux = sb("aux", [128, D])
    small = sb("small", [128, 128])
    convw = sb("convw", [128, 12, 3])
    convb = sb("convb", [128, 12])
    scw = sb("scw", [128, 4, 3])
    cc = sb("cc", [128, 8, 2])
    st8 = sb("st8", [128, 8])
    PH_BASE = ptr["persist"]

    def phase_tiles(space, specs):
        ptr[space] = PH_BASE
        d_ = {}
        for (name, shape, dt) in specs:
            if isinstance(name, tuple):
                d_[name[0]] = {k_: _alloc(space, "%s_%s" % (name[0], k_), shape, dt) for k_ in name[1]}
            else:
                d_[name] = _alloc(space, name, shape, dt)
        return d_

    dbl = lambda n: (n, ("0", "1"))
    T0 = phase_tiles("p0", [
        ("A_l", [128, D], F32), ("sh_l", [128, D], F32), ("A_c", [128, D], F32), ("sh_c", [128, D], F32),
        ("tmpA", [128, D], F32), ("stg0", [128, 1024], F32), ("stg1", [128, 1024], F32),
        ("modsb", [2, 3 * D], F32), ("bmod2", [2, 3 * D], F32),
        (dbl("xin"), [128, D], F32), (dbl("sq"), [128, D], F32), (dbl("tmpB"), [128, D], F32), (dbl("hb"), [128, D], BF16),
        (dbl("hT"), [128, 8, 128], BF16), (dbl("st"), [128, 8], F32)])
    TA = phase_tiles("pA", [
        ("stg0", [128, 1024], F32), ("stg1", [128, 1024], F32), (dbl("hTw"), [128, 8, 258], BF16),
        (dbl("xbcT"), [128, 12, 256], BF16), (dbl("cv"), [128, 258], F32), (dbl("cv2"), [128, 258], F32),
        (dbl("cvc"), [128, 4, 256], F32),
        (dbl("ropeT"), [128, 2, 256], F32), (dbl("ktile"), [128, 256], BF16), (dbl("xs"), [128, 1024], BF16),
        (dbl("btm"), [128, 256], BF16),
        (dbl("dtda"), [128, 64], F32), (dbl("dtr"), [128, 32], F32), ("ExS", [128, 64], F32), ("wsm", [128, 32], F32),
        (dbl("vtm"), [128, 128], BF16), ("rhsb", [128, 1024], BF16), ("S", [128, 1024], F32), ("S16", [128, 1024], BF16)])
    dbl = lambda n: (n, ("0", "1"))
    TC = phase_tiles("pC", [
        ("stg0", [128, 1024], F32), ("stg1", [128, 1024], F32),
        (dbl("xs"), [128, 1024], BF16), (dbl("btm"), [128, 256], BF16), (dbl("dtda"), [128, 64], F32),
        (dbl("hT"), [128, 8, 128], BF16), (dbl("btct"), [128, 4, 128], BF16), (dbl("sbin"), [128, 1024], BF16),
        (dbl("KTb"), [128, 5, 128], BF16),
        (dbl("Vb"), [128, 5, 128], BF16), (dbl("sz"), [128, 1024], F32), (dbl("Ex"), [128, 64], F32),
        (dbl("M_f"), [128, 16, 128], BF16), (dbl("M_b"), [128, 16, 128], BF16), (dbl("xdt_f"), [128, 1024], BF16),
        (dbl("xdt_b"), [128, 1024], BF16), (dbl("QT"), [128, 4, 128], BF16), (dbl("szA"), [64, 2, 4, 128], F32),
        ("OG", [64, 2, 4, 128], BF16), ("rden", [64, 4, 128], F32),
        ("hilo", [128, 64], BF16), ("dres", [128, 32], F32), ("wsm", [128, 32], F32), ("ExS", [128, 64], F32)])
    ptr["pCb"] = SB_BASE + 8 * NCC * 2
    lim["pCb"] = SB_BASE + 65536
    _pcb_base = ptr["pCb"]
    TCb = {}
    for (name, shape, dt) in [
            ("rb0", [128, 16, 128], BF16), ("rb1", [128, 16, 128], BF16), ("Lm0", [128, 4, 128], F32), ("Lm1", [128, 4, 128], F32),
            ("tmpA", [128, D], F32), ("tmpB", [128, D], F32), ("cbm", [128, 2, 2, 128], F32),
            ("ysb", [128, 1024], BF16), ("ysT", [128, 8, 128], BF16), ("rhsb", [128, 1024], BF16), ("S", [128, 1024], F32),
            ("S16", [128, 1024], BF16), ("PT0", [128, 512], BF16), ("PT1", [128, 512], BF16)]:
        TCb[name] = _alloc("pCb", name, shape, dt)
    TC.update(TCb)
    NBUF_C = 2
    for (name, shape, dt, sp_) in [
            ("tmpA", [128, D], F32, "pCb"), ("tmpB", [128, D], F32, "pCb"), ("ysb", [128, 1024], BF16, "pC"),
            ("ysT", [128, 8, 128], BF16, "pC"), ("PT0", [128, 512], BF16, "pC"), ("PT1", [128, 512], BF16, "pC"),
            ("rhsb", [128, 1024], BF16, "pC")]:
        TC[name] = {"0": TC[name], "1": _alloc(sp_, name + "_b", shape, dt)}
    for (name, shape, dt) in [("OG", [64, 2, 4, 128], BF16), ("rden", [64, 4, 128], F32), ("wsm", [128, 32], F32),
                              ("ExS", [128, 64], F32), ("stB", [128, 8], F32)]:
        first = TC[name] if name in TC else _alloc("pC", name + "_a", shape, dt)
        TC[name] = {"0": first, "1": _alloc("pC", name + "_b", shape, dt)}
    TQ = phase_tiles("pQ", [
        ("stg0", [128, 1024], F32), ("stg1", [128, 1024], F32), ("sct", [128, 512], F32), ("qtmp", [128, 1024], F32),
        (dbl("hTq"), [128, 8, 512], BF16), (dbl("cvq"), [128, 4, 512], F32), (dbl("ropq"), [128, 2, 512], F32),
        (dbl("scy"), [128, 4, 512], BF16), (dbl("QTq"), [128, 4, 512], BF16), (dbl("zaq"), [64, 4, 512], F32)])
    T3 = phase_tiles("p3", [
        ("stg0", [128, 1024], F32), ("stg1", [128, 1024], F32),
        (dbl("yc"), [128, 12, 128], BF16), (dbl("og"), [128, 4, 128], BF16), (dbl("xin"), [128, D], F32),
        (dbl("xnew"), [128, D], F32), (dbl("tmpB"), [128, D], F32), (dbl("st"), [128, 8], F32)])
    print("SBUF bytes: persist %d  p0 %d  pA %d  pC %d pCb %d/%d p3 %d pQ %d (end %d)" % (PH_BASE, ptr["p0"], ptr["pA"], ptr["pC"], ptr["pCb"], lim["pCb"], ptr["p3"], ptr["pQ"], SB_END))

    def ps(name, shape, dt=F32):
        return Tl(nc.alloc_psum_tensor(name, list(shape), dt), name)

    psT = ps("psT", [128, 1024], BF16)
    banks = [ps("bk%d" % i, [128, 512]) for i in range(7)]
    bpos = [0]

    bsel = [None]
    bsub = {}

    def pb():
        if bsel[0] is None:
            b = banks[bpos[0] % len(banks)]
            bpos[0] += 1
            return b
        lo, hi = bsel[0]
        c = bsub.get(bsel[0], 0)
        bsub[bsel[0]] = c + 1
        return banks[lo + c % (hi - lo)]

    def mm(out_ap, lhsT, rhs, start, stop, reads, writes):
        S.op("pe", lambda e: e.matmul(out_ap, lhsT=lhsT, rhs=rhs, start=start, stop=stop), reads, writes)

    def tr(out_ap, in_ap, reads, writes):
        S.op("pe", lambda e: e.transpose(out_ap, in_ap, ident_b[:]), list(reads) + [ident_b], writes)

    def bc_row(dst_tile, dst_ap, src_row_ap, n=128):
        S.dma("sp", dst_ap, src_row_ap.partition_broadcast(n), writes=[dst_tile])

    cast_rr = [0]
    W_A, W_Q, W_C, W_3 = (Tl(W.t, "W_A"), Tl(W.t, "W_Q"), Tl(W.t, "W_C"), Tl(W.t, "W_3"))
    QOFF = 8 * NA
    W3OFF = 32768

    def weight_pieces(TT, key, dst_off, src, ncols, kparts):
        pieces = []
        for k in range(kparts):
            c0 = 0
            while c0 < ncols:
                cw = min(1024, ncols - c0)

                def piece(k=k, c0=c0, cw=cw):
                    st = TT["stg%d" % (cast_rr[0] % 2)]
                    S.dma("sp", st[:, 0:cw], src[k * 128:(k + 1) * 128, c0:c0 + cw], writes=[st])
                    o = dst_off + k * ncols + c0
                    eng = ("act", "dve", "pool")[cast_rr[0] % 3]
                    if eng == "act":
                        S.op("act", lambda e: e.copy(out=W[:, o:o + cw], in_=st[:, 0:cw]), [st], [key])
                    else:
                        S.op(eng, lambda e: e.tensor_copy(out=W[:, o:o + cw], in_=st[:, 0:cw]), [st], [key])
                    cast_rr[0] += 1
                pieces.append(piece)
                c0 += cw
        return pieces

    def pieces_A(l, TT):
        return weight_pieces(TT, W_A, 0, wA_in[l], NA, 8)

    def pieces_Q(l, TT):
        return weight_pieces(TT, W_Q, QOFF, wQ_in[l], NQ, 8)

    def pieces_C(l, TT):
        return weight_pieces(TT, W_C, 0, wC_in[l], NCC, 8)

    def pieces_3(l, TT):
        ps_ = []
        for t in range(16):
            ps_ += weight_pieces(TT, W_3, W3OFF + t * 1024, wom_in[l, t * 128:(t + 1) * 128, :], 1024, 1)
        return ps_

    pending = {}

    def take(pieces, n):
        for _ in range(min(n, len(pieces))):
            pieces.pop(0)()

    def build_consts():
        def sel(t, pattern, cm, op):
            S.op("pool", lambda e: e.memset(t[:], 1.0), [], [t])
            S.op("pool", lambda e: e.affine_select(out=t[:], in_=t[:], pattern=pattern, compare_op=op,
                                                   fill=0.0, base=0, channel_multiplier=cm), [t], [t])
        sel(cm_f["UI"], [[1, 128]], -1, ALU.is_ge)
        sel(cm_f["LI"], [[-1, 128]], 1, ALU.is_ge)
        sel(cm_f["SU"], [[-1, 128]], 1, ALU.is_gt)
        sel(cm_f["SL"], [[1, 128]], -1, ALU.is_gt)
        S.op("pool", lambda e: e.memset(cm_f["ones"][:], 1.0), [], [cm_f["ones"]])
        for n in cm_f:
            S.op("dve", lambda e, n=n: e.tensor_copy(out=cm_b[n][:], in_=cm_f[n][:]), [cm_f[n]], [cm_b[n]])
        S.op("pool", lambda e: e.memset(ident_b[:], 1.0), [], [ident_b])
        S.op("pool", lambda e: e.affine_select(out=ident_b[:], in_=ident_b[:], pattern=[[-1, 128]],
                                               compare_op=ALU.is_equal, fill=0.0, base=0, channel_multiplier=1),
             [ident_b], [ident_b])
        for n in ("UI", "LI"):
            S.op("dve", lambda e, n=n: e.tensor_scalar(
                out=negm[n][:], in0=cm_f[n][:].unsqueeze(1).to_broadcast([128, 4, 128]), scalar1=-1.0, scalar2=2.4e5,
                op0=ALU.add, op1=ALU.mult), [cm_f[n]], [negm[n]])
        S.dma("sp", cc[:], cc_in, writes=[cc])
        S.op("act", lambda e: e.activation(out=cc[:], in_=cc[:], func=AF.Silu), [cc], [cc])

    def layer_setup(l):
        modsb, bmod2, tmpA = T0["modsb"], T0["bmod2"], T0["tmpA"]
        accs = [pb() for _ in range(6)]
        i = 0
        for kc in range(8):
            for third in range(3):
                st = T0["stg%d" % (i % 2)]
                i += 1
                S.dma("sp", st[:, 0:1024], wmod_in[l, kc * 128:(kc + 1) * 128, third * 1024:(third + 1) * 1024],
                      writes=[st])
                for j in range(2):
                    a = accs[third * 2 + j]
                    mm(a[0:2, 0:512], cc[:, kc, :], st[:, j * 512:(j + 1) * 512], kc == 0, kc == 7, [cc, st], [a])
        S.dma("sp", bmod2[:], bmod_in[l].partition_broadcast(2), writes=[bmod2])
        for j in range(6):
            S.op("dve", lambda e, j=j: e.tensor_tensor(out=modsb[:, j * 512:(j + 1) * 512], in0=accs[j][0:2, 0:512],
                                                        in1=bmod2[:, j * 512:(j + 1) * 512], op=ALU.add),
                 [accs[j], bmod2], [modsb])
        S.dma("sp", mod_d, modsb[:], reads=[modsb], writes=["mod_d"])
        for (row, sh_t, A_t, g_t) in ((0, T0["sh_l"], T0["A_l"], g_l), (1, T0["sh_c"], T0["A_c"], aux)):
            S.dma("sp", sh_t[:], mod_d[row:row + 1, 0:D].partition_broadcast(128), reads=["mod_d"], writes=[sh_t])
            S.dma("sp", A_t[:], mod_d[row:row + 1, D:2 * D].partition_broadcast(128), reads=["mod_d"], writes=[A_t])
            if row == 0 or l < DEPTH - 1:
                S.dma("sp", g_t[:], mod_d[row:row + 1, 2 * D:3 * D].partition_broadcast(128), reads=["mod_d"], writes=[g_t])
        if l == DEPTH - 1:
            bc_row(aux, aux[:], fnw_in)
        bc_row(tmpA, tmpA[:], normw_in[l])
        for A_t in (T0["A_l"], T0["A_c"]):
            S.op("dve", lambda e, A_t=A_t: e.scalar_tensor_tensor(out=A_t[:], in0=A_t[:], scalar=1.0, in1=tmpA[:],
                                                                    op0=ALU.add, op1=ALU.mult), [A_t, tmpA], [A_t])
        bc_row(snw_b, snw_b[:], snw_in[l])
        bc_row(small, small[:, 0:32], dtb_in[l])
        bc_row(small, small[:, 32:64], alog_in[l])
        bc_row(small, small[:, 64:80], dsk_in[l])
        bc_row(small, small[:, 80:88], sink_in[l])
        S.op("act", lambda e: e.activation(out=small[:, 32:64], in_=small[:, 32:64], func=AF.Exp), [small], [small])
        S.op("dve", lambda e: e.tensor_scalar_mul(out=small[:, 32:64], in0=small[:, 32:64], scalar1=-1.0), [small], [small])
        S.op("act", lambda e: e.activation(out=small[:, 80:88], in_=small[:, 80:88], func=AF.Exp), [small], [small])
        S.dma("sp", convw[:], convw_in[l], writes=[convw])
        S.dma("sp", convb[:], convb_in[l], writes=[convb])
        S.dma("sp", scw[:], scw_in[l], writes=[scw])

    def rms_rstd(src_tile, scratch, width, st8=st8):
        S.op("act", lambda e: e.activation(out=scratch[:, 0:width], in_=src_tile[:, 0:width], func=AF.Square),
             [src_tile], [scratch])
        S.op("dve", lambda e: e.reduce_sum(out=st8[:, 0:1], in_=scratch[:, 0:width], axis=AX.X), [scratch], [st8])
        S.op("dve", lambda e: e.tensor_scalar(out=st8[:, 0:1], in0=st8[:, 0:1], scalar1=1.0 / width, scalar2=EPS,
                                               op0=ALU.mult, op1=ALU.add), [st8], [st8])
        S.op("act", lambda e: e.activation(out=st8[:, 0:1], in_=st8[:, 0:1], func=AF.Sqrt), [st8], [st8])
        S.op("dve", lambda e: e.reciprocal(out=st8[:, 0:1], in_=st8[:, 0:1]), [st8], [st8])

    def pass0(l):
        S.begin("p0")
        for gc in range(NCH):
            S.itn = gc
            p = str(gc % 2)
            xin, sq, tmpB, hb, hT, stt = (T0[n][p] for n in ("xin", "sq", "tmpB", "hb", "hT", "st"))
            if gc < 2:
                src = (ctx_in if l == 0 else ctx1_d)[gc * 128:(gc + 1) * 128, :]
                A_t, sh_t = T0["A_c"], T0["sh_c"]
            else:
                src = (x_in if l == 0 else x1_d)[(gc - 2) * 128:(gc - 1) * 128, :]
                A_t, sh_t = T0["A_l"], T0["sh_l"]
            S.dma("sp", xin[:], src, reads=["x1_d", "ctx1_d"], writes=[xin])
            rms_rstd(xin, sq, D, st8=stt)
            S.op("dve", lambda e, A_t=A_t, xin=xin, tmpB=tmpB, stt=stt: e.scalar_tensor_tensor(
                out=tmpB[:], in0=xin[:], scalar=stt[:, 0:1], in1=A_t[:], op0=ALU.mult, op1=ALU.mult),
                [xin, stt, A_t], [tmpB])
            S.op("pool", lambda e, sh_t=sh_t, hb=hb, tmpB=tmpB: e.tensor_tensor(out=hb[:], in0=tmpB[:], in1=sh_t[:], op=ALU.add),
                 [tmpB, sh_t], [hb])
            for kc in range(8):
                tr(psT[:, kc * 128:(kc + 1) * 128], hb[:, kc * 128:(kc + 1) * 128], [hb], [psT])
            S.op("act", lambda e, hT=hT: e.copy(out=hT[:].rearrange("p k t -> p (k t)"), in_=psT[:]), [psT], [hT])
            S.dma("sp", hT_d[gc], hT[:], reads=[hT], writes=["hT_d"])
        S.end()
        S.emit_scheduled("p0", overlap=OVL_C)

    def state_update(TT, pmat_key, da_cols, dt_cols, gc, store_d=None):
        Sd, Sd16, xs, btm, dtda, Ex, wsm, rhsb = (TT[n] for n in ("S", "S16", "xs", "btm", "dtda", "ExS", "wsm", "rhsb"))
        p = pb()
        mm(p[:, 0:16], cm_f[pmat_key][:], dtda[:, da_cols[0]:da_cols[1]], True, True, [cm_f[pmat_key], dtda], [p])
        mm(p[:, 16:32], cm_f["ones"][:], dtda[:, da_cols[0]:da_cols[1]], True, True, [cm_f["ones"], dtda], [p])
        S.op("act", lambda e: e.activation(out=Ex[:, 0:32], in_=p[:, 0:32], func=AF.Exp), [p], [Ex])
        S.op("dve", lambda e: e.tensor_tensor(out=wsm[:, 0:16], in0=dtda[:, dt_cols[0]:dt_cols[1]], in1=Ex[:, 0:16],
                                               op=ALU.mult), [dtda, Ex], [wsm])
        S.op("dve", lambda e: e.tensor_tensor(out=rhsb[:].rearrange("p (h q) -> p h q", h=16),
                                               in0=xs[:].rearrange("p (h q) -> p h q", h=16),
                                               in1=wsm[:, 0:16].unsqueeze(2).to_broadcast([128, 16, 64]), op=ALU.mult),
             [xs, wsm], [rhsb])
        if store_d is not None:
            S.dma("sp", store_d[gc], Sd16[:], reads=[Sd16], writes=["sb_d"])
        pa, pb2 = pb(), pb()
        mm(pa[:, 0:512], btm[:, 0:128], rhsb[:, 0:512], True, True, [btm, rhsb], [pa])
        mm(pb2[:, 0:512], btm[:, 128:256], rhsb[:, 512:1024], True, True, [btm, rhsb], [pb2])
        S.op("pool", lambda e: e.tensor_tensor(out=Sd[:].rearrange("p (h q) -> p h q", h=16),
                                                in0=Sd[:].rearrange("p (h q) -> p h q", h=16),
                                                in1=Ex[:, 16:32].unsqueeze(2).to_broadcast([128, 16, 64]), op=ALU.mult),
             [Sd, Ex], [Sd])
        S.op("dve", lambda e: e.tensor_tensor(out=Sd[:, 0:512], in0=Sd[:, 0:512], in1=pa[:, 0:512], op=ALU.add),
             [Sd, pa], [Sd])
        S.op("dve", lambda e: e.tensor_tensor(out=Sd[:, 512:1024], in0=Sd[:, 512:1024], in1=pb2[:, 0:512], op=ALU.add),
             [Sd, pb2], [Sd])
        S.op("act", lambda e: e.copy(out=Sd16[:], in_=Sd[:]), [Sd], [Sd16])

    WA = lambda kc, c0, c1: W[:, kc * NA + c0: kc * NA + c1]

    def passA(l):
        if not pending.pop(("A", l), False):
            take(pieces_A(l, TA), 999)
        pf = pieces_Q(l, TA)
        pending[("Q", l)] = True
        S.op("pool", lambda e: e.memset(TA["S"][:], 0.0), [], [TA["S"]])
        S.op("pool", lambda e: e.memset(TA["S16"][:], 0.0), [], [TA["S16"]])
        order = [0] + [2 + 2 * j for j in range(15, -1, -1)]
        S.begin("pA")
        for oi, gc0 in enumerate(order):
            S.itn = oi
            take(pf, 2)
            passA_super(oi, gc0)
        take(pf, 999)
        S.end()
        S.emit_scheduled("pA", overlap=OVL_C)

    def passA_super(oi, gc0):
        if True:
            sp_ = str(oi % 2)
            (hTw, xbcT, cv, cv2, cvc, ropeT, ktile) = (TA[n][sp_] for n in ("hTw", "xbcT", "cv", "cv2", "cvc", "ropeT", "ktile"))
            is_ctx = gc0 < 2
            for c2 in range(2):
                S.dma("sp", hTw[:, :, 1 + c2 * 128: 1 + (c2 + 1) * 128], hT_d[gc0 + c2], reads=["hT_d"], writes=[hTw])
            if is_ctx or gc0 == 2:
                S.op("pool", lambda e: e.memset(hTw[:, :, 0:1], 0.0), [], [hTw])
            else:
                S.dma("sp", hTw[:, :, 0:1], hT_d[gc0 - 1, :, :, 127:128], reads=["hT_d"], writes=[hTw],
                      allow_slow_non_contiguous=True)
            if is_ctx or gc0 == NCH - 2:
                S.op("pool", lambda e: e.memset(hTw[:, :, 257:258], 0.0), [], [hTw])
            else:
                S.dma("sp", hTw[:, :, 257:258], hT_d[gc0 + 2, :, :, 0:1], reads=["hT_d"], writes=[hTw],
                      allow_slow_non_contiguous=True)
            for m in range(12):
                p = pb()
                for kc in range(8):
                    mm(p[:, 0:258], WA(kc, A_XBC + m * 128, A_XBC + (m + 1) * 128), hTw[:, kc, :], kc == 0, kc == 7,
                       [W_A, hTw], [p])
                S.op("dve", lambda e, p=p, m=m: e.tensor_scalar_mul(out=cv[:, 0:256], in0=p[:, 0:256],
                                                                     scalar1=convw[:, m, 0:1]), [p, convw], [cv])
                S.op("dve", lambda e, p=p, m=m: e.scalar_tensor_tensor(out=cv[:, 0:256], in0=p[:, 1:257],
                                                                        scalar=convw[:, m, 1:2], in1=cv[:, 0:256],
                                                                        op0=ALU.mult, op1=ALU.add), [p, convw, cv], [cv])
                S.op("dve", lambda e, p=p, m=m: e.scalar_tensor_tensor(out=cv[:, 0:256], in0=p[:, 2:258],
                                                                        scalar=convw[:, m, 2:3], in1=cv[:, 0:256],
                                                                        op0=ALU.mult, op1=ALU.add), [p, convw, cv], [cv])
                S.op("act", lambda e, m=m: e.activation(out=xbcT[:, m, :], in_=cv[:, 0:256], func=AF.Silu,
                                                         bias=convb[:, m:m + 1], scale=1.0), [cv, convb], [xbcT])
            for m in range(4):
                pv, pc = pb(), pb()
                for kc in range(8):
                    mm(pv[:, 0:258], WA(kc, A_SCV + m * 128, A_SCV + (m + 1) * 128), hTw[:, kc, :], kc == 0, kc == 7,
                       [W_A, hTw], [pv])
                for kc in range(8):
                    mm(pc[:, 0:258], WA(kc, A_SCC + m * 128, A_SCC + (m + 1) * 128), hTw[:, kc, :], kc == 0, kc == 7,
                       [W_A, hTw], [pc])
                S.op("act", lambda e, pv=pv: e.copy(out=cv2[:], in_=pv[:, 0:258]), [pv], [cv2])
                S.op("dve", lambda e, pc=pc: e.tensor_tensor(out=cv2[:], in0=pc[:, 0:258], in1=cv2[:], op=ALU.mult),
                     [pc, cv2], [cv2])
                S.op("dve", lambda e, m=m: e.tensor_scalar_mul(out=cvc[:, m, :], in0=cv2[:, 0:256],
                                                                 scalar1=scw[:, m, 0:1]), [cv2, scw], [cvc])
                S.op("dve", lambda e, m=m: e.scalar_tensor_tensor(out=cvc[:, m, :], in0=cv2[:, 1:257],
                                                                    scalar=scw[:, m, 1:2], in1=cvc[:, m, :],
                                                                    op0=ALU.mult, op1=ALU.add), [cv2, scw, cvc], [cvc])
                S.op("dve", lambda e, m=m: e.scalar_tensor_tensor(out=cvc[:, m, :], in0=cv2[:, 2:258],
                                                                    scalar=scw[:, m, 2:3], in1=cvc[:, m, :],
                                                                    op0=ALU.mult, op1=ALU.add), [cv2, scw, cvc], [cvc])
            for c2 in range(2):
                S.dma("sp", cvc_d[gc0 + c2], cvc[:, :, c2 * 128:(c2 + 1) * 128], reads=[cvc], writes=["cvc_d"])
            pk, pks = pb(), pb()
            for kc in range(8):
                mm(pk[:, 0:258], WA(kc, A_K, A_K + 128), hTw[:, kc, :], kc == 0, kc == 7, [W_A, hTw], [pk])
            if is_ctx:
                S.op("act", lambda e: e.copy(out=ktile[:], in_=pk[:, 1:257]), [pk], [ktile])
            else:
                for kc in range(8):
                    mm(pks[:, 0:258], WA(kc, A_KSW, A_KSW + 128), hTw[:, kc, :], kc == 0, kc == 7, [W_A, hTw], [pks])
                t0 = (gc0 - 2) * 128
                S.dma("sp", ropeT[:], rope_in[:, :, t0:t0 + 256], writes=[ropeT])
                S.op("dve", lambda e: e.tensor_tensor(out=cv[:, 0:256], in0=pk[:, 1:257], in1=ropeT[:, 0, :], op=ALU.mult),
                     [pk, ropeT], [cv])
                S.op("dve", lambda e: e.tensor_tensor(out=cv2[:, 0:256], in0=pks[:, 1:257], in1=ropeT[:, 1, :], op=ALU.mult),
                     [pks, ropeT], [cv2])
                S.op("pool", lambda e: e.tensor_tensor(out=ktile[:], in0=cv[:, 0:256], in1=cv2[:, 0:256], op=ALU.add),
                     [cv, cv2], [ktile])
            S.dma("sp", kt_d[:, gc0 * 128:(gc0 + 2) * 128], ktile[:], reads=[ktile], writes=["kt_d"])
            for c2 in (1, 0):
                passA_chunk(gc0, c2, hTw, xbcT)

    def passA_chunk(gc0, c2, hTw, xbcT):
        if True:
            if True:
                gc = gc0 + c2
                cp_ = str(gc % 2)
                xs, btm, dtda, dtr, vtm = (TA[n][cp_] for n in ("xs", "btm", "dtda", "dtr", "vtm"))
                TS = dict(TA)
                TS.update({"xs": xs, "btm": btm, "dtda": dtda})
                tsl = slice(1 + c2 * 128, 1 + (c2 + 1) * 128)
                csl = slice(c2 * 128, (c2 + 1) * 128)
                p = pb()
                for kc in range(8):
                    mm(p[:, 0:128], hTw[:, kc, tsl], WA(kc, A_V, A_V + 128), kc == 0, kc == 7, [W_A, hTw], [p])
                S.op("act", lambda e, p=p: e.copy(out=vtm[:], in_=p[:, 0:128]), [p], [vtm])
                S.dma("sp", v_d[gc], vtm[:], reads=[vtm], writes=["v_d"])
                p = pb()
                for kc in range(8):
                    mm(p[:, 0:32], hTw[:, kc, tsl], WA(kc, A_DT, A_DT + 32), kc == 0, kc == 7, [W_A, hTw], [p])
                S.op("dve", lambda e, p=p: e.tensor_tensor(out=dtr[:], in0=p[:, 0:32], in1=small[:, 0:32], op=ALU.add),
                     [p, small], [dtr])
                S.op("act", lambda e: e.activation(out=dtr[:], in_=dtr[:], func=AF.Exp), [dtr], [dtr])
                S.op("act", lambda e: e.activation(out=dtda[:, 0:32], in_=dtr[:], func=AF.Ln, bias=1.0, scale=1.0),
                     [dtr], [dtda])
                S.op("dve", lambda e: e.tensor_tensor(out=dtda[:, 32:64], in0=dtda[:, 0:32], in1=small[:, 32:64],
                                                       op=ALU.mult), [dtda, small], [dtda])
                S.dma("sp", dtda_d[gc], dtda[:], reads=[dtda], writes=["dtda_d"])
                for m in range(8):
                    tr(psT[:, m * 128:(m + 1) * 128], xbcT[:, m, csl], [xbcT], [psT])
                S.op("act", lambda e: e.copy(out=xs[:], in_=psT[:]), [psT], [xs])
                S.dma("sp", xs_d[gc], xs[:], reads=[xs], writes=["xs_d"])
                for m in range(2):
                    tr(psT[:, m * 128:(m + 1) * 128], xbcT[:, 8 + m, csl], [xbcT], [psT])
                S.op("dve", lambda e: e.tensor_copy(out=btm[:], in_=psT[:, 0:256]), [psT], [btm])
                S.dma("sp", btm_d[gc], btm[:], reads=[btm], writes=["btm_d"])
                S.dma("sp", bt_d[gc], xbcT[:, 8:10, csl], reads=[xbcT], writes=["bt_d"])
                S.dma("sp", ct_d[gc], xbcT[:, 10:12, csl], reads=[xbcT], writes=["ct_d"])
                state_update(TS, "SL", (48, 64), (16, 32), gc, store_d=sb_d)

    WQ = lambda kc, c0, c1: W[:, QOFF + kc * NQ + c0: QOFF + kc * NQ + c1]

    def passQ(l):
        if not pending.pop(("Q", l), False):
            take(pieces_Q(l, TQ), 999)
        pf = pieces_C(l, TQ)
        pending[("C", l)] = True
        sct, qtmp = TQ["sct"], TQ["qtmp"]
        quads = [(0, 2)] + [(2 + 4 * k, 4) for k in range(8)]
        S.begin("pQ")
        for qi, (gc0, nch) in enumerate(quads):
            S.itn = qi
            take(pf, 1)
            passQ_quad(qi, gc0, nch, sct, qtmp)
        take(pf, 999)
        S.end()
        S.emit_scheduled("pQ", overlap=OVL_C)

    def passQ_quad(qi, gc0, nch, sct, qtmp):
        p_ = str(qi % 2)
        hTq, cvq, ropq, scy, QTq = (TQ[n][p_] for n in ("hTq", "cvq", "ropq", "scy", "QTq"))
        N = nch * 128
        is_ctx = gc0 < 2
        for ci in range(nch):
            S.dma("sp", hTq[:, :, ci * 128:(ci + 1) * 128], hT_d[gc0 + ci], reads=["hT_d"], writes=[hTq])
            S.dma("sp", cvq[:, :, ci * 128:(ci + 1) * 128], cvc_d[gc0 + ci], reads=["cvc_d"], writes=[cvq])
        if not is_ctx:
            t0 = (gc0 - 2) * 128
            S.dma("sp", ropq[:, :, 0:N], rope_in[:, :, t0:t0 + N], writes=[ropq])
        for m in range(4):
            pB, pZ = pb(), pb()
            for kc in range(8):
                mm(pB[:, 0:N], WQ(kc, Q_SCB + m * 128, Q_SCB + (m + 1) * 128), hTq[:, kc, 0:N], kc == 0, kc == 7, [W_Q, hTq], [pB])
            for kc in range(8):
                mm(pZ[:, 0:N], WQ(kc, Q_SCZ + m * 128, Q_SCZ + (m + 1) * 128), hTq[:, kc, 0:N], kc == 0, kc == 7, [W_Q, hTq], [pZ])
            S.op("act", lambda e, pZ=pZ: e.activation(out=sct[:, 0:N], in_=pZ[:, 0:N], func=AF.Silu), [pZ], [sct])
            S.op("dve", lambda e, pB=pB, m=m: e.tensor_tensor(out=cvq[:, m, 0:N], in0=pB[:, 0:N], in1=cvq[:, m, 0:N], op=ALU.mult),
                 [pB, cvq], [cvq])
            S.op("pool", lambda e, m=m: e.tensor_tensor(out=scy[:, m, 0:N], in0=cvq[:, m, 0:N], in1=sct[:, 0:N], op=ALU.mult),
                 [cvq, sct], [scy])
        for ci in range(nch):
            S.dma("sp", yc_d[gc0 + ci, :, 8:12, :], scy[:, :, ci * 128:(ci + 1) * 128], reads=[scy], writes=["yc_d"])
        for j in range(4):
            pq = pb()
            for kc in range(8):
                mm(pq[:, 0:N], WQ(kc, Q_Q + j * 128, Q_Q + (j + 1) * 128), hTq[:, kc, 0:N], kc == 0, kc == 7, [W_Q, hTq], [pq])
            if is_ctx:
                S.op("act", lambda e, pq=pq, j=j: e.copy(out=QTq[:, j, 0:N], in_=pq[:, 0:N]), [pq], [QTq])
            else:
                pqs = pb()
                for kc in range(8):
                    mm(pqs[:, 0:N], WQ(kc, Q_QSW + j * 128, Q_QSW + (j + 1) * 128), hTq[:, kc, 0:N], kc == 0, kc == 7,
                       [W_Q, hTq], [pqs])
                S.op("dve", lambda e, pq=pq: e.tensor_tensor(out=qtmp[:, 0:N], in0=pq[:, 0:N], in1=ropq[:, 0, 0:N], op=ALU.mult),
                     [pq, ropq], [qtmp])
                S.op("dve", lambda e, pqs=pqs: e.tensor_tensor(out=qtmp[:, 512:512 + N], in0=pqs[:, 0:N], in1=ropq[:, 1, 0:N],
                                                               op=ALU.mult), [pqs, ropq], [qtmp])
                S.op("pool", lambda e, j=j: e.tensor_tensor(out=QTq[:, j, 0:N], in0=qtmp[:, 0:N], in1=qtmp[:, 512:512 + N],
                                                            op=ALU.add), [qtmp], [QTq])
        for ci in range(nch):
            S.dma("sp", qt_d[gc0 + ci], QTq[:, :, ci * 128:(ci + 1) * 128], reads=[QTq], writes=["qt_d"])
        for g in range(2):
            zaq = TQ["zaq"][str(g)]
            for j in range(4):
                hq = g * 4 + j
                pza = pb()
                for kc in range(8):
                    mm(pza[0:64, 0:N], WQ(kc, Q_ZA + hq * 64, Q_ZA + (hq + 1) * 64), hTq[:, kc, 0:N], kc == 0, kc == 7,
                       [W_Q, hTq], [pza])
                S.op("act", lambda e, pza=pza, j=j, zaq=zaq: e.activation(out=zaq[:, j, 0:N], in_=pza[0:64, 0:N], func=AF.Silu),
                     [pza], [zaq])
            for ci in range(nch):
                S.dma("sp", za_d[gc0 + ci, :, g * 4:(g + 1) * 4, :], zaq[:, :, ci * 128:(ci + 1) * 128], reads=[zaq],
                      writes=["za_d"])

    WC = lambda kc, c0, c1: W[:, kc * NCC + c0: kc * NCC + c1]
    WO0 = W3OFF
    WOA = WO0 + 12 * 1024

    def passC(l):
        last = (l == DEPTH - 1)
        rb = [TC["rb0"], TC["rb1"]]
        Lm = [TC["Lm0"], TC["Lm1"]]
        (cbm, hilo, dres, S16f) = (TC[n] for n in ("cbm", "hilo", "dres", "S16"))
        if not pending.pop(("C", l), False):
            take(pieces_C(l, TC), 999)
        pf = pieces_3(l, TC)
        pending[("3", l)] = True
        S.op("pool", lambda e: e.memset(TC["S"][:], 0.0), [], [TC["S"]])
        S.op("pool", lambda e: e.memset(TC["S16"][:], 0.0), [], [TC["S16"]])

        def tiles(gc):
            p = str(gc % NBUF_C)
            return {n: TC[n][p] for n in ("xs", "btm", "dtda", "hT", "btct", "sbin", "KTb", "Vb", "sz", "Ex",
                                          "M_f", "M_b", "xdt_f", "xdt_b", "QT", "szA")}

        def kblocks(gc):
            if gc < 2:
                return [(0, None), (1, None)]
            n = gc - 2
            kbs = []
            if n > 0:
                kbs.append((gc - 1, "LI"))
            kbs.append((gc, None))
            if n < 31:
                kbs.append((gc + 1, "UI"))
            return kbs + [(0, None), (1, None)]

        def stageF(gc):
            t = tiles(gc)
            xs, btm, dtda, hT, btct, sbin, KTb, Vb, sz, Ex, QT, szA = (t[n] for n in (
                "xs", "btm", "dtda", "hT", "btct", "sbin", "KTb", "Vb", "sz", "Ex", "QT", "szA"))
            Mm = {"f": t["M_f"], "b": t["M_b"]}
            xdt = {"f": t["xdt_f"], "b": t["xdt_b"]}
            is_ctx = gc < 2
            S.dma("sp", xs[:], xs_d[gc], reads=["xs_d"], writes=[xs])
            S.dma("sp", btm[:], btm_d[gc], reads=["btm_d"], writes=[btm])
            S.dma("sp", dtda[:], dtda_d[gc], reads=["dtda_d"], writes=[dtda])
            if is_ctx and last:
                return
            S.dma("sp", hT[:], hT_d[gc], reads=["hT_d"], writes=[hT])
            S.dma("sp", btct[:, 0:2, :], bt_d[gc], reads=["bt_d"], writes=[btct])
            S.dma("sp", btct[:, 2:4, :], ct_d[gc], reads=["ct_d"], writes=[btct])
            S.dma("sp", sbin[:], sb_d[gc], reads=["sb_d"], writes=[sbin])
            S.dma("sp", QT[:], qt_d[gc], reads=["qt_d"], writes=[QT])
            S.dma("sp", szA[:].rearrange("p g j t -> p (g j) t"), za_d[gc], reads=["za_d"], writes=[szA])
            kbs = kblocks(gc)
            for i, (kg, _) in enumerate(kbs):
                S.dma("sp", KTb[:, i, :], kt_d[:, kg * 128:(kg + 1) * 128], reads=["kt_d"], writes=[KTb])
                S.dma("sp", Vb[:, i, :], v_d[kg], reads=["v_d"], writes=[Vb])
            S.op("dve", lambda e: e.tensor_copy(out=hilo[:, 0:32], in_=dtda[:, 32:64]), [dtda], [hilo])
            S.op("dve", lambda e: e.tensor_tensor(out=dres[:], in0=dtda[:, 32:64], in1=hilo[:, 0:32], op=ALU.subtract),
                 [dtda, hilo], [dres])
            S.op("dve", lambda e: e.tensor_copy(out=hilo[:, 32:64], in_=dres[:]), [dres], [hilo])
            pz = [pb(), pb()]
            for h2 in range(2):
                for kc in range(8):
                    mm(pz[h2][:, 0:512], hT[:, kc, :], WC(kc, C_Z + h2 * 512, C_Z + (h2 + 1) * 512), kc == 0, kc == 7,
                       [W_C, hT], [pz[h2]])
                S.op("act", lambda e, h2=h2: e.activation(out=sz[:, h2 * 512:(h2 + 1) * 512], in_=pz[h2][:, 0:512],
                                                           func=AF.Silu), [pz[h2]], [sz])
            p = pb()
            mm(p[:, 32:48], cm_f["UI"][:], dtda[:, 32:48], True, True, [cm_f["UI"], dtda], [p])
            mm(p[:, 48:64], cm_f["LI"][:], dtda[:, 48:64], True, True, [cm_f["LI"], dtda], [p])
            S.op("act", lambda e: e.activation(out=Ex[:, 32:64], in_=p[:, 32:64], func=AF.Exp), [p], [Ex])
            pcb = pb()
            for g in range(2):
                mm(pcb[:, g * 128:(g + 1) * 128], btct[:, g, :], btct[:, 2 + g, :], True, True, [btct], [pcb])
            for di, mk in enumerate(("UI", "LI")):
                S.op("dve", lambda e, di=di, mk=mk: e.tensor_tensor(
                    out=cbm[:, di, :, :], in0=pcb[:, 0:256].rearrange("p (g l) -> p g l", g=2),
                    in1=cm_f[mk][:].unsqueeze(1).to_broadcast([128, 2, 128]), op=ALU.mult), [pcb, cm_f[mk]], [cbm])
            li = 0
            for di, (dk, lk, mk) in enumerate((("f", "SU", "UI"), ("b", "SL", "LI"))):
                for part in range(2):
                    eng = "dve"
                    S.op(eng, lambda e, part=part, di=di, mk=mk: e.tensor_tensor(
                        out=rb[part][:],
                        in0=hilo[:, part * 32 + di * 16: part * 32 + di * 16 + 16].unsqueeze(2).to_broadcast([128, 16, 128]),
                        in1=cm_b[mk][:].unsqueeze(1).to_broadcast([128, 16, 128]), op=ALU.mult),
                        [hilo, cm_b[mk]], [rb[part]])
                for q4 in range(4):
                    pD = pb()
                    for part in range(2):
                        mm(pD[:, 0:512], cm_b[lk][:], rb[part][:, q4 * 4:(q4 + 1) * 4, :].rearrange("p h l -> p (h l)"),
                           part == 0, part == 1, [cm_b[lk], rb[part]], [pD])
                    Lq = Lm[li % 2]
                    li += 1
                    S.op("act", lambda e, pD=pD, Lq=Lq: e.activation(out=Lq[:].rearrange("p h l -> p (h l)"),
                                                                      in_=pD[:, 0:512], func=AF.Exp), [pD], [Lq])
                    g = q4 // 2
                    eng = "pool"
                    S.op(eng, lambda e, dk=dk, di=di, q4=q4, g=g, Lq=Lq: e.tensor_tensor(
                        out=Mm[dk][:, q4 * 4:(q4 + 1) * 4, :], in0=Lq[:],
                        in1=cbm[:, di, g, :].unsqueeze(1).to_broadcast([128, 4, 128]), op=ALU.mult),
                        [Lq, cbm], [Mm[dk]])
                S.op("dve", lambda e, dk=dk, di=di: e.tensor_tensor(
                    out=xdt[dk][:].rearrange("p (h q) -> p h q", h=16), in0=xs[:].rearrange("p (h q) -> p h q", h=16),
                    in1=dtda[:, di * 16:(di + 1) * 16].unsqueeze(2).to_broadcast([128, 16, 64]), op=ALU.mult),
                    [xs, dtda], [xdt[dk]])

        def stageB(gc):
            t = tiles(gc)
            xs, btm, dtda, btct, sbin, KTb, Vb, sz, Ex, QT, szA = (t[n] for n in (
                "xs", "btm", "dtda", "btct", "sbin", "KTb", "Vb", "sz", "Ex", "QT", "szA"))
            Mm = {"f": t["M_f"], "b": t["M_b"]}
            xdt = {"f": t["xdt_f"], "b": t["xdt_b"]}
            bp = str(gc % 2)
            tmpA, tmpB, ysb, ysT, OG, rden, st8 = (TC[n][bp] for n in ("tmpA", "tmpB", "ysb", "ysT", "OG", "rden", "stB"))
            PT = [TC["PT0"][bp], TC["PT1"][bp]]
            TS = dict(TC)
            TS.update({"xs": xs, "btm": btm, "dtda": dtda, "rhsb": TC["rhsb"][bp], "wsm": TC["wsm"][bp], "ExS": TC["ExS"][bp]})
            is_ctx = gc < 2
            if is_ctx and last:
                state_update(TS, "SU", (32, 48), (0, 16), gc)
                return
            kbs = kblocks(gc)
            for g in range(2):
                gs = slice(g * 64, (g + 1) * 64)
                po, pd = pb(), pb()
                pscs = [pb(), pb()]
                nk = len(kbs)
                for i, (kg, mk) in enumerate(kbs):
                    psc = pscs[i % 2]
                    mm(psc[:, 0:512], KTb[gs, i, :], QT[gs, :, :].rearrange("p j t -> p (j t)"), True, mk is None, [KTb, QT], [psc])
                    if mk is not None:
                        mm(psc[:, 0:512], ident_b[:], negm[mk][:].rearrange("p j t -> p (j t)"), False, True,
                           [ident_b, negm[mk]], [psc])
                    pt = PT[i % 2]
                    S.op("act", lambda e, psc=psc, pt=pt: e.activation(out=pt[:], in_=psc[:, 0:512], func=AF.Exp, scale=0.125),
                         [psc], [pt])
                    mm(po[0:64, 0:512], Vb[:, i, gs], pt[:], i == 0, i == nk - 1, [Vb, pt], [po])
                    mm(pd[0:64, 0:512], cm_b["ones"][:, 0:64], pt[:], i == 0, i == nk - 1, [cm_b["ones"], pt], [pd])
                S.op("dve", lambda e, g=g, pd=pd: e.tensor_tensor(
                    out=rden[:], in0=pd[0:64, 0:512].rearrange("p (j t) -> p j t", j=4),
                    in1=small[0:64, 80 + g * 4: 84 + g * 4].unsqueeze(2).to_broadcast([64, 4, 128]), op=ALU.add),
                    [pd, small], [rden])
                S.op("dve", lambda e: e.reciprocal(out=rden[:], in_=rden[:]), [rden], [rden])
                S.op("dve", lambda e, po=po: e.tensor_tensor(out=rden[:].rearrange("p j t -> p (j t)"), in0=po[0:64, 0:512],
                                                              in1=rden[:].rearrange("p j t -> p (j t)"), op=ALU.mult),
                     [po, rden], [rden])
                S.op("pool", lambda e, g=g: e.tensor_tensor(out=OG[:, g, :, :], in0=rden[:], in1=szA[:, g, :, :], op=ALU.mult),
                     [rden, szA], [OG])
            S.dma("sp", og_d[gc], OG[:].rearrange("p g j t -> p (g j) t"), reads=[OG], writes=["og_d"])
            py = [pb(), pb()]
            for h in range(16):
                o = py[h // 8][:, (h % 8) * 64:(h % 8 + 1) * 64]
                mm(o, Mm["f"][:, h, :], xdt["f"][:, h * 64:(h + 1) * 64], True, False, [Mm["f"], xdt["f"]], [py[h // 8]])
                mm(o, Mm["b"][:, h, :], xdt["b"][:, h * 64:(h + 1) * 64], False, True, [Mm["b"], xdt["b"]], [py[h // 8]])
            pof = [pb(), pb()]
            for g in range(2):
                mm(pof[g][:, 0:512], btct[:, 2 + g, :], S16f[:, g * 512:(g + 1) * 512], True, True, [btct, S16f], [pof[g]])
            for g in range(2):
                S.op("dve", lambda e, g=g: e.tensor_tensor(
                    out=tmpA[:, g * 512:(g + 1) * 512].rearrange("p (h q) -> p h q", h=8),
                    in0=pof[g][:, 0:512].rearrange("p (h q) -> p h q", h=8),
                    in1=Ex[:, 32 + g * 8: 32 + (g + 1) * 8].unsqueeze(2).to_broadcast([128, 8, 64]), op=ALU.mult),
                    [pof[g], Ex], [tmpA])
                S.op("dve", lambda e, g=g: e.tensor_tensor(out=tmpA[:, g * 512:(g + 1) * 512], in0=tmpA[:, g * 512:(g + 1) * 512],
                                                            in1=py[g][:, 0:512], op=ALU.add), [tmpA, py[g]], [tmpA])
            pob = [pb(), pb()]
            for g in range(2):
                mm(pob[g][:, 0:512], btct[:, 2 + g, :], sbin[:, g * 512:(g + 1) * 512], True, True, [btct, sbin], [pob[g]])
            for g in range(2):
                S.op("dve", lambda e, g=g: e.tensor_tensor(
                    out=tmpB[:, g * 512:(g + 1) * 512].rearrange("p (h q) -> p h q", h=8),
                    in0=pob[g][:, 0:512].rearrange("p (h q) -> p h q", h=8),
                    in1=Ex[:, 48 + g * 8: 48 + (g + 1) * 8].unsqueeze(2).to_broadcast([128, 8, 64]), op=ALU.mult),
                    [pob[g], Ex], [tmpB])
            S.op("pool", lambda e: e.tensor_tensor(out=tmpA[:], in0=tmpA[:], in1=tmpB[:], op=ALU.add), [tmpA, tmpB], [tmpA])
            S.op("pool", lambda e: e.tensor_tensor(
                out=tmpB[:].rearrange("p (h q) -> p h q", h=16), in0=xs[:].rearrange("p (h q) -> p h q", h=16),
                in1=small[:, 64:80].unsqueeze(2).to_broadcast([128, 16, 64]), op=ALU.mult), [xs, small], [tmpB])
            S.op("pool", lambda e: e.tensor_tensor(out=tmpA[:], in0=tmpA[:], in1=tmpB[:], op=ALU.add), [tmpA, tmpB], [tmpA])
            S.op("dve", lambda e: e.tensor_tensor(out=tmpA[:], in0=tmpA[:], in1=sz[:], op=ALU.mult), [tmpA, sz], [tmpA])
            S.op("act", lambda e: e.activation(out=tmpB[:], in_=tmpA[:], func=AF.Square), [tmpA], [tmpB])
            S.op("dve", lambda e: e.reduce_sum(out=st8[:, 2:4], in_=tmpB[:].rearrange("p (g q) -> p g q", g=2), axis=AX.X),
                 [tmpB], [st8])
            S.op("dve", lambda e: e.tensor_scalar(out=st8[:, 2:4], in0=st8[:, 2:4], scalar1=1.0 / 512, scalar2=EPS,
                                                   op0=ALU.mult, op1=ALU.add), [st8], [st8])
            S.op("act", lambda e: e.activation(out=st8[:, 2:4], in_=st8[:, 2:4], func=AF.Sqrt), [st8], [st8])
            S.op("dve", lambda e: e.reciprocal(out=st8[:, 2:4], in_=st8[:, 2:4]), [st8], [st8])
            for g in range(2):
                S.op("dve", lambda e, g=g: e.scalar_tensor_tensor(
                    out=ysb[:, g * 512:(g + 1) * 512], in0=tmpA[:, g * 512:(g + 1) * 512], scalar=st8[:, 2 + g:3 + g],
                    in1=snw_b[:, g * 512:(g + 1) * 512], op0=ALU.mult, op1=ALU.mult), [tmpA, st8, snw_b], [ysb])
            for m in range(8):
                tr(psT[:, m * 128:(m + 1) * 128], ysb[:, m * 128:(m + 1) * 128], [ysb], [psT])
            S.op("act", lambda e: e.copy(out=ysT[:].rearrange("p k t -> p (k t)"), in_=psT[:]), [psT], [ysT])
            S.dma("sp", yc_d[gc, :, 0:8, :], ysT[:], reads=[ysT], writes=["yc_d"])
            state_update(TS, "SU", (32, 48), (0, 16), gc)

        S.begin("passC")
        for gc in range(NCH):
            S.itn = gc
            take(pf, 1)
            bsel[0] = (0, 3)
            stageF(gc)
            bsel[0] = (3, 7)
            stageB(gc)
        bsel[0] = None
        take(pf, 999)
        S.end()
        S.emit_scheduled("passC", overlap=OVL_C)

    def passC3(l):
        last = (l == DEPTH - 1)
        if not pending.pop(("3", l), False):
            take(pieces_3(l, T3), 999)
        pf = pieces_A(l + 1, T3) if l + 1 < DEPTH else []
        if pf:
            pending[("A", l + 1)] = True
        S.begin("p3")
        for gc in range(NCH):
            S.itn = gc
            take(pf, 1)
            is_ctx = gc < 2
            if is_ctx and last:
                continue
            p = str(gc % 2)
            yc, og, xin, xnew, tmpB, stt = (T3[n][p] for n in ("yc", "og", "xin", "xnew", "tmpB", "st"))
            if is_ctx:
                src = (ctx_in if l == 0 else ctx1_d)[gc * 128:(gc + 1) * 128, :]
            else:
                src = (x_in if l == 0 else x1_d)[(gc - 2) * 128:(gc - 1) * 128, :]
            S.dma("sp", yc[:], yc_d[gc], reads=["yc_d"], writes=[yc])
            ogv = og_d[gc].rearrange("d (t two) k -> two d t k", two=2)
            S.dma("sp", og[0:64, :, :], ogv[0], reads=["og_d"], writes=[og])
            S.dma("sp", og[64:128, :, :], ogv[1], reads=["og_d"], writes=[og])
            S.dma("sp", xin[:], src, reads=["x1_d", "ctx1_d"], writes=[xin])
            pout = [pb(), pb()]
            g_t = aux if is_ctx else g_l
            for h2 in range(2):
                ns = slice(h2 * 512, (h2 + 1) * 512)
                steps = []
                for t in range(12):
                    steps.append((yc[:, t, :], W[:, WO0 + t * 1024 + h2 * 512: WO0 + t * 1024 + (h2 + 1) * 512], [yc, W_3]))
                for t in range(4):
                    steps.append((og[:, t, :], W[:, WO0 + (12 + t) * 1024 + h2 * 512: WO0 + (12 + t) * 1024 + (h2 + 1) * 512], [og, W_3]))
                for i, (lt, rh, rd) in enumerate(steps):
                    mm(pout[h2][:, 0:512], lt, rh, i == 0, i == len(steps) - 1, rd, [pout[h2]])
                S.op("dve", lambda e, h2=h2, ns=ns, g_t=g_t, xnew=xnew, pout=pout: e.tensor_tensor(
                    out=xnew[:, ns], in0=pout[h2][:, 0:512], in1=g_t[:, ns], op=ALU.mult), [pout[h2], g_t], [xnew])
            S.op("pool", lambda e, xnew=xnew, xin=xin: e.tensor_tensor(out=xnew[:], in0=xnew[:], in1=xin[:], op=ALU.add),
                 [xnew, xin], [xnew])
            if not last:
                if is_ctx:
                    S.dma("sp", ctx1_d[gc * 128:(gc + 1) * 128, :], xnew[:], reads=[xnew], writes=["ctx1_d"])
                else:
                    S.dma("sp", x1_d[(gc - 2) * 128:(gc - 1) * 128, :], xnew[:], reads=[xnew], writes=["x1_d"])
            else:
                rms_rstd(xnew, tmpB, D, st8=stt)
                S.op("dve", lambda e, xnew=xnew, tmpB=tmpB, stt=stt: e.scalar_tensor_tensor(
                    out=tmpB[:], in0=xnew[:], scalar=stt[:, 0:1], in1=aux[:], op0=ALU.mult, op1=ALU.mult),
                    [xnew, stt, aux], [tmpB])
                S.dma("sp", out_d[(gc - 2) * 128:(gc - 1) * 128, :], tmpB[:], reads=[tmpB], writes=["out_d"])
        take(pf, 999)
        S.end()
        S.emit_scheduled("p3", overlap=OVL_C)

    build_consts()
    for l in range(n_layers):
        S.barrier()
        layer_setup(l)
        if stop_after == (l, "setup"):
            break
        pass0(l)
        if stop_after == (l, "p0"):
            break
        S.barrier()
        passA(l)
        if stop_after == (l, "pA"):
            break
        S.barrier()
        passQ(l)
        S.barrier()
        passC(l)
        S.barrier()
        passC3(l)
    S.wait_all("sp")
    build_program.stats = (S.ninst, S.nwaits, getattr(S, "sim_time", 0.0))
    build_program.sim_log = getattr(S, "sim_log", [])
    return nc


def _rope_table():
    f32 = np.float32
    tok = np.arange(T_LAT)
    rows = (tok // 64).astype(f32)
    cols = (tok % 64).astype(f32)
    inv = (f32(10000.0) ** (-(np.arange(0, 32, 2).astype(f32)) / f32(32))).astype(f32)
    ang = np.concatenate([rows[:, None] * inv[None, :], cols[:, None] * inv[None, :]], axis=-1).astype(f32)
    cos, sin = np.cos(ang).astype(f32), np.sin(ang).astype(f32)
    tab = np.zeros((128, 2, T_LAT), f32)
    for p in range(128):
        d = p % 64
        a, r = divmod(d, 32)
        j, i = divmod(r, 16)
        tab[p, 0] = cos[:, a * 16 + i]
        tab[p, 1] = sin[:, a * 16 + i] * (-1.0 if j == 0 else 1.0)
    return tab


def _prep_shared(inp):
    f = lambda a: np.ascontiguousarray(np.asarray(a, dtype=np.float32))
    cA, cC, cQ = np.array(_cols_A()), np.array(_cols_C()), np.array(_cols_Q())
    w_in = np.asarray(inp["w_in"], dtype=np.float32)
    w_out = np.asarray(inp["w_out"], dtype=np.float32)
    sh = {
        "norm_w": f(inp["norm_w"]).reshape(DEPTH, 1, D),
        "w_mod": f(inp["w_mod"]),
        "b_mod": f(inp["b_mod"]).reshape(DEPTH, 1, 3 * D),
        "wA": f(w_in[:, :, cA]),
        "wC": f(w_in[:, :, cC]),
        "wQ": f(w_in[:, :, cQ]),
        "wo_all": f(w_out),
        "convw": f(np.asarray(inp["ssd_conv_w"]).reshape(DEPTH, 3, 12, 128).transpose(0, 3, 2, 1)),
        "convb": f(np.asarray(inp["ssd_conv_b"]).reshape(DEPTH, 12, 128).transpose(0, 2, 1)),
        "scw": f(np.asarray(inp["sc_conv_w"]).reshape(DEPTH, 3, 4, 128).transpose(0, 3, 2, 1)),
        "dt_bias": f(inp["ssd_dt_bias"]).reshape(DEPTH, 1, 32),
        "a_log": f(inp["ssd_a_log"]).reshape(DEPTH, 1, 32),
        "ssd_d": f(inp["ssd_d"]).reshape(DEPTH, 1, 16),
        "ssd_norm_w": f(inp["ssd_norm_w"]).reshape(DEPTH, 1, D),
        "sink": f(inp["attn_sink"]).reshape(DEPTH, 1, 8),
        "final_norm_w": f(inp["final_norm_w"]).reshape(1, D),
        "ropecs": _rope_table(),
    }
    return sh


def _in_maps(inp, n_cores=8):
    sh = _prep_shared(inp)
    x = np.asarray(inp["x"], dtype=np.float32)
    c = np.asarray(inp["c"], dtype=np.float32)
    ctx = np.asarray(inp["ctx"], dtype=np.float32)
    c_ctx = np.asarray(inp["c_ctx"], dtype=np.float32)
    maps = []
    for core in range(n_cores):
        b = core % 4
        cc = np.stack([c[b].reshape(8, 128).T, c_ctx.reshape(8, 128).T], axis=-1)
        m = dict(sh)
        m["x"] = np.ascontiguousarray(x[b])
        m["ctx"] = np.ascontiguousarray(ctx[b])
        m["cc"] = np.ascontiguousarray(cc.astype(np.float32))
        maps.append(m)
    return maps


def kernel(x, c, ctx, c_ctx, norm_w, w_mod, b_mod, w_in, ssd_conv_w, ssd_conv_b, ssd_dt_bias, ssd_a_log, ssd_d,
           ssd_norm_w, sc_conv_w, attn_sink, w_out, final_norm_w):
    inp = dict(x=x, c=c, ctx=ctx, c_ctx=c_ctx, norm_w=norm_w, w_mod=w_mod, b_mod=b_mod, w_in=w_in,
               ssd_conv_w=ssd_conv_w, ssd_conv_b=ssd_conv_b, ssd_dt_bias=ssd_dt_bias, ssd_a_log=ssd_a_log,
               ssd_d=ssd_d, ssd_norm_w=ssd_norm_w, sc_conv_w=sc_conv_w, attn_sink=attn_sink, w_out=w_out,
               final_norm_w=final_norm_w)
    nc = build_program()
    maps = _in_maps(inp, 8)
    res = run_bass_kernel_spmd(nc, maps, core_ids=list(range(8)))
    out = np.stack([np.asarray(res.results[b]["out"], dtype=np.float32).reshape(T_LAT, D) for b in range(4)], axis=0)
    return out
```

```python
import numpy as np
import concourse.bass as bass
import concourse.mybir as mybir
from concourse.bass_utils import run_bass_kernel_spmd

F32 = mybir.dt.float32
BF16 = mybir.dt.bfloat16
AF = mybir.ActivationFunctionType
ALU = mybir.AluOpType
AX = mybir.AxisListType

D = 1024
T_LAT = 4096
T_CTX = 256
NCH = 34
T_ALL = NCH * 128
DEPTH = 2
EPS = 1e-6

O_Z = 0
O_XS = 1024
O_B = 2048
O_C = 2304
O_DT = 2560
O_SCV = 2592
O_SCC = 3104
O_SCB = 3616
O_SCZ = 4128
O_Q = 4640
O_K = 5152
O_V = 5280
O_ZA = 5408


def _rope_partner(d):
    a, r = divmod(d, 32)
    j, i = divmod(r, 16)
    return a * 32 + (1 - j) * 16 + i


def _cols_A():
    cols = list(range(O_XS, O_XS + 1536))
    cols += list(range(O_SCV, O_SCV + 512))
    cols += list(range(O_SCC, O_SCC + 512))
    cols += list(range(O_K, O_K + 128))
    cols += [O_K + g * 64 + _rope_partner(d) for g in range(2) for d in range(64)]
    cols += list(range(O_V, O_V + 128))
    cols += list(range(O_DT, O_DT + 32))
    return cols


A_XBC, A_SCV, A_SCC, A_K, A_KSW, A_V, A_DT, NA = 0, 1536, 2048, 2560, 2688, 2816, 2944, 2976


def _cols_C():
    return list(range(O_Z, O_Z + 1024))


def _cols_Q():
    cols = list(range(O_SCB, O_SCB + 512))
    cols += list(range(O_SCZ, O_SCZ + 512))
    qt = []
    qs = []
    for j in range(4):
        for hq in (j, 4 + j):
            qt += [O_Q + hq * 64 + d for d in range(64)]
            qs += [O_Q + hq * 64 + _rope_partner(d) for d in range(64)]
    cols += qt + qs
    cols += list(range(O_ZA, O_ZA + 512))
    return cols


C_Z, NCC = 0, 1024
Q_SCB, Q_SCZ, Q_Q, Q_QSW, Q_ZA, NQ = 0, 512, 1024, 1536, 2048, 2560


class Tl:
    def __init__(self, t, k):
        self.t = t
        self.k = k

    def __getitem__(self, idx):
        return self.t[idx]


class Sched:
    def __init__(self, nc, n_dma_sems=12):
        self.nc = nc
        self.engs = {"pe": nc.tensor, "act": nc.scalar, "dve": nc.vector,
                     "pool": nc.gpsimd, "sp": nc.sync}
        self.sem = {}
        self.cnt = {}
        for e in self.engs:
            self.sem[e] = nc.alloc_semaphore("s_" + e)
            self.cnt[e] = 0
        self.dring = {}
        for q in ("sp", "act", "pool"):
            self.dring[q] = [[nc.alloc_semaphore("d_%s%d" % (q, i)), 0] for i in range(n_dma_sems)]
        self.dpos = {q: 0 for q in self.dring}
        self.seen = {e: {} for e in self.engs}
        self.state = {}
        self.ninst = 0
        self.nwaits = 0
        self.cur = None
        self.stages = {}
        self.itn = 0

    def begin(self, name):
        self.cur = self.stages.setdefault(name, [])

    def end(self):
        self.cur = None

    def emit(self, name):
        assert self.cur is None
        for item in self.stages.pop(name, []):
            self._emit_item(item)

    def emit_scheduled(self, name, overlap=3.0):
        assert self.cur is None
        items = self.stages.pop(name, [])
        n = len(items)
        if n == 0:
            return

        class _Probe:
            def __init__(self):
                self.nm, self.a, self.k = None, (), {}

            def __getattr__(self, nm):
                def f(*a, **k):
                    self.nm, self.a, self.k = nm, a, k
                    return self
                return f

        def cost_of(it):
            if it[0] == "dma":
                return 0.1, 2.2
            pr = _Probe()
            try:
                it[2](pr)
                out = pr.k.get("out", pr.a[0] if pr.a else None)
                shp = tuple(out.shape)
                nel = 1
                for d_ in shp[1:]:
                    nel *= int(d_)
            except Exception:
                nel = 512
            eng = it[1]
            if eng == "pe":
                c = 0.12 if pr.nm == "transpose" else 0.04 + nel / 1400.0
            elif eng == "act":
                c = 0.2 + nel / 1150.0
            elif eng == "dve":
                c = 0.15 + nel / 680.0
            else:
                c = 0.2 + nel / 580.0
            return c, c

        last_w = {}
        readers = {}
        preds = [set() for _ in range(n)]
        for i, it in enumerate(items):
            if it[0] == "dma":
                reads, writes = it[4], it[5]
            else:
                reads, writes = it[3], it[4]
            for k in reads:
                w = last_w.get(k)
                if w is not None:
                    preds[i].add(w)
            for k in writes:
                w = last_w.get(k)
                if w is not None:
                    preds[i].add(w)
                for r in readers.get(k, ()):
                    preds[i].add(r)
            for k in reads:
                readers.setdefault(k, []).append(i)
            for k in writes:
                last_w[k] = i
                readers[k] = []
        succs = [[] for _ in range(n)]
        indeg = [0] * n
        for i in range(n):
            preds[i].discard(i)
            indeg[i] = len(preds[i])
            for p_ in preds[i]:
                succs[p_].append(i)
        engs = [it[1] for it in items]
        costs = [cost_of(it) for it in items]
        ready_t = [0.0] * n
        free = {}
        ready = set(i for i in range(n) if indeg[i] == 0)
        window = int(overlap * 1200)
        done = 0
        lowest = 0
        emitted = [False] * n
        while ready:
            best, bkey = None, None
            for i in ready:
                if i - lowest > window:
                    continue
                st = max(ready_t[i], free.get(engs[i], 0.0))
                key = (st, i)
                if bkey is None or key < bkey:
                    best, bkey = i, key
            if best is None:
                best = min(ready)
                bkey = (max(ready_t[best], free.get(engs[best], 0.0)), best)
            i = best
            ready.discard(i)
            st = bkey[0]
            busy, lat = costs[i]
            free[engs[i]] = st + busy
            fin = st + lat
            it = items[i]
            if it[0] == "op":
                self.op(*it[1:5])
            else:
                self.dma(*it[1:6], **it[6])
            emitted[i] = True
            while lowest < n and emitted[lowest]:
                lowest += 1
            done += 1
            for j in succs[i]:
                hop = 0.05 if engs[j] == engs[i] else 0.45
                if fin + hop > ready_t[j]:
                    ready_t[j] = fin + hop
                indeg[j] -= 1
                if indeg[j] == 0:
                    ready.add(j)
        assert done == n, (done, n)
        self.sim_time = getattr(self, "sim_time", 0.0) + max(free.values())
        busy = {}
        for i in range(n):
            busy[engs[i]] = busy.get(engs[i], 0.0) + costs[i][0]
        self.sim_log = getattr(self, "sim_log", [])
        self.sim_log.append((str(name), round(max(free.values()), 1), {k_: round(v_, 1) for k_, v_ in busy.items()}))

    def _emit_item(self, item):
        if item[0] == "op":
            self.op(*item[1:5])
        else:
            self.dma(*item[1:6], **item[6])

    def emit_merged(self, name_a, name_b):
        assert self.cur is None
        la = self.stages.pop(name_a, [])
        lb = self.stages.pop(name_b, [])
        i = j = 0
        while i < len(la) or j < len(lb):
            if j >= len(lb) or (i < len(la) and i * len(lb) <= j * len(la)):
                self._emit_item(la[i])
                i += 1
            else:
                self._emit_item(lb[j])
                j += 1

    @staticmethod
    def _keys(lst):
        return [x.k if isinstance(x, Tl) else x for x in lst]

    def _need(self, eng, ev):
        sem, val, src = ev
        if src == "pe" and eng == "pe":
            return
        cur = self.seen[eng].get(sem.name, 0)
        if cur >= val:
            return
        self.seen[eng][sem.name] = val
        self.engs[eng].wait_ge(sem, val)
        self.nwaits += 1

    def _deps(self, eng, reads, writes):
        for k in reads:
            st = self.state.get(k)
            if st and st[0] is not None:
                self._need(eng, st[0])
        for k in writes:
            st = self.state.get(k)
            if st:
                if st[0] is not None:
                    self._need(eng, st[0])
                for ev in st[1].values():
                    self._need(eng, ev)

    def _record(self, ev, reads, writes):
        for k in reads:
            st = self.state.setdefault(k, [None, {}])
            old = st[1].get(ev[0].name)
            if old is None or old[1] < ev[1]:
                st[1][ev[0].name] = ev
        for k in writes:
            self.state[k] = [ev, {}]

    def op(self, eng, fn, reads=(), writes=()):
        reads = self._keys(reads)
        writes = self._keys(writes)
        if self.cur is not None:
            self.cur.append(("op", eng, fn, reads, writes, self.itn))
            return
        self._deps(eng, reads, writes)
        ins = fn(self.engs[eng])
        self.cnt[eng] += 1
        ins.then_inc(self.sem[eng], 1)
        ev = (self.sem[eng], self.cnt[eng], eng)
        self._record(ev, reads, writes)
        self.ninst += 1

    def dma(self, q, out, in_, reads=(), writes=(), **kw):
        reads = self._keys(reads)
        writes = self._keys(writes)
        if self.cur is not None:
            self.cur.append(("dma", q, out, in_, reads, writes, kw, self.itn))
            return
        ring = self.dring[q]
        slot = ring[self.dpos[q] % len(ring)]
        self.dpos[q] += 1
        sem, tot = slot
        if tot > 0:
            self._need(q, (sem, tot, None))
        self._deps(q, reads, writes)
        ins = self.engs[q].dma_start(out=out, in_=in_, **kw)
        slot[1] = tot + 16
        ins.then_inc(sem, 16)
        ev = (sem, slot[1], None)
        self._record(ev, reads, writes)
        self.ninst += 1

    def barrier(self):
        evs = [(self.sem[e], self.cnt[e], e) for e in self.engs if self.cnt[e] > 0]
        for q in self.dring:
            for sem, tot in self.dring[q]:
                if tot > 0:
                    evs.append((sem, tot, None))
        for e in self.engs:
            for ev in evs:
                self._need(e, ev)

    def wait_all(self, eng="sp"):
        for k, st in list(self.state.items()):
            if st[0] is not None:
                self._need(eng, st[0])
            for ev in st[1].values():
                self._need(eng, ev)


OVL_C = 3.0


def build_program(debug_out=None, n_layers=DEPTH, stop_after=None):
    nc = bass.Bass("TRN2", target_bir_lowering=False)
    S = Sched(nc)

    def din(name, shape, dt=F32):
        return nc.dram_tensor(name, list(shape), dt, kind="ExternalInput").ap()

    dbg = set(debug_out or [])

    def dscr(name, shape, dt=F32):
        kind = "ExternalOutput" if name in dbg else "Internal"
        return nc.dram_tensor(name, list(shape), dt, kind=kind).ap()

    x_in = din("x", [T_LAT, D])
    ctx_in = din("ctx", [T_CTX, D])
    cc_in = din("cc", [128, 8, 2])
    normw_in = din("norm_w", [DEPTH, 1, D])
    wmod_in = din("w_mod", [DEPTH, D, 3 * D])
    bmod_in = din("b_mod", [DEPTH, 1, 3 * D])
    wA_in = din("wA", [DEPTH, D, NA])
    wC_in = din("wC", [DEPTH, D, NCC])
    wQ_in = din("wQ", [DEPTH, D, NQ])
    wom_in = din("wo_all", [DEPTH, 2048, D])
    convw_in = din("convw", [DEPTH, 128, 12, 3])
    convb_in = din("convb", [DEPTH, 128, 12])
    scw_in = din("scw", [DEPTH, 128, 4, 3])
    dtb_in = din("dt_bias", [DEPTH, 1, 32])
    alog_in = din("a_log", [DEPTH, 1, 32])
    dsk_in = din("ssd_d", [DEPTH, 1, 16])
    snw_in = din("ssd_norm_w", [DEPTH, 1, D])
    sink_in = din("sink", [DEPTH, 1, 8])
    fnw_in = din("final_norm_w", [1, D])
    rope_in = din("ropecs", [128, 2, T_LAT])
    out_d = nc.dram_tensor("out", [T_LAT, D], F32, kind="ExternalOutput").ap()

    x1_d = dscr("x1", [T_LAT, D])
    ctx1_d = dscr("ctx1", [T_CTX, D])
    hT_d = dscr("hT_all", [NCH, 128, 8, 128], BF16)
    xs_d = dscr("xs_all", [NCH, 128, 1024], BF16)
    btm_d = dscr("btm_all", [NCH, 128, 256], BF16)
    bt_d = dscr("bt_all", [NCH, 128, 2, 128], BF16)
    ct_d = dscr("ct_all", [NCH, 128, 2, 128], BF16)
    dtda_d = dscr("dtda_all", [NCH, 128, 64])
    sb_d = dscr("sb_all", [NCH, 128, 1024], BF16)
    cvc_d = dscr("cvc_all", [NCH, 128, 4, 128])
    kt_d = dscr("kt_all", [128, T_ALL], BF16)
    v_d = dscr("v_all", [NCH, 128, 128], BF16)
    mod_d = dscr("mod_scr", [2, 3 * D])
    yc_d = dscr("yc_all", [NCH, 128, 12, 128], BF16)
    og_d = dscr("og_all", [NCH, 64, 8, 128], BF16)
    qt_d = dscr("qt_all", [NCH, 128, 4, 128], BF16)
    za_d = dscr("za_all", [NCH, 64, 8, 128])
    sz_d = dscr("sz_all", [NCH, 128, 1024])
    dbg_ys = dscr("dbg_ys", [NCH, 128, 1024], BF16) if "dbg_ys" in dbg else None
    dbg_sc = dscr("dbg_sc", [NCH, 128, 4, 128], BF16) if "dbg_sc" in dbg else None
    dbg_og = dscr("dbg_og", [NCH, 64, 2, 4, 128], BF16) if "dbg_og" in dbg else None

    SB_BASE, SB_END = 16640, 229376
    DTB = {F32: 4, BF16: 2}
    ptr = {"persist": SB_BASE}
    lim = {}

    def _alloc(space, name, shape, dt):
        n = 1
        for d_ in shape[1:]:
            n *= d_
        nbytes = (n * DTB[dt] + 63) // 64 * 64
        off = ptr[space]
        ptr[space] = off + nbytes
        assert ptr[space] <= lim.get(space, SB_END), (space, name, ptr[space])
        uname = "%s_%s" % (space, name)
        return Tl(nc.alloc_sbuf_tensor_at(uname, list(shape), dt, offset=off), uname)

    def sb(name, shape, dt=F32):
        return _alloc("persist", name, shape, dt)

    W = sb("W", [128, 49152], BF16)
    ident_b = sb("ident_b", [128, 128], BF16)
    cm_f = {n: sb("cm_" + n, [128, 128], F32) for n in ("UI", "LI", "SU", "SL", "ones")}
    cm_b = {n: sb("cb_" + n, [128, 128], BF16) for n in ("UI", "LI", "SU", "SL", "ones")}
    negm = {n: sb("negm_" + n, [128, 4, 128], BF16) for n in ("UI", "LI")}
    g_l = sb("g_l", [128, D])
    snw_b = sb("snw_b", [128, D])
    aux = sb("aux", [128, D])
    small = sb("small", [128, 128])
    convw = sb("convw", [128, 12, 3])
    convb = sb("convb", [128, 12])
    scw = sb("scw", [128, 4, 3])
    cc = sb("cc", [128, 8, 2])
    st8 = sb("st8", [128, 8])
    PH_BASE = ptr["persist"]

    def phase_tiles(space, specs):
        ptr[space] = PH_BASE
        d_ = {}
        for (name, shape, dt) in specs:
            if isinstance(name, tuple):
                d_[name[0]] = {k_: _alloc(space, "%s_%s" % (name[0], k_), shape, dt) for k_ in name[1]}
            else:
                d_[name] = _alloc(space, name, shape, dt)
        return d_

    dbl = lambda n: (n, ("0", "1"))
    T0 = phase_tiles("p0", [
        ("A_l", [128, D], F32), ("sh_l", [128, D], F32), ("A_c", [128, D], F32), ("sh_c", [128, D], F32),
        ("tmpA", [128, D], F32), ("stg0", [128, 1024], F32), ("stg1", [128, 1024], F32),
        ("modsb", [2, 3 * D], F32), ("bmod2", [2, 3 * D], F32),
        (dbl("xin"), [128, D], F32), (dbl("sq"), [128, D], F32), (dbl("tmpB"), [128, D], F32), (dbl("hb"), [128, D], BF16),
        (dbl("hT"), [128, 8, 128], BF16), (dbl("st"), [128, 8], F32)])
    TA = phase_tiles("pA", [
        ("stg0", [128, 1024], F32), ("stg1", [128, 1024], F32), (dbl("hTw"), [128, 8, 258], BF16),
        (dbl("xbcT"), [128, 12, 256], BF16), (dbl("cv"), [128, 258], F32), (dbl("cv2"), [128, 258], F32),
        (dbl("cvc"), [128, 4, 256], F32),
        (dbl("ropeT"), [128, 2, 256], F32), (dbl("ktile"), [128, 256], BF16), (dbl("xs"), [128, 1024], BF16),
        (dbl("btm"), [128, 256], BF16),
        (dbl("dtda"), [128, 64], F32), (dbl("dtr"), [128, 32], F32), ("ExS", [128, 64], F32), ("wsm", [128, 32], F32),
        (dbl("vtm"), [128, 128], BF16), ("rhsb", [128, 1024], BF16), ("S", [128, 1024], F32), ("S16", [128, 1024], BF16)])
    dbl = lambda n: (n, ("0", "1"))
    TC = phase_tiles("pC", [
        ("stg0", [128, 1024], F32), ("stg1", [128, 1024], F32),
        (dbl("xs"), [128, 1024], BF16), (dbl("btm"), [128, 256], BF16), (dbl("dtda"), [128, 64], F32),
        (dbl("hT"), [128, 8, 128], BF16), (dbl("btct"), [128, 4, 128], BF16), (dbl("sbin"), [128, 1024], BF16),
        (dbl("KTb"), [128, 5, 128], BF16),
        (dbl("Vb"), [128, 5, 128], BF16), (dbl("sz"), [128, 1024], F32), (dbl("Ex"), [128, 64], F32),
        (dbl("M_f"), [128, 16, 128], BF16), (dbl("M_b"), [128, 16, 128], BF16), (dbl("xdt_f"), [128, 1024], BF16),
        (dbl("xdt_b"), [128, 1024], BF16), (dbl("QT"), [128, 4, 128], BF16), (dbl("szA"), [64, 2, 4, 128], F32),
        ("OG", [64, 2, 4, 128], BF16), ("rden", [64, 4, 128], F32),
        ("hilo", [128, 64], BF16), ("dres", [128, 32], F32), ("wsm", [128, 32], F32), ("ExS", [128, 64], F32)])
    ptr["pCb"] = SB_BASE + 8 * NCC * 2
    lim["pCb"] = SB_BASE + 65536
    _pcb_base = ptr["pCb"]
    TCb = {}
    for (name, shape, dt) in [
            ("rb0", [128, 16, 128], BF16), ("rb1", [128, 16, 128], BF16), ("Lm0", [128, 4, 128], F32), ("Lm1", [128, 4, 128], F32),
            ("tmpA", [128, D], F32), ("tmpB", [128, D], F32), ("cbm", [128, 2, 2, 128], F32),
            ("ysb", [128, 1024], BF16), ("ysT", [128, 8, 128], BF16), ("rhsb", [128, 1024], BF16), ("S", [128, 1024], F32),
            ("S16", [128, 1024], BF16), ("PT0", [128, 512], BF16), ("PT1", [128, 512], BF16)]:
        TCb[name] = _alloc("pCb", name, shape, dt)
    TC.update(TCb)
    NBUF_C = 2
    for (name, shape, dt, sp_) in [
            ("tmpA", [128, D], F32, "pCb"), ("tmpB", [128, D], F32, "pCb"), ("ysb", [128, 1024], BF16, "pC"),
            ("ysT", [128, 8, 128], BF16, "pC"), ("PT0", [128, 512], BF16, "pC"), ("PT1", [128, 512], BF16, "pC"),
            ("rhsb", [128, 1024], BF16, "pC")]:
        TC[name] = {"0": TC[name], "1": _alloc(sp_, name + "_b", shape, dt)}
    for (name, shape, dt) in [("OG", [64, 2, 4, 128], BF16), ("rden", [64, 4, 128], F32), ("wsm", [128, 32], F32),
                              ("ExS", [128, 64], F32), ("stB", [128, 8], F32)]:
        first = TC[name] if name in TC else _alloc("pC", name + "_a", shape, dt)
        TC[name] = {"0": first, "1": _alloc("pC", name + "_b", shape, dt)}
    TQ = phase_tiles("pQ", [
        ("stg0", [128, 1024], F32), ("stg1", [128, 1024], F32), ("sct", [128, 512], F32), ("qtmp", [128, 1024], F32),
        ("szq", [128, 1024], F32),
        (dbl("hTq"), [128, 8, 512], BF16), (dbl("cvq"), [128, 4, 512], F32), (dbl("ropq"), [128, 2, 512], F32),
        (dbl("scy"), [128, 4, 512], BF16), (dbl("QTq"), [128, 4, 512], BF16), (dbl("zaq"), [64, 4, 512], F32)])
    T3 = phase_tiles("p3", [
        ("stg0", [128, 1024], F32), ("stg1", [128, 1024], F32),
        (dbl("yc"), [128, 12, 128], BF16), (dbl("og"), [128, 4, 128], BF16), (dbl("xin"), [128, D], F32),
        (dbl("xnew"), [128, D], F32), (dbl("tmpB"), [128, D], F32), (dbl("st"), [128, 8], F32)])
    print("SBUF bytes: persist %d  p0 %d  pA %d  pC %d pCb %d/%d p3 %d pQ %d (end %d)" % (PH_BASE, ptr["p0"], ptr["pA"], ptr["pC"], ptr["pCb"], lim["pCb"], ptr["p3"], ptr["pQ"], SB_END))

    def ps(name, shape, dt=F32):
        return Tl(nc.alloc_psum_tensor(name, list(shape), dt), name)

    psT = ps("psT", [128, 1024], BF16)
    banks = [ps("bk%d" % i, [128, 512]) for i in range(7)]
    bpos = [0]

    bsel = [None]
    bsub = {}

    def pb():
        if bsel[0] is None:
            b = banks[bpos[0] % len(banks)]
            bpos[0] += 1
            return b
        lo, hi = bsel[0]
        c = bsub.get(bsel[0], 0)
        bsub[bsel[0]] = c + 1
        return banks[lo + c % (hi - lo)]

    def mm(out_ap, lhsT, rhs, start, stop, reads, writes):
        S.op("pe", lambda e: e.matmul(out_ap, lhsT=lhsT, rhs=rhs, start=start, stop=stop), reads, writes)

    def tr(out_ap, in_ap, reads, writes):
        S.op("pe", lambda e: e.transpose(out_ap, in_ap, ident_b[:]), list(reads) + [ident_b], writes)

    def bc_row(dst_tile, dst_ap, src_row_ap, n=128):
        S.dma("sp", dst_ap, src_row_ap.partition_broadcast(n), writes=[dst_tile])

    cast_rr = [0]
    W_A, W_Q, W_C, W_3 = (Tl(W.t, "W_A"), Tl(W.t, "W_Q"), Tl(W.t, "W_C"), Tl(W.t, "W_3"))
    QOFF = 8 * NA
    W3OFF = 32768

    def weight_pieces(TT, key, dst_off, src, ncols, kparts):
        pieces = []
        for k in range(kparts):
            c0 = 0
            while c0 < ncols:
                cw = min(1024, ncols - c0)

                def piece(k=k, c0=c0, cw=cw):
                    st = TT["stg%d" % (cast_rr[0] % 2)]
                    S.dma("sp", st[:, 0:cw], src[k * 128:(k + 1) * 128, c0:c0 + cw], writes=[st])
                    o = dst_off + k * ncols + c0
                    eng = ("act", "dve", "pool")[cast_rr[0] % 3]
                    if eng == "act":
                        S.op("act", lambda e: e.copy(out=W[:, o:o + cw], in_=st[:, 0:cw]), [st], [key])
                    else:
                        S.op(eng, lambda e: e.tensor_copy(out=W[:, o:o + cw], in_=st[:, 0:cw]), [st], [key])
                    cast_rr[0] += 1
                pieces.append(piece)
                c0 += cw
        return pieces

    def pieces_A(l, TT):
        return weight_pieces(TT, W_A, 0, wA_in[l], NA, 8)

    def pieces_Q(l, TT):
        return weight_pieces(TT, W_Q, QOFF, wQ_in[l], NQ, 8)

    def pieces_C(l, TT):
        return weight_pieces(TT, W_C, 0, wC_in[l], NCC, 8)

    def pieces_3(l, TT):
        ps_ = []
        for t in range(16):
            ps_ += weight_pieces(TT, W_3, W3OFF + t * 1024, wom_in[l, t * 128:(t + 1) * 128, :], 1024, 1)
        return ps_

    pending = {}

    def take(pieces, n):
        for _ in range(min(n, len(pieces))):
            pieces.pop(0)()

    def build_consts():
        def sel(t, pattern, cm, op):
            S.op("pool", lambda e: e.memset(t[:], 1.0), [], [t])
            S.op("pool", lambda e: e.affine_select(out=t[:], in_=t[:], pattern=pattern, compare_op=op,
                                                   fill=0.0, base=0, channel_multiplier=cm), [t], [t])
        sel(cm_f["UI"], [[1, 128]], -1, ALU.is_ge)
        sel(cm_f["LI"], [[-1, 128]], 1, ALU.is_ge)
        sel(cm_f["SU"], [[-1, 128]], 1, ALU.is_gt)
        sel(cm_f["SL"], [[1, 128]], -1, ALU.is_gt)
        S.op("pool", lambda e: e.memset(cm_f["ones"][:], 1.0), [], [cm_f["ones"]])
        for n in cm_f:
            S.op("dve", lambda e, n=n: e.tensor_copy(out=cm_b[n][:], in_=cm_f[n][:]), [cm_f[n]], [cm_b[n]])
        S.op("pool", lambda e: e.memset(ident_b[:], 1.0), [], [ident_b])
        S.op("pool", lambda e: e.affine_select(out=ident_b[:], in_=ident_b[:], pattern=[[-1, 128]],
                                               compare_op=ALU.is_equal, fill=0.0, base=0, channel_multiplier=1),
             [ident_b], [ident_b])
        for n in ("UI", "LI"):
            S.op("dve", lambda e, n=n: e.tensor_scalar(
                out=negm[n][:], in0=cm_f[n][:].unsqueeze(1).to_broadcast([128, 4, 128]), scalar1=-1.0, scalar2=2.4e5,
                op0=ALU.add, op1=ALU.mult), [cm_f[n]], [negm[n]])
        S.dma("sp", cc[:], cc_in, writes=[cc])
        S.op("act", lambda e: e.activation(out=cc[:], in_=cc[:], func=AF.Silu), [cc], [cc])

    def layer_setup(l):
        modsb, bmod2, tmpA = T0["modsb"], T0["bmod2"], T0["tmpA"]
        accs = [pb() for _ in range(6)]
        i = 0
        for kc in range(8):
            for third in range(3):
                st = T0["stg%d" % (i % 2)]
                i += 1
                S.dma("sp", st[:, 0:1024], wmod_in[l, kc * 128:(kc + 1) * 128, third * 1024:(third + 1) * 1024],
                      writes=[st])
                for j in range(2):
                    a = accs[third * 2 + j]
                    mm(a[0:2, 0:512], cc[:, kc, :], st[:, j * 512:(j + 1) * 512], kc == 0, kc == 7, [cc, st], [a])
        S.dma("sp", bmod2[:], bmod_in[l].partition_broadcast(2), writes=[bmod2])
        for j in range(6):
            S.op("dve", lambda e, j=j: e.tensor_tensor(out=modsb[:, j * 512:(j + 1) * 512], in0=accs[j][0:2, 0:512],
                                                        in1=bmod2[:, j * 512:(j + 1) * 512], op=ALU.add),
                 [accs[j], bmod2], [modsb])
        S.dma("sp", mod_d, modsb[:], reads=[modsb], writes=["mod_d"])
        for (row, sh_t, A_t, g_t) in ((0, T0["sh_l"], T0["A_l"], g_l), (1, T0["sh_c"], T0["A_c"], aux)):
            S.dma("sp", sh_t[:], mod_d[row:row + 1, 0:D].partition_broadcast(128), reads=["mod_d"], writes=[sh_t])
            S.dma("sp", A_t[:], mod_d[row:row + 1, D:2 * D].partition_broadcast(128), reads=["mod_d"], writes=[A_t])
            if row == 0 or l < DEPTH - 1:
                S.dma("sp", g_t[:], mod_d[row:row + 1, 2 * D:3 * D].partition_broadcast(128), reads=["mod_d"], writes=[g_t])
        if l == DEPTH - 1:
            bc_row(aux, aux[:], fnw_in)
        bc_row(tmpA, tmpA[:], normw_in[l])
        for A_t in (T0["A_l"], T0["A_c"]):
            S.op("dve", lambda e, A_t=A_t: e.scalar_tensor_tensor(out=A_t[:], in0=A_t[:], scalar=1.0, in1=tmpA[:],
                                                                    op0=ALU.add, op1=ALU.mult), [A_t, tmpA], [A_t])
        bc_row(snw_b, snw_b[:], snw_in[l])
        bc_row(small, small[:, 0:32], dtb_in[l])
        bc_row(small, small[:, 32:64], alog_in[l])
        bc_row(small, small[:, 64:80], dsk_in[l])
        bc_row(small, small[:, 80:88], sink_in[l])
        S.op("act", lambda e: e.activation(out=small[:, 32:64], in_=small[:, 32:64], func=AF.Exp), [small], [small])
        S.op("dve", lambda e: e.tensor_scalar_mul(out=small[:, 32:64], in0=small[:, 32:64], scalar1=-1.0), [small], [small])
        S.op("act", lambda e: e.activation(out=small[:, 80:88], in_=small[:, 80:88], func=AF.Exp), [small], [small])
        S.dma("sp", convw[:], convw_in[l], writes=[convw])
        S.dma("sp", convb[:], convb_in[l], writes=[convb])
        S.dma("sp", scw[:], scw_in[l], writes=[scw])

    def rms_rstd(src_tile, scratch, width, st8=st8):
        S.op("act", lambda e: e.activation(out=scratch[:, 0:width], in_=src_tile[:, 0:width], func=AF.Square),
             [src_tile], [scratch])
        S.op("dve", lambda e: e.reduce_sum(out=st8[:, 0:1], in_=scratch[:, 0:width], axis=AX.X), [scratch], [st8])
        S.op("dve", lambda e: e.tensor_scalar(out=st8[:, 0:1], in0=st8[:, 0:1], scalar1=1.0 / width, scalar2=EPS,
                                               op0=ALU.mult, op1=ALU.add), [st8], [st8])
        S.op("act", lambda e: e.activation(out=st8[:, 0:1], in_=st8[:, 0:1], func=AF.Sqrt), [st8], [st8])
        S.op("dve", lambda e: e.reciprocal(out=st8[:, 0:1], in_=st8[:, 0:1]), [st8], [st8])

    def pass0(l):
        S.begin("p0")
        for gc in range(NCH):
            S.itn = gc
            p = str(gc % 2)
            xin, sq, tmpB, hb, hT, stt = (T0[n][p] for n in ("xin", "sq", "tmpB", "hb", "hT", "st"))
            if gc < 2:
                src = (ctx_in if l == 0 else ctx1_d)[gc * 128:(gc + 1) * 128, :]
                A_t, sh_t = T0["A_c"], T0["sh_c"]
            else:
                src = (x_in if l == 0 else x1_d)[(gc - 2) * 128:(gc - 1) * 128, :]
                A_t, sh_t = T0["A_l"], T0["sh_l"]
            S.dma("sp", xin[:], src, reads=["x1_d", "ctx1_d"], writes=[xin])
            rms_rstd(xin, sq, D, st8=stt)
            S.op("dve", lambda e, A_t=A_t, xin=xin, tmpB=tmpB, stt=stt: e.scalar_tensor_tensor(
                out=tmpB[:], in0=xin[:], scalar=stt[:, 0:1], in1=A_t[:], op0=ALU.mult, op1=ALU.mult),
                [xin, stt, A_t], [tmpB])
            S.op("pool", lambda e, sh_t=sh_t, hb=hb, tmpB=tmpB: e.tensor_tensor(out=hb[:], in0=tmpB[:], in1=sh_t[:], op=ALU.add),
                 [tmpB, sh_t], [hb])
            for kc in range(8):
                tr(psT[:, kc * 128:(kc + 1) * 128], hb[:, kc * 128:(kc + 1) * 128], [hb], [psT])
            S.op("act", lambda e, hT=hT: e.copy(out=hT[:].rearrange("p k t -> p (k t)"), in_=psT[:]), [psT], [hT])
            S.dma("sp", hT_d[gc], hT[:], reads=[hT], writes=["hT_d"])
        S.end()
        S.emit_scheduled("p0", overlap=OVL_C)

    def state_update(TT, pmat_key, da_cols, dt_cols, gc, store_d=None):
        Sd, Sd16, xs, btm, dtda, Ex, wsm, rhsb = (TT[n] for n in ("S", "S16", "xs", "btm", "dtda", "ExS", "wsm", "rhsb"))
        p = pb()
        mm(p[:, 0:16], cm_f[pmat_key][:], dtda[:, da_cols[0]:da_cols[1]], True, True, [cm_f[pmat_key], dtda], [p])
        mm(p[:, 16:32], cm_f["ones"][:], dtda[:, da_cols[0]:da_cols[1]], True, True, [cm_f["ones"], dtda], [p])
        S.op("act", lambda e: e.activation(out=Ex[:, 0:32], in_=p[:, 0:32], func=AF.Exp), [p], [Ex])
        S.op("dve", lambda e: e.tensor_tensor(out=wsm[:, 0:16], in0=dtda[:, dt_cols[0]:dt_cols[1]], in1=Ex[:, 0:16],
                                               op=ALU.mult), [dtda, Ex], [wsm])
        S.op("dve", lambda e: e.tensor_tensor(out=rhsb[:].rearrange("p (h q) -> p h q", h=16),
                                               in0=xs[:].rearrange("p (h q) -> p h q", h=16),
                                               in1=wsm[:, 0:16].unsqueeze(2).to_broadcast([128, 16, 64]), op=ALU.mult),
             [xs, wsm], [rhsb])
        if store_d is not None:
            S.dma("sp", store_d[gc], Sd16[:], reads=[Sd16], writes=["sb_d"])
        pa, pb2 = pb(), pb()
        mm(pa[:, 0:512], btm[:, 0:128], rhsb[:, 0:512], True, True, [btm, rhsb], [pa])
        mm(pb2[:, 0:512], btm[:, 128:256], rhsb[:, 512:1024], True, True, [btm, rhsb], [pb2])
        S.op("pool", lambda e: e.tensor_tensor(out=Sd[:].rearrange("p (h q) -> p h q", h=16),
                                                in0=Sd[:].rearrange("p (h q) -> p h q", h=16),
                                                in1=Ex[:, 16:32].unsqueeze(2).to_broadcast([128, 16, 64]), op=ALU.mult),
             [Sd, Ex], [Sd])
        S.op("dve", lambda e: e.tensor_tensor(out=Sd[:, 0:512], in0=Sd[:, 0:512], in1=pa[:, 0:512], op=ALU.add),
             [Sd, pa], [Sd])
        S.op("dve", lambda e: e.tensor_tensor(out=Sd[:, 512:1024], in0=Sd[:, 512:1024], in1=pb2[:, 0:512], op=ALU.add),
             [Sd, pb2], [Sd])
        S.op("act", lambda e: e.copy(out=Sd16[:], in_=Sd[:]), [Sd], [Sd16])

    WA = lambda kc, c0, c1: W[:, kc * NA + c0: kc * NA + c1]

    def passA(l):
        if not pending.pop(("A", l), False):
            take(pieces_A(l, TA), 999)
        pf = pieces_Q(l, TA)
        pending[("Q", l)] = True
        S.op("pool", lambda e: e.memset(TA["S"][:], 0.0), [], [TA["S"]])
        S.op("pool", lambda e: e.memset(TA["S16"][:], 0.0), [], [TA["S16"]])
        order = [0] + [2 + 2 * j for j in range(15, -1, -1)]
        S.begin("pA")
        for oi, gc0 in enumerate(order):
            S.itn = oi
            take(pf, 2)
            passA_super(oi, gc0)
        take(pf, 999)
        S.end()
        S.emit_scheduled("pA", overlap=OVL_C)

    def passA_super(oi, gc0):
        if True:
            sp_ = str(oi % 2)
            (hTw, xbcT, cv, cv2, cvc, ropeT, ktile) = (TA[n][sp_] for n in ("hTw", "xbcT", "cv", "cv2", "cvc", "ropeT", "ktile"))
            is_ctx = gc0 < 2
            for c2 in range(2):
                S.dma("sp", hTw[:, :, 1 + c2 * 128: 1 + (c2 + 1) * 128], hT_d[gc0 + c2], reads=["hT_d"], writes=[hTw])
            if is_ctx or gc0 == 2:
                S.op("pool", lambda e: e.memset(hTw[:, :, 0:1], 0.0), [], [hTw])
            else:
                S.dma("sp", hTw[:, :, 0:1], hT_d[gc0 - 1, :, :, 127:128], reads=["hT_d"], writes=[hTw],
                      allow_slow_non_contiguous=True)
            if is_ctx or gc0 == NCH - 2:
                S.op("pool", lambda e: e.memset(hTw[:, :, 257:258], 0.0), [], [hTw])
            else:
                S.dma("sp", hTw[:, :, 257:258], hT_d[gc0 + 2, :, :, 0:1], reads=["hT_d"], writes=[hTw],
                      allow_slow_non_contiguous=True)
            for m in range(12):
                p = pb()
                for kc in range(8):
                    mm(p[:, 0:258], WA(kc, A_XBC + m * 128, A_XBC + (m + 1) * 128), hTw[:, kc, :], kc == 0, kc == 7,
                       [W_A, hTw], [p])
                S.op("dve", lambda e, p=p, m=m: e.tensor_scalar_mul(out=cv[:, 0:256], in0=p[:, 0:256],
                                                                     scalar1=convw[:, m, 0:1]), [p, convw], [cv])
                S.op("dve", lambda e, p=p, m=m: e.scalar_tensor_tensor(out=cv[:, 0:256], in0=p[:, 1:257],
                                                                        scalar=convw[:, m, 1:2], in1=cv[:, 0:256],
                                                                        op0=ALU.mult, op1=ALU.add), [p, convw, cv], [cv])
                S.op("dve", lambda e, p=p, m=m: e.scalar_tensor_tensor(out=cv[:, 0:256], in0=p[:, 2:258],
                                                                        scalar=convw[:, m, 2:3], in1=cv[:, 0:256],
                                                                        op0=ALU.mult, op1=ALU.add), [p, convw, cv], [cv])
                S.op("act", lambda e, m=m: e.activation(out=xbcT[:, m, :], in_=cv[:, 0:256], func=AF.Silu,
                                                         bias=convb[:, m:m + 1], scale=1.0), [cv, convb], [xbcT])
            for m in range(4):
                pv, pc = pb(), pb()
                for kc in range(8):
                    mm(pv[:, 0:258], WA(kc, A_SCV + m * 128, A_SCV + (m + 1) * 128), hTw[:, kc, :], kc == 0, kc == 7,
                       [W_A, hTw], [pv])
                for kc in range(8):
                    mm(pc[:, 0:258], WA(kc, A_SCC + m * 128, A_SCC + (m + 1) * 128), hTw[:, kc, :], kc == 0, kc == 7,
                       [W_A, hTw], [pc])
                S.op("act", lambda e, pv=pv: e.copy(out=cv2[:], in_=pv[:, 0:258]), [pv], [cv2])
                S.op("dve", lambda e, pc=pc: e.tensor_tensor(out=cv2[:], in0=pc[:, 0:258], in1=cv2[:], op=ALU.mult),
                     [pc, cv2], [cv2])
                S.op("dve", lambda e, m=m: e.tensor_scalar_mul(out=cvc[:, m, :], in0=cv2[:, 0:256],
                                                                 scalar1=scw[:, m, 0:1]), [cv2, scw], [cvc])
                S.op("dve", lambda e, m=m: e.scalar_tensor_tensor(out=cvc[:, m, :], in0=cv2[:, 1:257],
                                                                    scalar=scw[:, m, 1:2], in1=cvc[:, m, :],
                                                                    op0=ALU.mult, op1=ALU.add), [cv2, scw, cvc], [cvc])
                S.op("dve", lambda e, m=m: e.scalar_tensor_tensor(out=cvc[:, m, :], in0=cv2[:, 2:258],
                                                                    scalar=scw[:, m, 2:3], in1=cvc[:, m, :],
                                                                    op0=ALU.mult, op1=ALU.add), [cv2, scw, cvc], [cvc])
            for c2 in range(2):
                S.dma("sp", cvc_d[gc0 + c2], cvc[:, :, c2 * 128:(c2 + 1) * 128], reads=[cvc], writes=["cvc_d"])
            pk, pks = pb(), pb()
            for kc in range(8):
                mm(pk[:, 0:258], WA(kc, A_K, A_K + 128), hTw[:, kc, :], kc == 0, kc == 7, [W_A, hTw], [pk])
            if is_ctx:
                S.op("act", lambda e: e.copy(out=ktile[:], in_=pk[:, 1:257]), [pk], [ktile])
            else:
                for kc in range(8):
                    mm(pks[:, 0:258], WA(kc, A_KSW, A_KSW + 128), hTw[:, kc, :], kc == 0, kc == 7, [W_A, hTw], [pks])
                t0 = (gc0 - 2) * 128
                S.dma("sp", ropeT[:], rope_in[:, :, t0:t0 + 256], writes=[ropeT])
                S.op("dve", lambda e: e.tensor_tensor(out=cv[:, 0:256], in0=pk[:, 1:257], in1=ropeT[:, 0, :], op=ALU.mult),
                     [pk, ropeT], [cv])
                S.op("dve", lambda e: e.tensor_tensor(out=cv2[:, 0:256], in0=pks[:, 1:257], in1=ropeT[:, 1, :], op=ALU.mult),
                     [pks, ropeT], [cv2])
                S.op("pool", lambda e: e.tensor_tensor(out=ktile[:], in0=cv[:, 0:256], in1=cv2[:, 0:256], op=ALU.add),
                     [cv, cv2], [ktile])
            S.dma("sp", kt_d[:, gc0 * 128:(gc0 + 2) * 128], ktile[:], reads=[ktile], writes=["kt_d"])
            for c2 in (1, 0):
                passA_chunk(gc0, c2, hTw, xbcT)

    def passA_chunk(gc0, c2, hTw, xbcT):
        if True:
            if True:
                gc = gc0 + c2
                cp_ = str(gc % 2)
                xs, btm, dtda, dtr, vtm = (TA[n][cp_] for n in ("xs", "btm", "dtda", "dtr", "vtm"))
                TS = dict(TA)
                TS.update({"xs": xs, "btm": btm, "dtda": dtda})
                tsl = slice(1 + c2 * 128, 1 + (c2 + 1) * 128)
                csl = slice(c2 * 128, (c2 + 1) * 128)
                p = pb()
                for kc in range(8):
                    mm(p[:, 0:128], hTw[:, kc, tsl], WA(kc, A_V, A_V + 128), kc == 0, kc == 7, [W_A, hTw], [p])
                S.op("act", lambda e, p=p: e.copy(out=vtm[:], in_=p[:, 0:128]), [p], [vtm])
                S.dma("sp", v_d[gc], vtm[:], reads=[vtm], writes=["v_d"])
                p = pb()
                for kc in range(8):
                    mm(p[:, 0:32], hTw[:, kc, tsl], WA(kc, A_DT, A_DT + 32), kc == 0, kc == 7, [W_A, hTw], [p])
                S.op("dve", lambda e, p=p: e.tensor_tensor(out=dtr[:], in0=p[:, 0:32], in1=small[:, 0:32], op=ALU.add),
                     [p, small], [dtr])
                S.op("act", lambda e: e.activation(out=dtr[:], in_=dtr[:], func=AF.Exp), [dtr], [dtr])
                S.op("act", lambda e: e.activation(out=dtda[:, 0:32], in_=dtr[:], func=AF.Ln, bias=1.0, scale=1.0),
                     [dtr], [dtda])
                S.op("dve", lambda e: e.tensor_tensor(out=dtda[:, 32:64], in0=dtda[:, 0:32], in1=small[:, 32:64],
                                                       op=ALU.mult), [dtda, small], [dtda])
                S.dma("sp", dtda_d[gc], dtda[:], reads=[dtda], writes=["dtda_d"])
                for m in range(8):
                    tr(psT[:, m * 128:(m + 1) * 128], xbcT[:, m, csl], [xbcT], [psT])
                S.op("act", lambda e: e.copy(out=xs[:], in_=psT[:]), [psT], [xs])
                S.dma("sp", xs_d[gc], xs[:], reads=[xs], writes=["xs_d"])
                for m in range(2):
                    tr(psT[:, m * 128:(m + 1) * 128], xbcT[:, 8 + m, csl], [xbcT], [psT])
                S.op("dve", lambda e: e.tensor_copy(out=btm[:], in_=psT[:, 0:256]), [psT], [btm])
                S.dma("sp", btm_d[gc], btm[:], reads=[btm], writes=["btm_d"])
                S.dma("sp", bt_d[gc], xbcT[:, 8:10, csl], reads=[xbcT], writes=["bt_d"])
                S.dma("sp", ct_d[gc], xbcT[:, 10:12, csl], reads=[xbcT], writes=["ct_d"])
                state_update(TS, "SL", (48, 64), (16, 32), gc, store_d=sb_d)

    WQ = lambda kc, c0, c1: W[:, QOFF + kc * NQ + c0: QOFF + kc * NQ + c1]

    def passQ(l):
        if not pending.pop(("Q", l), False):
            take(pieces_Q(l, TQ), 999)
        pf = pieces_C(l, TQ)
        sct, qtmp = TQ["sct"], TQ["qtmp"]
        quads = [(0, 2)] + [(2 + 4 * k, 4) for k in range(8)]
        S.begin("pQ")
        for qi, (gc0, nch) in enumerate(quads):
            S.itn = qi
            take(pf, 999)
            passQ_quad(qi, gc0, nch, sct, qtmp)
        take(pf, 999)
        S.end()
        S.emit_scheduled("pQ", overlap=OVL_C)

    def passQ_quad(qi, gc0, nch, sct, qtmp):
        p_ = str(qi % 2)
        hTq, cvq, ropq, scy, QTq = (TQ[n][p_] for n in ("hTq", "cvq", "ropq", "scy", "QTq"))
        N = nch * 128
        is_ctx = gc0 < 2
        for ci in range(nch):
            S.dma("sp", hTq[:, :, ci * 128:(ci + 1) * 128], hT_d[gc0 + ci], reads=["hT_d"], writes=[hTq])
            S.dma("sp", cvq[:, :, ci * 128:(ci + 1) * 128], cvc_d[gc0 + ci], reads=["cvc_d"], writes=[cvq])
        if not is_ctx:
            t0 = (gc0 - 2) * 128
            S.dma("sp", ropq[:, :, 0:N], rope_in[:, :, t0:t0 + N], writes=[ropq])
        for m in range(4):
            pB, pZ = pb(), pb()
            for kc in range(8):
                mm(pB[:, 0:N], WQ(kc, Q_SCB + m * 128, Q_SCB + (m + 1) * 128), hTq[:, kc, 0:N], kc == 0, kc == 7, [W_Q, hTq], [pB])
            for kc in range(8):
                mm(pZ[:, 0:N], WQ(kc, Q_SCZ + m * 128, Q_SCZ + (m + 1) * 128), hTq[:, kc, 0:N], kc == 0, kc == 7, [W_Q, hTq], [pZ])
            S.op("act", lambda e, pZ=pZ: e.activation(out=sct[:, 0:N], in_=pZ[:, 0:N], func=AF.Silu), [pZ], [sct])
            S.op("dve", lambda e, pB=pB, m=m: e.tensor_tensor(out=cvq[:, m, 0:N], in0=pB[:, 0:N], in1=cvq[:, m, 0:N], op=ALU.mult),
                 [pB, cvq], [cvq])
            S.op("pool", lambda e, m=m: e.tensor_tensor(out=scy[:, m, 0:N], in0=cvq[:, m, 0:N], in1=sct[:, 0:N], op=ALU.mult),
                 [cvq, sct], [scy])
        for ci in range(nch):
            S.dma("sp", yc_d[gc0 + ci, :, 8:12, :], scy[:, :, ci * 128:(ci + 1) * 128], reads=[scy], writes=["yc_d"])
        for j in range(4):
            pq = pb()
            for kc in range(8):
                mm(pq[:, 0:N], WQ(kc, Q_Q + j * 128, Q_Q + (j + 1) * 128), hTq[:, kc, 0:N], kc == 0, kc == 7, [W_Q, hTq], [pq])
            if is_ctx:
                S.op("act", lambda e, pq=pq, j=j: e.copy(out=QTq[:, j, 0:N], in_=pq[:, 0:N]), [pq], [QTq])
            else:
                pqs = pb()
                for kc in range(8):
                    mm(pqs[:, 0:N], WQ(kc, Q_QSW + j * 128, Q_QSW + (j + 1) * 128), hTq[:, kc, 0:N], kc == 0, kc == 7,
                       [W_Q, hTq], [pqs])
                S.op("dve", lambda e, pq=pq: e.tensor_tensor(out=qtmp[:, 0:N], in0=pq[:, 0:N], in1=ropq[:, 0, 0:N], op=ALU.mult),
                     [pq, ropq], [qtmp])
                S.op("dve", lambda e, pqs=pqs: e.tensor_tensor(out=qtmp[:, 512:512 + N], in0=pqs[:, 0:N], in1=ropq[:, 1, 0:N],
                                                               op=ALU.mult), [pqs, ropq], [qtmp])
                S.op("pool", lambda e, j=j: e.tensor_tensor(out=QTq[:, j, 0:N], in0=qtmp[:, 0:N], in1=qtmp[:, 512:512 + N],
                                                            op=ALU.add), [qtmp], [QTq])
        for ci in range(nch):
            S.dma("sp", qt_d[gc0 + ci], QTq[:, :, ci * 128:(ci + 1) * 128], reads=[QTq], writes=["qt_d"])
        for g in range(2):
            zaq = TQ["zaq"][str(g)]
            for j in range(4):
                hq = g * 4 + j
                pza = pb()
                for kc in range(8):
                    mm(pza[0:64, 0:N], WQ(kc, Q_ZA + hq * 64, Q_ZA + (hq + 1) * 64), hTq[:, kc, 0:N], kc == 0, kc == 7,
                       [W_Q, hTq], [pza])
                S.op("act", lambda e, pza=pza, j=j, zaq=zaq: e.activation(out=zaq[:, j, 0:N], in_=pza[0:64, 0:N], func=AF.Silu),
                     [pza], [zaq])
            for ci in range(nch):
                S.dma("sp", za_d[gc0 + ci, :, g * 4:(g + 1) * 4, :], zaq[:, :, ci * 128:(ci + 1) * 128], reads=[zaq],
                      writes=["za_d"])
        passQ_z(gc0, nch, hTq)

    def passQ_z(gc0, nch, hTq):
        szq = TQ["szq"]
        for ci in range(nch):
            for h2 in range(2):
                pz = pb()
                for kc in range(8):
                    mm(pz[:, 0:512], hTq[:, kc, ci * 128:(ci + 1) * 128], WC(kc, C_Z + h2 * 512, C_Z + (h2 + 1) * 512),
                       kc == 0, kc == 7, [W_C, hTq], [pz])
                S.op("act", lambda e, pz=pz, h2=h2: e.activation(out=szq[:, h2 * 512:(h2 + 1) * 512], in_=pz[:, 0:512],
                                                                  func=AF.Silu), [pz], [szq])
            S.dma("sp", sz_d[gc0 + ci], szq[:], reads=[szq], writes=["sz_d"])

    WC = lambda kc, c0, c1: W[:, kc * NCC + c0: kc * NCC + c1]
    WO0 = W3OFF
    WOA = WO0 + 12 * 1024

    def passC(l):
        last = (l == DEPTH - 1)
        rb = [TC["rb0"], TC["rb1"]]
        Lm = [TC["Lm0"], TC["Lm1"]]
        (cbm, hilo, dres, S16f) = (TC[n] for n in ("cbm", "hilo", "dres", "S16"))
        pf = pieces_3(l, TC)
        pending[("3", l)] = True
        S.op("pool", lambda e: e.memset(TC["S"][:], 0.0), [], [TC["S"]])
        S.op("pool", lambda e: e.memset(TC["S16"][:], 0.0), [], [TC["S16"]])

        def tiles(gc):
            p = str(gc % NBUF_C)
            return {n: TC[n][p] for n in ("xs", "btm", "dtda", "hT", "btct", "sbin", "KTb", "Vb", "sz", "Ex",
                                          "M_f", "M_b", "xdt_f", "xdt_b", "QT", "szA")}

        def kblocks(gc):
            if gc < 2:
                return [(0, None), (1, None)]
            n = gc - 2
            kbs = []
            if n > 0:
                kbs.append((gc - 1, "LI"))
            kbs.append((gc, None))
            if n < 31:
                kbs.append((gc + 1, "UI"))
            return kbs + [(0, None), (1, None)]

        def stageF(gc):
            t = tiles(gc)
            xs, btm, dtda, hT, btct, sbin, KTb, Vb, sz, Ex, QT, szA = (t[n] for n in (
                "xs", "btm", "dtda", "hT", "btct", "sbin", "KTb", "Vb", "sz", "Ex", "QT", "szA"))
            Mm = {"f": t["M_f"], "b": t["M_b"]}
            xdt = {"f": t["xdt_f"], "b": t["xdt_b"]}
            is_ctx = gc < 2
            S.dma("sp", xs[:], xs_d[gc], reads=["xs_d"], writes=[xs])
            S.dma("sp", btm[:], btm_d[gc], reads=["btm_d"], writes=[btm])
            S.dma("sp", dtda[:], dtda_d[gc], reads=["dtda_d"], writes=[dtda])
            if is_ctx and last:
                return
            S.dma("sp", sz[:], sz_d[gc], reads=["sz_d"], writes=[sz])
            S.dma("sp", btct[:, 0:2, :], bt_d[gc], reads=["bt_d"], writes=[btct])
            S.dma("sp", btct[:, 2:4, :], ct_d[gc], reads=["ct_d"], writes=[btct])
            S.dma("sp", sbin[:], sb_d[gc], reads=["sb_d"], writes=[sbin])
            S.dma("sp", QT[:], qt_d[gc], reads=["qt_d"], writes=[QT])
            S.dma("sp", szA[:].rearrange("p g j t -> p (g j) t"), za_d[gc], reads=["za_d"], writes=[szA])
            kbs = kblocks(gc)
            for i, (kg, _) in enumerate(kbs):
                S.dma("sp", KTb[:, i, :], kt_d[:, kg * 128:(kg + 1) * 128], reads=["kt_d"], writes=[KTb])
                S.dma("sp", Vb[:, i, :], v_d[kg], reads=["v_d"], writes=[Vb])
            S.op("dve", lambda e: e.tensor_copy(out=hilo[:, 0:32], in_=dtda[:, 32:64]), [dtda], [hilo])
            S.op("dve", lambda e: e.tensor_tensor(out=dres[:], in0=dtda[:, 32:64], in1=hilo[:, 0:32], op=ALU.subtract),
                 [dtda, hilo], [dres])
            S.op("dve", lambda e: e.tensor_copy(out=hilo[:, 32:64], in_=dres[:]), [dres], [hilo])
            p = pb()
            mm(p[:, 32:48], cm_f["UI"][:], dtda[:, 32:48], True, True, [cm_f["UI"], dtda], [p])
            mm(p[:, 48:64], cm_f["LI"][:], dtda[:, 48:64], True, True, [cm_f["LI"], dtda], [p])
            S.op("act", lambda e: e.activation(out=Ex[:, 32:64], in_=p[:, 32:64], func=AF.Exp), [p], [Ex])
            pcb = pb()
            for g in range(2):
                mm(pcb[:, g * 128:(g + 1) * 128], btct[:, g, :], btct[:, 2 + g, :], True, True, [btct], [pcb])
            for di, mk in enumerate(("UI", "LI")):
                S.op("dve", lambda e, di=di, mk=mk: e.tensor_tensor(
                    out=cbm[:, di, :, :], in0=pcb[:, 0:256].rearrange("p (g l) -> p g l", g=2),
                    in1=cm_f[mk][:].unsqueeze(1).to_broadcast([128, 2, 128]), op=ALU.mult), [pcb, cm_f[mk]], [cbm])
            li = 0
            for di, (dk, lk, mk) in enumerate((("f", "SU", "UI"), ("b", "SL", "LI"))):
                for part in range(2):
                    eng = "pool" if part == 0 else "dve"
                    S.op(eng, lambda e, part=part, di=di, mk=mk: e.tensor_tensor(
                        out=rb[part][:],
                        in0=hilo[:, part * 32 + di * 16: part * 32 + di * 16 + 16].unsqueeze(2).to_broadcast([128, 16, 128]),
                        in1=cm_b[mk][:].unsqueeze(1).to_broadcast([128, 16, 128]), op=ALU.mult),
                        [hilo, cm_b[mk]], [rb[part]])
                for q4 in range(4):
                    pD = pb()
                    for part in range(2):
                        mm(pD[:, 0:512], cm_b[lk][:], rb[part][:, q4 * 4:(q4 + 1) * 4, :].rearrange("p h l -> p (h l)"),
                           part == 0, part == 1, [cm_b[lk], rb[part]], [pD])
                    Lq = Lm[li % 2]
                    li += 1
                    S.op("act", lambda e, pD=pD, Lq=Lq: e.activation(out=Lq[:].rearrange("p h l -> p (h l)"),
                                                                      in_=pD[:, 0:512], func=AF.Exp), [pD], [Lq])
                    g = q4 // 2
                    eng = "pool"
                    S.op(eng, lambda e, dk=dk, di=di, q4=q4, g=g, Lq=Lq: e.tensor_tensor(
                        out=Mm[dk][:, q4 * 4:(q4 + 1) * 4, :], in0=Lq[:],
                        in1=cbm[:, di, g, :].unsqueeze(1).to_broadcast([128, 4, 128]), op=ALU.mult),
                        [Lq, cbm], [Mm[dk]])
                S.op("dve", lambda e, dk=dk, di=di: e.tensor_tensor(
                    out=xdt[dk][:].rearrange("p (h q) -> p h q", h=16), in0=xs[:].rearrange("p (h q) -> p h q", h=16),
                    in1=dtda[:, di * 16:(di + 1) * 16].unsqueeze(2).to_broadcast([128, 16, 64]), op=ALU.mult),
                    [xs, dtda], [xdt[dk]])

        def stageB(gc):
            t = tiles(gc)
            xs, btm, dtda, btct, sbin, KTb, Vb, sz, Ex, QT, szA = (t[n] for n in (
                "xs", "btm", "dtda", "btct", "sbin", "KTb", "Vb", "sz", "Ex", "QT", "szA"))
            Mm = {"f": t["M_f"], "b": t["M_b"]}
            xdt = {"f": t["xdt_f"], "b": t["xdt_b"]}
            bp = str(gc % 2)
            tmpA, tmpB, ysb, ysT, OG, rden, st8 = (TC[n][bp] for n in ("tmpA", "tmpB", "ysb", "ysT", "OG", "rden", "stB"))
            PT = [TC["PT0"][bp], TC["PT1"][bp]]
            TS = dict(TC)
            TS.update({"xs": xs, "btm": btm, "dtda": dtda, "rhsb": TC["rhsb"][bp], "wsm": TC["wsm"][bp], "ExS": TC["ExS"][bp]})
            is_ctx = gc < 2
            if is_ctx and last:
                state_update(TS, "SU", (32, 48), (0, 16), gc)
                return
            kbs = kblocks(gc)
            for g in range(2):
                gs = slice(g * 64, (g + 1) * 64)
                po, pd = pb(), pb()
                pscs = [pb(), pb()]
                nk = len(kbs)
                for i, (kg, mk) in enumerate(kbs):
                    psc = pscs[i % 2]
                    mm(psc[:, 0:512], KTb[gs, i, :], QT[gs, :, :].rearrange("p j t -> p (j t)"), True, mk is None, [KTb, QT], [psc])
                    if mk is not None:
                        mm(psc[:, 0:512], ident_b[:], negm[mk][:].rearrange("p j t -> p (j t)"), False, True,
                           [ident_b, negm[mk]], [psc])
                    pt = PT[i % 2]
                    S.op("act", lambda e, psc=psc, pt=pt: e.activation(out=pt[:], in_=psc[:, 0:512], func=AF.Exp, scale=0.125),
                         [psc], [pt])
                    mm(po[0:64, 0:512], Vb[:, i, gs], pt[:], i == 0, i == nk - 1, [Vb, pt], [po])
                    mm(pd[0:64, 0:512], cm_b["ones"][:, 0:64], pt[:], i == 0, i == nk - 1, [cm_b["ones"], pt], [pd])
                S.op("dve", lambda e, g=g, pd=pd: e.tensor_tensor(
                    out=rden[:], in0=pd[0:64, 0:512].rearrange("p (j t) -> p j t", j=4),
                    in1=small[0:64, 80 + g * 4: 84 + g * 4].unsqueeze(2).to_broadcast([64, 4, 128]), op=ALU.add),
                    [pd, small], [rden])
                S.op("dve", lambda e: e.reciprocal(out=rden[:], in_=rden[:]), [rden], [rden])
                S.op("dve", lambda e, po=po: e.tensor_tensor(out=rden[:].rearrange("p j t -> p (j t)"), in0=po[0:64, 0:512],
                                                              in1=rden[:].rearrange("p j t -> p (j t)"), op=ALU.mult),
                     [po, rden], [rden])
                S.op("pool", lambda e, g=g: e.tensor_tensor(out=OG[:, g, :, :], in0=rden[:], in1=szA[:, g, :, :], op=ALU.mult),
                     [rden, szA], [OG])
            S.dma("sp", og_d[gc], OG[:].rearrange("p g j t -> p (g j) t"), reads=[OG], writes=["og_d"])
            py = [pb(), pb()]
            for h in range(16):
                o = py[h // 8][:, (h % 8) * 64:(h % 8 + 1) * 64]
                mm(o, Mm["f"][:, h, :], xdt["f"][:, h * 64:(h + 1) * 64], True, False, [Mm["f"], xdt["f"]], [py[h // 8]])
                mm(o, Mm["b"][:, h, :], xdt["b"][:, h * 64:(h + 1) * 64], False, True, [Mm["b"], xdt["b"]], [py[h // 8]])
            pof = [pb(), pb()]
            for g in range(2):
                mm(pof[g][:, 0:512], btct[:, 2 + g, :], S16f[:, g * 512:(g + 1) * 512], True, True, [btct, S16f], [pof[g]])
            for g in range(2):
                S.op("dve", lambda e, g=g: e.tensor_tensor(
                    out=tmpA[:, g * 512:(g + 1) * 512].rearrange("p (h q) -> p h q", h=8),
                    in0=pof[g][:, 0:512].rearrange("p (h q) -> p h q", h=8),
                    in1=Ex[:, 32 + g * 8: 32 + (g + 1) * 8].unsqueeze(2).to_broadcast([128, 8, 64]), op=ALU.mult),
                    [pof[g], Ex], [tmpA])
                S.op("dve", lambda e, g=g: e.tensor_tensor(out=tmpA[:, g * 512:(g + 1) * 512], in0=tmpA[:, g * 512:(g + 1) * 512],
                                                            in1=py[g][:, 0:512], op=ALU.add), [tmpA, py[g]], [tmpA])
            pob = [pb(), pb()]
            for g in range(2):
                mm(pob[g][:, 0:512], btct[:, 2 + g, :], sbin[:, g * 512:(g + 1) * 512], True, True, [btct, sbin], [pob[g]])
            for g in range(2):
                S.op("dve", lambda e, g=g: e.tensor_tensor(
                    out=tmpB[:, g * 512:(g + 1) * 512].rearrange("p (h q) -> p h q", h=8),
                    in0=pob[g][:, 0:512].rearrange("p (h q) -> p h q", h=8),
                    in1=Ex[:, 48 + g * 8: 48 + (g + 1) * 8].unsqueeze(2).to_broadcast([128, 8, 64]), op=ALU.mult),
                    [pob[g], Ex], [tmpB])
            S.op("pool", lambda e: e.tensor_tensor(out=tmpA[:], in0=tmpA[:], in1=tmpB[:], op=ALU.add), [tmpA, tmpB], [tmpA])
            S.op("pool", lambda e: e.tensor_tensor(
                out=tmpB[:].rearrange("p (h q) -> p h q", h=16), in0=xs[:].rearrange("p (h q) -> p h q", h=16),
                in1=small[:, 64:80].unsqueeze(2).to_broadcast([128, 16, 64]), op=ALU.mult), [xs, small], [tmpB])
            S.op("pool", lambda e: e.tensor_tensor(out=tmpA[:], in0=tmpA[:], in1=tmpB[:], op=ALU.add), [tmpA, tmpB], [tmpA])
            S.op("dve", lambda e: e.tensor_tensor(out=tmpA[:], in0=tmpA[:], in1=sz[:], op=ALU.mult), [tmpA, sz], [tmpA])
            S.op("act", lambda e: e.activation(out=tmpB[:], in_=tmpA[:], func=AF.Square), [tmpA], [tmpB])
            S.op("dve", lambda e: e.reduce_sum(out=st8[:, 2:4], in_=tmpB[:].rearrange("p (g q) -> p g q", g=2), axis=AX.X),
                 [tmpB], [st8])
            S.op("dve", lambda e: e.tensor_scalar(out=st8[:, 2:4], in0=st8[:, 2:4], scalar1=1.0 / 512, scalar2=EPS,
                                                   op0=ALU.mult, op1=ALU.add), [st8], [st8])
            S.op("act", lambda e: e.activation(out=st8[:, 2:4], in_=st8[:, 2:4], func=AF.Ln), [st8], [st8])
            S.op("act", lambda e: e.activation(out=st8[:, 2:4], in_=st8[:, 2:4], func=AF.Exp, scale=-0.5), [st8], [st8])
            for g in range(2):
                S.op("dve", lambda e, g=g: e.scalar_tensor_tensor(
                    out=ysb[:, g * 512:(g + 1) * 512], in0=tmpA[:, g * 512:(g + 1) * 512], scalar=st8[:, 2 + g:3 + g],
                    in1=snw_b[:, g * 512:(g + 1) * 512], op0=ALU.mult, op1=ALU.mult), [tmpA, st8, snw_b], [ysb])
            for m in range(8):
                tr(psT[:, m * 128:(m + 1) * 128], ysb[:, m * 128:(m + 1) * 128], [ysb], [psT])
            S.op("act", lambda e: e.copy(out=ysT[:].rearrange("p k t -> p (k t)"), in_=psT[:]), [psT], [ysT])
            S.dma("sp", yc_d[gc, :, 0:8, :], ysT[:], reads=[ysT], writes=["yc_d"])
            state_update(TS, "SU", (32, 48), (0, 16), gc)

        S.begin("passC")
        for gc in range(NCH):
            S.itn = gc
            take(pf, 1)
            bsel[0] = (0, 3)
            stageF(gc)
            bsel[0] = (3, 7)
            stageB(gc)
        bsel[0] = None
        take(pf, 999)
        S.end()
        S.emit_scheduled("passC", overlap=OVL_C)

    def passC3(l):
        last = (l == DEPTH - 1)
        if not pending.pop(("3", l), False):
            take(pieces_3(l, T3), 999)
        pf = pieces_A(l + 1, T3) if l + 1 < DEPTH else []
        if pf:
            pending[("A", l + 1)] = True
        S.begin("p3")
        for gc in range(NCH):
            S.itn = gc
            take(pf, 1)
            is_ctx = gc < 2
            if is_ctx and last:
                continue
            p = str(gc % 2)
            yc, og, xin, xnew, tmpB, stt = (T3[n][p] for n in ("yc", "og", "xin", "xnew", "tmpB", "st"))
            if is_ctx:
                src = (ctx_in if l == 0 else ctx1_d)[gc * 128:(gc + 1) * 128, :]
            else:
                src = (x_in if l == 0 else x1_d)[(gc - 2) * 128:(gc - 1) * 128, :]
            S.dma("sp", yc[:], yc_d[gc], reads=["yc_d"], writes=[yc])
            ogv = og_d[gc].rearrange("d (t two) k -> two d t k", two=2)
            S.dma("sp", og[0:64, :, :], ogv[0], reads=["og_d"], writes=[og])
            S.dma("sp", og[64:128, :, :], ogv[1], reads=["og_d"], writes=[og])
            S.dma("sp", xin[:], src, reads=["x1_d", "ctx1_d"], writes=[xin])
            pout = [pb(), pb()]
            g_t = aux if is_ctx else g_l
            for h2 in range(2):
                ns = slice(h2 * 512, (h2 + 1) * 512)
                steps = []
                for t in range(12):
                    steps.append((yc[:, t, :], W[:, WO0 + t * 1024 + h2 * 512: WO0 + t * 1024 + (h2 + 1) * 512], [yc, W_3]))
                for t in range(4):
                    steps.append((og[:, t, :], W[:, WO0 + (12 + t) * 1024 + h2 * 512: WO0 + (12 + t) * 1024 + (h2 + 1) * 512], [og, W_3]))
                for i, (lt, rh, rd) in enumerate(steps):
                    mm(pout[h2][:, 0:512], lt, rh, i == 0, i == len(steps) - 1, rd, [pout[h2]])
                S.op("dve", lambda e, h2=h2, ns=ns, g_t=g_t, xnew=xnew, pout=pout: e.tensor_tensor(
                    out=xnew[:, ns], in0=pout[h2][:, 0:512], in1=g_t[:, ns], op=ALU.mult), [pout[h2], g_t], [xnew])
            S.op("pool", lambda e, xnew=xnew, xin=xin: e.tensor_tensor(out=xnew[:], in0=xnew[:], in1=xin[:], op=ALU.add),
                 [xnew, xin], [xnew])
            if not last:
                if is_ctx:
                    S.dma("sp", ctx1_d[gc * 128:(gc + 1) * 128, :], xnew[:], reads=[xnew], writes=["ctx1_d"])
                else:
                    S.dma("sp", x1_d[(gc - 2) * 128:(gc - 1) * 128, :], xnew[:], reads=[xnew], writes=["x1_d"])
            else:
                rms_rstd(xnew, tmpB, D, st8=stt)
                S.op("dve", lambda e, xnew=xnew, tmpB=tmpB, stt=stt: e.scalar_tensor_tensor(
                    out=tmpB[:], in0=xnew[:], scalar=stt[:, 0:1], in1=aux[:], op0=ALU.mult, op1=ALU.mult),
                    [xnew, stt, aux], [tmpB])
                S.dma("sp", out_d[(gc - 2) * 128:(gc - 1) * 128, :], tmpB[:], reads=[tmpB], writes=["out_d"])
        take(pf, 999)
        S.end()
        S.emit_scheduled("p3", overlap=OVL_C)

    build_consts()
    for l in range(n_layers):
        S.barrier()
        layer_setup(l)
        if stop_after == (l, "setup"):
            break
        pass0(l)
        if stop_after == (l, "p0"):
            break
        S.barrier()
        passA(l)
        if stop_after == (l, "pA"):
            break
        S.barrier()
        passQ(l)
        S.barrier()
        passC(l)
        S.barrier()
        passC3(l)
    S.wait_all("sp")
    build_program.stats = (S.ninst, S.nwaits, getattr(S, "sim_time", 0.0))
    build_program.sim_log = getattr(S, "sim_log", [])
    return nc


def _rope_table():
    f32 = np.float32
    tok = np.arange(T_LAT)
    rows = (tok // 64).astype(f32)
    cols = (tok % 64).astype(f32)
    inv = (f32(10000.0) ** (-(np.arange(0, 32, 2).astype(f32)) / f32(32))).astype(f32)
    ang = np.concatenate([rows[:, None] * inv[None, :], cols[:, None] * inv[None, :]], axis=-1).astype(f32)
    cos, sin = np.cos(ang).astype(f32), np.sin(ang).astype(f32)
    tab = np.zeros((128, 2, T_LAT), f32)
    for p in range(128):
        d = p % 64
        a, r = divmod(d, 32)
        j, i = divmod(r, 16)
        tab[p, 0] = cos[:, a * 16 + i]
        tab[p, 1] = sin[:, a * 16 + i] * (-1.0 if j == 0 else 1.0)
    return tab


def _prep_shared(inp):
    f = lambda a: np.ascontiguousarray(np.asarray(a, dtype=np.float32))
    cA, cC, cQ = np.array(_cols_A()), np.array(_cols_C()), np.array(_cols_Q())
    w_in = np.asarray(inp["w_in"], dtype=np.float32)
    w_out = np.asarray(inp["w_out"], dtype=np.float32)
    sh = {
        "norm_w": f(inp["norm_w"]).reshape(DEPTH, 1, D),
        "w_mod": f(inp["w_mod"]),
        "b_mod": f(inp["b_mod"]).reshape(DEPTH, 1, 3 * D),
        "wA": f(w_in[:, :, cA]),
        "wC": f(w_in[:, :, cC]),
        "wQ": f(w_in[:, :, cQ]),
        "wo_all": f(w_out),
        "convw": f(np.asarray(inp["ssd_conv_w"]).reshape(DEPTH, 3, 12, 128).transpose(0, 3, 2, 1)),
        "convb": f(np.asarray(inp["ssd_conv_b"]).reshape(DEPTH, 12, 128).transpose(0, 2, 1)),
        "scw": f(np.asarray(inp["sc_conv_w"]).reshape(DEPTH, 3, 4, 128).transpose(0, 3, 2, 1)),
        "dt_bias": f(inp["ssd_dt_bias"]).reshape(DEPTH, 1, 32),
        "a_log": f(inp["ssd_a_log"]).reshape(DEPTH, 1, 32),
        "ssd_d": f(inp["ssd_d"]).reshape(DEPTH, 1, 16),
        "ssd_norm_w": f(inp["ssd_norm_w"]).reshape(DEPTH, 1, D),
        "sink": f(inp["attn_sink"]).reshape(DEPTH, 1, 8),
        "final_norm_w": f(inp["final_norm_w"]).reshape(1, D),
        "ropecs": _rope_table(),
    }
    return sh


def _in_maps(inp, n_cores=8):
    sh = _prep_shared(inp)
    x = np.asarray(inp["x"], dtype=np.float32)
    c = np.asarray(inp["c"], dtype=np.float32)
    ctx = np.asarray(inp["ctx"], dtype=np.float32)
    c_ctx = np.asarray(inp["c_ctx"], dtype=np.float32)
    maps = []
    for core in range(n_cores):
        b = core % 4
        cc = np.stack([c[b].reshape(8, 128).T, c_ctx.reshape(8, 128).T], axis=-1)
        m = dict(sh)
        m["x"] = np.ascontiguousarray(x[b])
        m["ctx"] = np.ascontiguousarray(ctx[b])
        m["cc"] = np.ascontiguousarray(cc.astype(np.float32))
        maps.append(m)
    return maps


def kernel(x, c, ctx, c_ctx, norm_w, w_mod, b_mod, w_in, ssd_conv_w, ssd_conv_b, ssd_dt_bias, ssd_a_log, ssd_d,
           ssd_norm_w, sc_conv_w, attn_sink, w_out, final_norm_w):
    inp = dict(x=x, c=c, ctx=ctx, c_ctx=c_ctx, norm_w=norm_w, w_mod=w_mod, b_mod=b_mod, w_in=w_in,
               ssd_conv_w=ssd_conv_w, ssd_conv_b=ssd_conv_b, ssd_dt_bias=ssd_dt_bias, ssd_a_log=ssd_a_log,
               ssd_d=ssd_d, ssd_norm_w=ssd_norm_w, sc_conv_w=sc_conv_w, attn_sink=attn_sink, w_out=w_out,
               final_norm_w=final_norm_w)
    nc = build_program()
    maps = _in_maps(inp, 8)
    res = run_bass_kernel_spmd(nc, maps, core_ids=list(range(8)))
    out = np.stack([np.asarray(res.results[b]["out"], dtype=np.float32).reshape(T_LAT, D) for b in range(4)], axis=0)
    return out
```

```python
import numpy as np
import concourse.bass as bass
import concourse.mybir as mybir
from concourse.bass_utils import run_bass_kernel_spmd

F32 = mybir.dt.float32
BF16 = mybir.dt.bfloat16
AF = mybir.ActivationFunctionType
ALU = mybir.AluOpType
AX = mybir.AxisListType

D = 1024
T_LAT = 4096
T_CTX = 256
NCH = 34
T_ALL = NCH * 128
DEPTH = 2
EPS = 1e-6

O_Z = 0
O_XS = 1024
O_B = 2048
O_C = 2304
O_DT = 2560
O_SCV = 2592
O_SCC = 3104
O_SCB = 3616
O_SCZ = 4128
O_Q = 4640
O_K = 5152
O_V = 5280
O_ZA = 5408


def _rope_partner(d):
    a, r = divmod(d, 32)
    j, i = divmod(r, 16)
    return a * 32 + (1 - j) * 16 + i


def _cols_A():
    cols = list(range(O_XS, O_XS + 1536))
    cols += list(range(O_SCV, O_SCV + 512))
    cols += list(range(O_SCC, O_SCC + 512))
    cols += list(range(O_K, O_K + 128))
    cols += [O_K + g * 64 + _rope_partner(d) for g in range(2) for d in range(64)]
    cols += list(range(O_V, O_V + 128))
    cols += list(range(O_DT, O_DT + 32))
    return cols


A_XBC, A_SCV, A_SCC, A_K, A_KSW, A_V, A_DT, NA = 0, 1536, 2048, 2560, 2688, 2816, 2944, 2976


def _cols_C():
    return list(range(O_Z, O_Z + 1024))


def _cols_Q():
    cols = list(range(O_SCB, O_SCB + 512))
    cols += list(range(O_SCZ, O_SCZ + 512))
    qt = []
    qs = []
    for j in range(4):
        for hq in (j, 4 + j):
            qt += [O_Q + hq * 64 + d for d in range(64)]
            qs += [O_Q + hq * 64 + _rope_partner(d) for d in range(64)]
    cols += qt + qs
    cols += list(range(O_ZA, O_ZA + 512))
    return cols


C_Z, NCC = 0, 1024
Q_SCB, Q_SCZ, Q_Q, Q_QSW, Q_ZA, NQ = 0, 512, 1024, 1536, 2048, 2560


class Tl:
    def __init__(self, t, k):
        self.t = t
        self.k = k

    def __getitem__(self, idx):
        return self.t[idx]


class Sched:
    def __init__(self, nc, n_dma_sems=12):
        self.nc = nc
        self.engs = {"pe": nc.tensor, "act": nc.scalar, "dve": nc.vector,
                     "pool": nc.gpsimd, "sp": nc.sync}
        self.sem = {}
        self.cnt = {}
        for e in self.engs:
            self.sem[e] = nc.alloc_semaphore("s_" + e)
            self.cnt[e] = 0
        self.dring = {}
        for q in ("sp", "act", "pool"):
            self.dring[q] = [[nc.alloc_semaphore("d_%s%d" % (q, i)), 0] for i in range(n_dma_sems)]
        self.dpos = {q: 0 for q in self.dring}
        self.seen = {e: {} for e in self.engs}
        self.state = {}
        self.ninst = 0
        self.nwaits = 0
        self.cur = None
        self.stages = {}
        self.itn = 0

    def begin(self, name):
        self.cur = self.stages.setdefault(name, [])

    def end(self):
        self.cur = None

    def emit(self, name):
        assert self.cur is None
        for item in self.stages.pop(name, []):
            self._emit_item(item)

    def emit_scheduled(self, name, overlap=3.0):
        assert self.cur is None
        items = self.stages.pop(name, [])
        n = len(items)
        if n == 0:
            return

        class _Probe:
            def __init__(self):
                self.nm, self.a, self.k = None, (), {}

            def __getattr__(self, nm):
                def f(*a, **k):
                    self.nm, self.a, self.k = nm, a, k
                    return self
                return f

        def cost_of(it):
            if it[0] == "dma":
                return 0.1, 2.2
            pr = _Probe()
            try:
                it[2](pr)
                out = pr.k.get("out", pr.a[0] if pr.a else None)
                shp = tuple(out.shape)
                nel = 1
                for d_ in shp[1:]:
                    nel *= int(d_)
            except Exception:
                nel = 512
            eng = it[1]
            if eng == "pe":
                c = 0.12 if pr.nm == "transpose" else 0.04 + nel / 1400.0
            elif eng == "act":
                c = 0.2 + nel / 1150.0
            elif eng == "dve":
                c = 0.15 + nel / 680.0
            else:
                c = 0.2 + nel / 580.0
            return c, c

        last_w = {}
        readers = {}
        preds = [set() for _ in range(n)]
        for i, it in enumerate(items):
            if it[0] == "dma":
                reads, writes = it[4], it[5]
            else:
                reads, writes = it[3], it[4]
            for k in reads:
                w = last_w.get(k)
                if w is not None:
                    preds[i].add(w)
            for k in writes:
                w = last_w.get(k)
                if w is not None:
                    preds[i].add(w)
                for r in readers.get(k, ()):
                    preds[i].add(r)
            for k in reads:
                readers.setdefault(k, []).append(i)
            for k in writes:
                last_w[k] = i
                readers[k] = []
        succs = [[] for _ in range(n)]
        indeg = [0] * n
        for i in range(n):
            preds[i].discard(i)
            indeg[i] = len(preds[i])
            for p_ in preds[i]:
                succs[p_].append(i)
        engs = [it[1] for it in items]
        costs = [cost_of(it) for it in items]
        ready_t = [0.0] * n
        free = {}
        ready = set(i for i in range(n) if indeg[i] == 0)
        window = int(overlap * 1200)
        done = 0
        lowest = 0
        emitted = [False] * n
        while ready:
            best, bkey = None, None
            for i in ready:
                if i - lowest > window:
                    continue
                st = max(ready_t[i], free.get(engs[i], 0.0))
                key = (st, i)
                if bkey is None or key < bkey:
                    best, bkey = i, key
            if best is None:
                best = min(ready)
                bkey = (max(ready_t[best], free.get(engs[best], 0.0)), best)
            i = best
            ready.discard(i)
            st = bkey[0]
            busy, lat = costs[i]
            free[engs[i]] = st + busy
            fin = st + lat
            it = items[i]
            if it[0] == "op":
                self.op(*it[1:5])
            else:
                self.dma(*it[1:6], **it[6])
            emitted[i] = True
            while lowest < n and emitted[lowest]:
                lowest += 1
            done += 1
            for j in succs[i]:
                hop = 0.05 if engs[j] == engs[i] else 0.45
                if fin + hop > ready_t[j]:
                    ready_t[j] = fin + hop
                indeg[j] -= 1
                if indeg[j] == 0:
                    ready.add(j)
        assert done == n, (done, n)
        self.sim_time = getattr(self, "sim_time", 0.0) + max(free.values())
        busy = {}
        for i in range(n):
            busy[engs[i]] = busy.get(engs[i], 0.0) + costs[i][0]
        self.sim_log = getattr(self, "sim_log", [])
        self.sim_log.append((str(name), round(max(free.values()), 1), {k_: round(v_, 1) for k_, v_ in busy.items()}))

    def _emit_item(self, item):
        if item[0] == "op":
            self.op(*item[1:5])
        else:
            self.dma(*item[1:6], **item[6])

    def emit_merged(self, name_a, name_b):
        assert self.cur is None
        la = self.stages.pop(name_a, [])
        lb = self.stages.pop(name_b, [])
        i = j = 0
        while i < len(la) or j < len(lb):
            if j >= len(lb) or (i < len(la) and i * len(lb) <= j * len(la)):
                self._emit_item(la[i])
                i += 1
            else:
                self._emit_item(lb[j])
                j += 1

    @staticmethod
    def _keys(lst):
        return [x.k if isinstance(x, Tl) else x for x in lst]

    def _need(self, eng, ev):
        sem, val, src = ev
        if src == "pe" and eng == "pe":
            return
        cur = self.seen[eng].get(sem.name, 0)
        if cur >= val:
            return
        self.seen[eng][sem.name] = val
        self.engs[eng].wait_ge(sem, val)
        self.nwaits += 1

    def _deps(self, eng, reads, writes):
        for k in reads:
            st = self.state.get(k)
            if st and st[0] is not None:
                self._need(eng, st[0])
        for k in writes:
            st = self.state.get(k)
            if st:
                if st[0] is not None:
                    self._need(eng, st[0])
                for ev in st[1].values():
                    self._need(eng, ev)

    def _record(self, ev, reads, writes):
        for k in reads:
            st = self.state.setdefault(k, [None, {}])
            old = st[1].get(ev[0].name)
            if old is None or old[1] < ev[1]:
                st[1][ev[0].name] = ev
        for k in writes:
            self.state[k] = [ev, {}]

    def op(self, eng, fn, reads=(), writes=()):
        reads = self._keys(reads)
        writes = self._keys(writes)
        if self.cur is not None:
            self.cur.append(("op", eng, fn, reads, writes, self.itn))
            return
        self._deps(eng, reads, writes)
        ins = fn(self.engs[eng])
        self.cnt[eng] += 1
        ins.then_inc(self.sem[eng], 1)
        ev = (self.sem[eng], self.cnt[eng], eng)
        self._record(ev, reads, writes)
        self.ninst += 1

    def dma(self, q, out, in_, reads=(), writes=(), **kw):
        reads = self._keys(reads)
        writes = self._keys(writes)
        if self.cur is not None:
            self.cur.append(("dma", q, out, in_, reads, writes, kw, self.itn))
            return
        ring = self.dring[q]
        slot = ring[self.dpos[q] % len(ring)]
        self.dpos[q] += 1
        sem, tot = slot
        if tot > 0:
            self._need(q, (sem, tot, None))
        self._deps(q, reads, writes)
        ins = self.engs[q].dma_start(out=out, in_=in_, **kw)
        slot[1] = tot + 16
        ins.then_inc(sem, 16)
        ev = (sem, slot[1], None)
        self._record(ev, reads, writes)
        self.ninst += 1

    def barrier(self):
        evs = [(self.sem[e], self.cnt[e], e) for e in self.engs if self.cnt[e] > 0]
        for q in self.dring:
            for sem, tot in self.dring[q]:
                if tot > 0:
                    evs.append((sem, tot, None))
        for e in self.engs:
            for ev in evs:
                self._need(e, ev)

    def wait_all(self, eng="sp"):
        for k, st in list(self.state.items()):
            if st[0] is not None:
                self._need(eng, st[0])
            for ev in st[1].values():
                self._need(eng, ev)


OVL_C = 3.0


def build_program(debug_out=None, n_layers=DEPTH, stop_after=None):
    nc = bass.Bass("TRN2", target_bir_lowering=False)
    S = Sched(nc)

    def din(name, shape, dt=F32):
        return nc.dram_tensor(name, list(shape), dt, kind="ExternalInput").ap()

    dbg = set(debug_out or [])

    def dscr(name, shape, dt=F32):
        kind = "ExternalOutput" if name in dbg else "Internal"
        return nc.dram_tensor(name, list(shape), dt, kind=kind).ap()

    x_in = din("x", [T_LAT, D])
    ctx_in = din("ctx", [T_CTX, D])
    cc_in = din("cc", [128, 8, 2])
    normw_in = din("norm_w", [DEPTH, 1, D])
    wmod_in = din("w_mod", [DEPTH, D, 3 * D])
    bmod_in = din("b_mod", [DEPTH, 1, 3 * D])
    wA_in = din("wA", [DEPTH, D, NA])
    wC_in = din("wC", [DEPTH, D, NCC])
    wQ_in = din("wQ", [DEPTH, D, NQ])
    wom_in = din("wo_all", [DEPTH, 2048, D])
    convw_in = din("convw", [DEPTH, 128, 12, 3])
    convb_in = din("convb", [DEPTH, 128, 12])
    scw_in = din("scw", [DEPTH, 128, 4, 3])
    dtb_in = din("dt_bias", [DEPTH, 1, 32])
    alog_in = din("a_log", [DEPTH, 1, 32])
    dsk_in = din("ssd_d", [DEPTH, 1, 16])
    snw_in = din("ssd_norm_w", [DEPTH, 1, D])
    sink_in = din("sink", [DEPTH, 1, 8])
    fnw_in = din("final_norm_w", [1, D])
    rope_in = din("ropecs", [128, 2, T_LAT])
    out_d = nc.dram_tensor("out", [T_LAT, D], F32, kind="ExternalOutput").ap()

    x1_d = dscr("x1", [T_LAT, D])
    ctx1_d = dscr("ctx1", [T_CTX, D])
    hT_d = dscr("hT_all", [NCH, 128, 8, 128], BF16)
    xs_d = dscr("xs_all", [NCH, 128, 1024], BF16)
    btm_d = dscr("btm_all", [NCH, 128, 256], BF16)
    bt_d = dscr("bt_all", [NCH, 128, 2, 128], BF16)
    ct_d = dscr("ct_all", [NCH, 128, 2, 128], BF16)
    dtda_d = dscr("dtda_all", [NCH, 128, 64])
    sb_d = dscr("sb_all", [NCH, 128, 1024], BF16)
    cvc_d = dscr("cvc_all", [NCH, 128, 4, 128])
    kt_d = dscr("kt_all", [128, T_ALL], BF16)
    v_d = dscr("v_all", [NCH, 128, 128], BF16)
    mod_d = dscr("mod_scr", [2, 3 * D])
    yc_d = dscr("yc_all", [NCH, 128, 12, 128], BF16)
    og_d = dscr("og_all", [NCH, 64, 8, 128], BF16)
    qt_d = dscr("qt_all", [NCH, 128, 4, 128], BF16)
    za_d = dscr("za_all", [NCH, 64, 8, 128])
    dbg_ys = dscr("dbg_ys", [NCH, 128, 1024], BF16) if "dbg_ys" in dbg else None
    dbg_sc = dscr("dbg_sc", [NCH, 128, 4, 128], BF16) if "dbg_sc" in dbg else None
    dbg_og = dscr("dbg_og", [NCH, 64, 2, 4, 128], BF16) if "dbg_og" in dbg else None

    SB_BASE, SB_END = 16640, 229376
    DTB = {F32: 4, BF16: 2}
    ptr = {"persist": SB_BASE}
    lim = {}

    def _alloc(space, name, shape, dt):
        n = 1
        for d_ in shape[1:]:
            n *= d_
        nbytes = (n * DTB[dt] + 63) // 64 * 64
        off = ptr[space]
        ptr[space] = off + nbytes
        assert ptr[space] <= lim.get(space, SB_END), (space, name, ptr[space])
        uname = "%s_%s" % (space, name)
        return Tl(nc.alloc_sbuf_tensor_at(uname, list(shape), dt, offset=off), uname)

    def sb(name, shape, dt=F32):
        return _alloc("persist", name, shape, dt)

    W = sb("W", [128, 49152], BF16)
    ident_b = sb("ident_b", [128, 128], BF16)
    cm_f = {n: sb("cm_" + n, [128, 128], F32) for n in ("UI", "LI", "SU", "SL", "ones")}
    cm_b = {n: sb("cb_" + n, [128, 128], BF16) for n in ("UI", "LI", "SU", "SL", "ones")}
    negm = {n: sb("negm_" + n, [128, 4, 128], BF16) for n in ("UI", "LI")}
    g_l = sb("g_l", [128, D])
    snw_b = sb("snw_b", [128, D])
    aux = sb("aux", [128, D])
    small = sb("small", [128, 128])
    convw = sb("convw", [128, 12, 3])
    convb = sb("convb", [128, 12])
    scw = sb("scw", [128, 4, 3])
    cc = sb("cc", [128, 8, 2])
    st8 = sb("st8", [128, 8])
    PH_BASE = ptr["persist"]

    def phase_tiles(space, specs):
        ptr[space] = PH_BASE
        d_ = {}
        for (name, shape, dt) in specs:
            if isinstance(name, tuple):
                d_[name[0]] = {k_: _alloc(space, "%s_%s" % (name[0], k_), shape, dt) for k_ in name[1]}
            else:
                d_[name] = _alloc(space, name, shape, dt)
        return d_

    dbl = lambda n: (n, ("0", "1"))
    T0 = phase_tiles("p0", [
        ("A_l", [128, D], F32), ("sh_l", [128, D], F32), ("A_c", [128, D], F32), ("sh_c", [128, D], F32),
        ("tmpA", [128, D], F32), ("stg0", [128, 1024], F32), ("stg1", [128, 1024], F32),
        ("modsb", [2, 3 * D], F32), ("bmod2", [2, 3 * D], F32),
        (dbl("xin"), [128, D], F32), (dbl("sq"), [128, D], F32), (dbl("tmpB"), [128, D], F32), (dbl("hb"), [128, D], BF16),
        (dbl("hT"), [128, 8, 128], BF16), (dbl("st"), [128, 8], F32)])
    TA = phase_tiles("pA", [
        ("stg0", [128, 1024], F32), ("stg1", [128, 1024], F32), (dbl("hTw"), [128, 8, 258], BF16),
        (dbl("xbcT"), [128, 12, 256], BF16), (dbl("cv"), [128, 258], F32), (dbl("cv2"), [128, 258], F32),
        (dbl("cvc"), [128, 4, 256], F32),
        (dbl("ropeT"), [128, 2, 256], F32), (dbl("ktile"), [128, 256], BF16), (dbl("xs"), [128, 1024], BF16),
        (dbl("btm"), [128, 256], BF16),
        (dbl("dtda"), [128, 64], F32), (dbl("dtr"), [128, 32], F32), ("ExS", [128, 64], F32), ("wsm", [128, 32], F32),
        (dbl("vtm"), [128, 128], BF16), ("rhsb", [128, 1024], BF16), ("S", [128, 1024], F32), ("S16", [128, 1024], BF16)])
    dbl = lambda n: (n, ("0", "1"))
    TC = phase_tiles("pC", [
        ("stg0", [128, 1024], F32), ("stg1", [128, 1024], F32),
        (dbl("xs"), [128, 1024], BF16), (dbl("btm"), [128, 256], BF16), (dbl("dtda"), [128, 64], F32),
        (dbl("hT"), [128, 8, 128], BF16), (dbl("btct"), [128, 4, 128], BF16), (dbl("sbin"), [128, 1024], BF16),
        (dbl("KTb"), [128, 5, 128], BF16),
        (dbl("Vb"), [128, 5, 128], BF16), (dbl("sz"), [128, 1024], F32), (dbl("Ex"), [128, 64], F32),
        (dbl("M_f"), [128, 16, 128], BF16), (dbl("M_b"), [128, 16, 128], BF16), (dbl("xdt_f"), [128, 1024], BF16),
        (dbl("xdt_b"), [128, 1024], BF16), (dbl("QT"), [128, 4, 128], BF16), (dbl("szA"), [64, 2, 4, 128], F32),
        ("OG", [64, 2, 4, 128], BF16), ("rden", [64, 4, 128], F32),
        ("hilo", [128, 64], BF16), ("dres", [128, 32], F32), ("wsm", [128, 32], F32), ("ExS", [128, 64], F32)])
    ptr["pCb"] = SB_BASE + 8 * NCC * 2
    lim["pCb"] = SB_BASE + 65536
    _pcb_base = ptr["pCb"]
    TCb = {}
    for (name, shape, dt) in [
            ("rb0", [128, 16, 128], BF16), ("rb1", [128, 16, 128], BF16), ("Lm0", [128, 4, 128], F32), ("Lm1", [128, 4, 128], F32),
            ("tmpA", [128, D], F32), ("tmpB", [128, D], F32), ("cbm", [128, 2, 2, 128], F32),
            ("ysb", [128, 1024], BF16), ("ysT", [128, 8, 128], BF16), ("rhsb", [128, 1024], BF16), ("S", [128, 1024], F32),
            ("S16", [128, 1024], BF16), ("PT0", [128, 512], BF16), ("PT1", [128, 512], BF16)]:
        TCb[name] = _alloc("pCb", name, shape, dt)
    TC.update(TCb)
    NBUF_C = 2
    for (name, shape, dt, sp_) in [
            ("tmpA", [128, D], F32, "pCb"), ("tmpB", [128, D], F32, "pCb"), ("ysb", [128, 1024], BF16, "pC"),
            ("ysT", [128, 8, 128], BF16, "pC"), ("PT0", [128, 512], BF16, "pC"), ("PT1", [128, 512], BF16, "pC"),
            ("rhsb", [128, 1024], BF16, "pC")]:
        TC[name] = {"0": TC[name], "1": _alloc(sp_, name + "_b", shape, dt)}
    for (name, shape, dt) in [("OG", [64, 2, 4, 128], BF16), ("rden", [64, 4, 128], F32), ("wsm", [128, 32], F32),
                              ("ExS", [128, 64], F32), ("stB", [128, 8], F32)]:
        first = TC[name] if name in TC else _alloc("pC", name + "_a", shape, dt)
        TC[name] = {"0": first, "1": _alloc("pC", name + "_b", shape, dt)}
    TQ = phase_tiles("pQ", [
        ("stg0", [128, 1024], F32), ("stg1", [128, 1024], F32), ("sct", [128, 512], F32), ("qtmp", [128, 1024], F32),
        (dbl("hTq"), [128, 8, 512], BF16), (dbl("cvq"), [128, 4, 512], F32), (dbl("ropq"), [128, 2, 512], F32),
        (dbl("scy"), [128, 4, 512], BF16), (dbl("QTq"), [128, 4, 512], BF16), (dbl("zaq"), [64, 4, 512], F32)])
    T3 = phase_tiles("p3", [
        ("stg0", [128, 1024], F32), ("stg1", [128, 1024], F32),
        (dbl("yc"), [128, 12, 128], BF16), (dbl("og"), [128, 4, 128], BF16), (dbl("xin"), [128, D], F32),
        (dbl("xnew"), [128, D], F32), (dbl("tmpB"), [128, D], F32), (dbl("st"), [128, 8], F32)])
    print("SBUF bytes: persist %d  p0 %d  pA %d  pC %d pCb %d/%d p3 %d pQ %d (end %d)" % (PH_BASE, ptr["p0"], ptr["pA"], ptr["pC"], ptr["pCb"], lim["pCb"], ptr["p3"], ptr["pQ"], SB_END))

    def ps(name, shape, dt=F32):
        return Tl(nc.alloc_psum_tensor(name, list(shape), dt), name)

    psT = ps("psT", [128, 1024], BF16)
    banks = [ps("bk%d" % i, [128, 512]) for i in range(7)]
    bpos = [0]

    bsel = [None]
    bsub = {}

    def pb():
        if bsel[0] is None:
            b = banks[bpos[0] % len(banks)]
            bpos[0] += 1
            return b
        lo, hi = bsel[0]
        c = bsub.get(bsel[0], 0)
        bsub[bsel[0]] = c + 1
        return banks[lo + c % (hi - lo)]

    def mm(out_ap, lhsT, rhs, start, stop, reads, writes):
        S.op("pe", lambda e: e.matmul(out_ap, lhsT=lhsT, rhs=rhs, start=start, stop=stop), reads, writes)

    def tr(out_ap, in_ap, reads, writes):
        S.op("pe", lambda e: e.transpose(out_ap, in_ap, ident_b[:]), list(reads) + [ident_b], writes)

    def bc_row(dst_tile, dst_ap, src_row_ap, n=128):
        S.dma("sp", dst_ap, src_row_ap.partition_broadcast(n), writes=[dst_tile])

    cast_rr = [0]
    W_A, W_Q, W_C, W_3 = (Tl(W.t, "W_A"), Tl(W.t, "W_Q"), Tl(W.t, "W_C"), Tl(W.t, "W_3"))
    QOFF = 8 * NA
    W3OFF = 32768

    def weight_pieces(TT, key, dst_off, src, ncols, kparts):
        pieces = []
        for k in range(kparts):
            c0 = 0
            while c0 < ncols:
                cw = min(1024, ncols - c0)

                def piece(k=k, c0=c0, cw=cw):
                    st = TT["stg%d" % (cast_rr[0] % 2)]
                    S.dma("sp", st[:, 0:cw], src[k * 128:(k + 1) * 128, c0:c0 + cw], writes=[st])
                    o = dst_off + k * ncols + c0
                    eng = ("act", "dve", "pool")[cast_rr[0] % 3]
                    if eng == "act":
                        S.op("act", lambda e: e.copy(out=W[:, o:o + cw], in_=st[:, 0:cw]), [st], [key])
                    else:
                        S.op(eng, lambda e: e.tensor_copy(out=W[:, o:o + cw], in_=st[:, 0:cw]), [st], [key])
                    cast_rr[0] += 1
                pieces.append(piece)
                c0 += cw
        return pieces

    def pieces_A(l, TT):
        return weight_pieces(TT, W_A, 0, wA_in[l], NA, 8)

    def pieces_Q(l, TT):
        return weight_pieces(TT, W_Q, QOFF, wQ_in[l], NQ, 8)

    def pieces_C(l, TT):
        return weight_pieces(TT, W_C, 0, wC_in[l], NCC, 8)

    def pieces_3(l, TT):
        ps_ = []
        for t in range(16):
            ps_ += weight_pieces(TT, W_3, W3OFF + t * 1024, wom_in[l, t * 128:(t + 1) * 128, :], 1024, 1)
        return ps_

    pending = {}

    def take(pieces, n):
        for _ in range(min(n, len(pieces))):
            pieces.pop(0)()

    def build_consts():
        def sel(t, pattern, cm, op):
            S.op("pool", lambda e: e.memset(t[:], 1.0), [], [t])
            S.op("pool", lambda e: e.affine_select(out=t[:], in_=t[:], pattern=pattern, compare_op=op,
                                                   fill=0.0, base=0, channel_multiplier=cm), [t], [t])
        sel(cm_f["UI"], [[1, 128]], -1, ALU.is_ge)
        sel(cm_f["LI"], [[-1, 128]], 1, ALU.is_ge)
        sel(cm_f["SU"], [[-1, 128]], 1, ALU.is_gt)
        sel(cm_f["SL"], [[1, 128]], -1, ALU.is_gt)
        S.op("pool", lambda e: e.memset(cm_f["ones"][:], 1.0), [], [cm_f["ones"]])
        for n in cm_f:
            S.op("dve", lambda e, n=n: e.tensor_copy(out=cm_b[n][:], in_=cm_f[n][:]), [cm_f[n]], [cm_b[n]])
        S.op("pool", lambda e: e.memset(ident_b[:], 1.0), [], [ident_b])
        S.op("pool", lambda e: e.affine_select(out=ident_b[:], in_=ident_b[:], pattern=[[-1, 128]],
                                               compare_op=ALU.is_equal, fill=0.0, base=0, channel_multiplier=1),
             [ident_b], [ident_b])
        for n in ("UI", "LI"):
            S.op("dve", lambda e, n=n: e.tensor_scalar(
                out=negm[n][:], in0=cm_f[n][:].unsqueeze(1).to_broadcast([128, 4, 128]), scalar1=-1.0, scalar2=2.4e5,
                op0=ALU.add, op1=ALU.mult), [cm_f[n]], [negm[n]])
        S.dma("sp", cc[:], cc_in, writes=[cc])
        S.op("act", lambda e: e.activation(out=cc[:], in_=cc[:], func=AF.Silu), [cc], [cc])

    def layer_setup(l):
        modsb, bmod2, tmpA = T0["modsb"], T0["bmod2"], T0["tmpA"]
        accs = [pb() for _ in range(6)]
        i = 0
        for kc in range(8):
            for third in range(3):
                st = T0["stg%d" % (i % 2)]
                i += 1
                S.dma("sp", st[:, 0:1024], wmod_in[l, kc * 128:(kc + 1) * 128, third * 1024:(third + 1) * 1024],
                      writes=[st])
                for j in range(2):
                    a = accs[third * 2 + j]
                    mm(a[0:2, 0:512], cc[:, kc, :], st[:, j * 512:(j + 1) * 512], kc == 0, kc == 7, [cc, st], [a])
        S.dma("sp", bmod2[:], bmod_in[l].partition_broadcast(2), writes=[bmod2])
        for j in range(6):
            S.op("dve", lambda e, j=j: e.tensor_tensor(out=modsb[:, j * 512:(j + 1) * 512], in0=accs[j][0:2, 0:512],
                                                        in1=bmod2[:, j * 512:(j + 1) * 512], op=ALU.add),
                 [accs[j], bmod2], [modsb])
        S.dma("sp", mod_d, modsb[:], reads=[modsb], writes=["mod_d"])
        for (row, sh_t, A_t, g_t) in ((0, T0["sh_l"], T0["A_l"], g_l), (1, T0["sh_c"], T0["A_c"], aux)):
            S.dma("sp", sh_t[:], mod_d[row:row + 1, 0:D].partition_broadcast(128), reads=["mod_d"], writes=[sh_t])
            S.dma("sp", A_t[:], mod_d[row:row + 1, D:2 * D].partition_broadcast(128), reads=["mod_d"], writes=[A_t])
            if row == 0 or l < DEPTH - 1:
                S.dma("sp", g_t[:], mod_d[row:row + 1, 2 * D:3 * D].partition_broadcast(128), reads=["mod_d"], writes=[g_t])
        if l == DEPTH - 1:
            bc_row(aux, aux[:], fnw_in)
        bc_row(tmpA, tmpA[:], normw_in[l])
        for A_t in (T0["A_l"], T0["A_c"]):
            S.op("dve", lambda e, A_t=A_t: e.scalar_tensor_tensor(out=A_t[:], in0=A_t[:], scalar=1.0, in1=tmpA[:],
                                                                    op0=ALU.add, op1=ALU.mult), [A_t, tmpA], [A_t])
        bc_row(snw_b, snw_b[:], snw_in[l])
        bc_row(small, small[:, 0:32], dtb_in[l])
        bc_row(small, small[:, 32:64], alog_in[l])
        bc_row(small, small[:, 64:80], dsk_in[l])
        bc_row(small, small[:, 80:88], sink_in[l])
        S.op("act", lambda e: e.activation(out=small[:, 32:64], in_=small[:, 32:64], func=AF.Exp), [small], [small])
        S.op("dve", lambda e: e.tensor_scalar_mul(out=small[:, 32:64], in0=small[:, 32:64], scalar1=-1.0), [small], [small])
        S.op("act", lambda e: e.activation(out=small[:, 80:88], in_=small[:, 80:88], func=AF.Exp), [small], [small])
        S.dma("sp", convw[:], convw_in[l], writes=[convw])
        S.dma("sp", convb[:], convb_in[l], writes=[convb])
        S.dma("sp", scw[:], scw_in[l], writes=[scw])

    def rms_rstd(src_tile, scratch, width, st8=st8):
        S.op("act", lambda e: e.activation(out=scratch[:, 0:width], in_=src_tile[:, 0:width], func=AF.Square),
             [src_tile], [scratch])
        S.op("dve", lambda e: e.reduce_sum(out=st8[:, 0:1], in_=scratch[:, 0:width], axis=AX.X), [scratch], [st8])
        S.op("dve", lambda e: e.tensor_scalar(out=st8[:, 0:1], in0=st8[:, 0:1], scalar1=1.0 / width, scalar2=EPS,
                                               op0=ALU.mult, op1=ALU.add), [st8], [st8])
        S.op("act", lambda e: e.activation(out=st8[:, 0:1], in_=st8[:, 0:1], func=AF.Sqrt), [st8], [st8])
        S.op("dve", lambda e: e.reciprocal(out=st8[:, 0:1], in_=st8[:, 0:1]), [st8], [st8])

    def pass0(l):
        pf = []
        if ("A", l) not in pending:
            pf = pieces_A(l, T0)
            pending[("A", l)] = True
        S.begin("p0")
        for gc in range(NCH):
            S.itn = gc
            take(pf, 1)
            p = str(gc % 2)
            xin, sq, tmpB, hb, hT, stt = (T0[n][p] for n in ("xin", "sq", "tmpB", "hb", "hT", "st"))
            if gc < 2:
                src = (ctx_in if l == 0 else ctx1_d)[gc * 128:(gc + 1) * 128, :]
                A_t, sh_t = T0["A_c"], T0["sh_c"]
            else:
                src = (x_in if l == 0 else x1_d)[(gc - 2) * 128:(gc - 1) * 128, :]
                A_t, sh_t = T0["A_l"], T0["sh_l"]
            S.dma("sp", xin[:], src, reads=["x1_d", "ctx1_d"], writes=[xin])
            rms_rstd(xin, sq, D, st8=stt)
            S.op("dve", lambda e, A_t=A_t, xin=xin, tmpB=tmpB, stt=stt: e.scalar_tensor_tensor(
                out=tmpB[:], in0=xin[:], scalar=stt[:, 0:1], in1=A_t[:], op0=ALU.mult, op1=ALU.mult),
                [xin, stt, A_t], [tmpB])
            S.op("pool", lambda e, sh_t=sh_t, hb=hb, tmpB=tmpB: e.tensor_tensor(out=hb[:], in0=tmpB[:], in1=sh_t[:], op=ALU.add),
                 [tmpB, sh_t], [hb])
            for kc in range(8):
                tr(psT[:, kc * 128:(kc + 1) * 128], hb[:, kc * 128:(kc + 1) * 128], [hb], [psT])
            S.op("act", lambda e, hT=hT: e.copy(out=hT[:].rearrange("p k t -> p (k t)"), in_=psT[:]), [psT], [hT])
            S.dma("sp", hT_d[gc], hT[:], reads=[hT], writes=["hT_d"])
        take(pf, 999)
        S.end()
        S.emit_scheduled("p0", overlap=OVL_C)

    def state_update(TT, pmat_key, da_cols, dt_cols, gc, store_d=None):
        Sd, Sd16, xs, btm, dtda, Ex, wsm, rhsb = (TT[n] for n in ("S", "S16", "xs", "btm", "dtda", "ExS", "wsm", "rhsb"))
        p = pb()
        mm(p[:, 0:16], cm_f[pmat_key][:], dtda[:, da_cols[0]:da_cols[1]], True, True, [cm_f[pmat_key], dtda], [p])
        mm(p[:, 16:32], cm_f["ones"][:], dtda[:, da_cols[0]:da_cols[1]], True, True, [cm_f["ones"], dtda], [p])
        S.op("act", lambda e: e.activation(out=Ex[:, 0:32], in_=p[:, 0:32], func=AF.Exp), [p], [Ex])
        S.op("dve", lambda e: e.tensor_tensor(out=wsm[:, 0:16], in0=dtda[:, dt_cols[0]:dt_cols[1]], in1=Ex[:, 0:16],
                                               op=ALU.mult), [dtda, Ex], [wsm])
        S.op("dve", lambda e: e.tensor_tensor(out=rhsb[:].rearrange("p (h q) -> p h q", h=16),
                                               in0=xs[:].rearrange("p (h q) -> p h q", h=16),
                                               in1=wsm[:, 0:16].unsqueeze(2).to_broadcast([128, 16, 64]), op=ALU.mult),
             [xs, wsm], [rhsb])
        if store_d is not None:
            S.dma("sp", store_d[gc], Sd16[:], reads=[Sd16], writes=["sb_d"])
        pa, pb2 = pb(), pb()
        mm(pa[:, 0:512], btm[:, 0:128], rhsb[:, 0:512], True, True, [btm, rhsb], [pa])
        mm(pb2[:, 0:512], btm[:, 128:256], rhsb[:, 512:1024], True, True, [btm, rhsb], [pb2])
        S.op("pool", lambda e: e.tensor_tensor(out=Sd[:].rearrange("p (h q) -> p h q", h=16),
                                                in0=Sd[:].rearrange("p (h q) -> p h q", h=16),
                                                in1=Ex[:, 16:32].unsqueeze(2).to_broadcast([128, 16, 64]), op=ALU.mult),
             [Sd, Ex], [Sd])
        S.op("dve", lambda e: e.tensor_tensor(out=Sd[:, 0:512], in0=Sd[:, 0:512], in1=pa[:, 0:512], op=ALU.add),
             [Sd, pa], [Sd])
        S.op("dve", lambda e: e.tensor_tensor(out=Sd[:, 512:1024], in0=Sd[:, 512:1024], in1=pb2[:, 0:512], op=ALU.add),
             [Sd, pb2], [Sd])
        S.op("act", lambda e: e.copy(out=Sd16[:], in_=Sd[:]), [Sd], [Sd16])

    WA = lambda kc, c0, c1: W[:, kc * NA + c0: kc * NA + c1]

    def passA(l):
        if not pending.pop(("A", l), False):
            take(pieces_A(l, TA), 999)
        pf = pieces_Q(l, TA)
        pending[("Q", l)] = True
        S.op("pool", lambda e: e.memset(TA["S"][:], 0.0), [], [TA["S"]])
        S.op("pool", lambda e: e.memset(TA["S16"][:], 0.0), [], [TA["S16"]])
        order = [0] + [2 + 2 * j for j in range(15, -1, -1)]
        S.begin("pA")
        for oi, gc0 in enumerate(order):
            S.itn = oi
            take(pf, 2)
            passA_super(oi, gc0)
        take(pf, 999)
        S.end()
        S.emit_scheduled("pA", overlap=OVL_C)

    def passA_super(oi, gc0):
        if True:
            sp_ = str(oi % 2)
            (hTw, xbcT, cv, cv2, cvc, ropeT, ktile) = (TA[n][sp_] for n in ("hTw", "xbcT", "cv", "cv2", "cvc", "ropeT", "ktile"))
            is_ctx = gc0 < 2
            for c2 in range(2):
                S.dma("sp", hTw[:, :, 1 + c2 * 128: 1 + (c2 + 1) * 128], hT_d[gc0 + c2], reads=["hT_d"], writes=[hTw])
            if is_ctx or gc0 == 2:
                S.op("pool", lambda e: e.memset(hTw[:, :, 0:1], 0.0), [], [hTw])
            else:
                S.dma("sp", hTw[:, :, 0:1], hT_d[gc0 - 1, :, :, 127:128], reads=["hT_d"], writes=[hTw],
                      allow_slow_non_contiguous=True)
            if is_ctx or gc0 == NCH - 2:
                S.op("pool", lambda e: e.memset(hTw[:, :, 257:258], 0.0), [], [hTw])
            else:
                S.dma("sp", hTw[:, :, 257:258], hT_d[gc0 + 2, :, :, 0:1], reads=["hT_d"], writes=[hTw],
                      allow_slow_non_contiguous=True)
            for m in range(12):
                p = pb()
                for kc in range(8):
                    mm(p[:, 0:258], WA(kc, A_XBC + m * 128, A_XBC + (m + 1) * 128), hTw[:, kc, :], kc == 0, kc == 7,
                       [W_A, hTw], [p])
                S.op("dve", lambda e, p=p, m=m: e.tensor_scalar_mul(out=cv[:, 0:256], in0=p[:, 0:256],
                                                                     scalar1=convw[:, m, 0:1]), [p, convw], [cv])
                S.op("dve", lambda e, p=p, m=m: e.scalar_tensor_tensor(out=cv[:, 0:256], in0=p[:, 1:257],
                                                                        scalar=convw[:, m, 1:2], in1=cv[:, 0:256],
                                                                        op0=ALU.mult, op1=ALU.add), [p, convw, cv], [cv])
                S.op("dve", lambda e, p=p, m=m: e.scalar_tensor_tensor(out=cv[:, 0:256], in0=p[:, 2:258],
                                                                        scalar=convw[:, m, 2:3], in1=cv[:, 0:256],
                                                                        op0=ALU.mult, op1=ALU.add), [p, convw, cv], [cv])
                S.op("act", lambda e, m=m: e.activation(out=xbcT[:, m, :], in_=cv[:, 0:256], func=AF.Silu,
                                                         bias=convb[:, m:m + 1], scale=1.0), [cv, convb], [xbcT])
            for m in range(4):
                pv, pc = pb(), pb()
                for kc in range(8):
                    mm(pv[:, 0:258], WA(kc, A_SCV + m * 128, A_SCV + (m + 1) * 128), hTw[:, kc, :], kc == 0, kc == 7,
                       [W_A, hTw], [pv])
                for kc in range(8):
                    mm(pc[:, 0:258], WA(kc, A_SCC + m * 128, A_SCC + (m + 1) * 128), hTw[:, kc, :], kc == 0, kc == 7,
                       [W_A, hTw], [pc])
                S.op("act", lambda e, pv=pv: e.copy(out=cv2[:], in_=pv[:, 0:258]), [pv], [cv2])
                S.op("dve", lambda e, pc=pc: e.tensor_tensor(out=cv2[:], in0=pc[:, 0:258], in1=cv2[:], op=ALU.mult),
                     [pc, cv2], [cv2])
                S.op("dve", lambda e, m=m: e.tensor_scalar_mul(out=cvc[:, m, :], in0=cv2[:, 0:256],
                                                                 scalar1=scw[:, m, 0:1]), [cv2, scw], [cvc])
                S.op("dve", lambda e, m=m: e.scalar_tensor_tensor(out=cvc[:, m, :], in0=cv2[:, 1:257],
                                                                    scalar=scw[:, m, 1:2], in1=cvc[:, m, :],
                                                                    op0=ALU.mult, op1=ALU.add), [cv2, scw, cvc], [cvc])
                S.op("dve", lambda e, m=m: e.scalar_tensor_tensor(out=cvc[:, m, :], in0=cv2[:, 2:258],
                                                                    scalar=scw[:, m, 2:3], in1=cvc[:, m, :],
                                                                    op0=ALU.mult, op1=ALU.add), [cv2, scw, cvc], [cvc])
            for c2 in range(2):
                S.dma("sp", cvc_d[gc0 + c2], cvc[:, :, c2 * 128:(c2 + 1) * 128], reads=[cvc], writes=["cvc_d"])
            pk, pks = pb(), pb()
            for kc in range(8):
                mm(pk[:, 0:258], WA(kc, A_K, A_K + 128), hTw[:, kc, :], kc == 0, kc == 7, [W_A, hTw], [pk])
            if is_ctx:
                S.op("act", lambda e: e.copy(out=ktile[:], in_=pk[:, 1:257]), [pk], [ktile])
            else:
                for kc in range(8):
                    mm(pks[:, 0:258], WA(kc, A_KSW, A_KSW + 128), hTw[:, kc, :], kc == 0, kc == 7, [W_A, hTw], [pks])
                t0 = (gc0 - 2) * 128
                S.dma("sp", ropeT[:], rope_in[:, :, t0:t0 + 256], writes=[ropeT])
                S.op("dve", lambda e: e.tensor_tensor(out=cv[:, 0:256], in0=pk[:, 1:257], in1=ropeT[:, 0, :], op=ALU.mult),
                     [pk, ropeT], [cv])
                S.op("dve", lambda e: e.tensor_tensor(out=cv2[:, 0:256], in0=pks[:, 1:257], in1=ropeT[:, 1, :], op=ALU.mult),
                     [pks, ropeT], [cv2])
                S.op("pool", lambda e: e.tensor_tensor(out=ktile[:], in0=cv[:, 0:256], in1=cv2[:, 0:256], op=ALU.add),
                     [cv, cv2], [ktile])
            S.dma("sp", kt_d[:, gc0 * 128:(gc0 + 2) * 128], ktile[:], reads=[ktile], writes=["kt_d"])
            for c2 in (1, 0):
                passA_chunk(gc0, c2, hTw, xbcT)

    def passA_chunk(gc0, c2, hTw, xbcT):
        if True:
            if True:
                gc = gc0 + c2
                cp_ = str(gc % 2)
                xs, btm, dtda, dtr, vtm = (TA[n][cp_] for n in ("xs", "btm", "dtda", "dtr", "vtm"))
                TS = dict(TA)
                TS.update({"xs": xs, "btm": btm, "dtda": dtda})
                tsl = slice(1 + c2 * 128, 1 + (c2 + 1) * 128)
                csl = slice(c2 * 128, (c2 + 1) * 128)
                p = pb()
                for kc in range(8):
                    mm(p[:, 0:128], hTw[:, kc, tsl], WA(kc, A_V, A_V + 128), kc == 0, kc == 7, [W_A, hTw], [p])
                S.op("act", lambda e, p=p: e.copy(out=vtm[:], in_=p[:, 0:128]), [p], [vtm])
                S.dma("sp", v_d[gc], vtm[:], reads=[vtm], writes=["v_d"])
                p = pb()
                for kc in range(8):
                    mm(p[:, 0:32], hTw[:, kc, tsl], WA(kc, A_DT, A_DT + 32), kc == 0, kc == 7, [W_A, hTw], [p])
                S.op("dve", lambda e, p=p: e.tensor_tensor(out=dtr[:], in0=p[:, 0:32], in1=small[:, 0:32], op=ALU.add),
                     [p, small], [dtr])
                S.op("act", lambda e: e.activation(out=dtr[:], in_=dtr[:], func=AF.Exp), [dtr], [dtr])
                S.op("act", lambda e: e.activation(out=dtda[:, 0:32], in_=dtr[:], func=AF.Ln, bias=1.0, scale=1.0),
                     [dtr], [dtda])
                S.op("dve", lambda e: e.tensor_tensor(out=dtda[:, 32:64], in0=dtda[:, 0:32], in1=small[:, 32:64],
                                                       op=ALU.mult), [dtda, small], [dtda])
                S.dma("sp", dtda_d[gc], dtda[:], reads=[dtda], writes=["dtda_d"])
                for m in range(8):
                    tr(psT[:, m * 128:(m + 1) * 128], xbcT[:, m, csl], [xbcT], [psT])
                S.op("act", lambda e: e.copy(out=xs[:], in_=psT[:]), [psT], [xs])
                S.dma("sp", xs_d[gc], xs[:], reads=[xs], writes=["xs_d"])
                for m in range(2):
                    tr(psT[:, m * 128:(m + 1) * 128], xbcT[:, 8 + m, csl], [xbcT], [psT])
                S.op("dve", lambda e: e.tensor_copy(out=btm[:], in_=psT[:, 0:256]), [psT], [btm])
                S.dma("sp", btm_d[gc], btm[:], reads=[btm], writes=["btm_d"])
                S.dma("sp", bt_d[gc], xbcT[:, 8:10, csl], reads=[xbcT], writes=["bt_d"])
                S.dma("sp", ct_d[gc], xbcT[:, 10:12, csl], reads=[xbcT], writes=["ct_d"])
                state_update(TS, "SL", (48, 64), (16, 32), gc, store_d=sb_d)

    WQ = lambda kc, c0, c1: W[:, QOFF + kc * NQ + c0: QOFF + kc * NQ + c1]

    def passQ(l):
        if not pending.pop(("Q", l), False):
            take(pieces_Q(l, TQ), 999)
        pf = pieces_C(l, TQ)
        pending[("C", l)] = True
        sct, qtmp = TQ["sct"], TQ["qtmp"]
        quads = [(0, 2)] + [(2 + 4 * k, 4) for k in range(8)]
        S.begin("pQ")
        for qi, (gc0, nch) in enumerate(quads):
            S.itn = qi
            take(pf, 1)
            passQ_quad(qi, gc0, nch, sct, qtmp)
        take(pf, 999)
        S.end()
        S.emit_scheduled("pQ", overlap=OVL_C)

    def passQ_quad(qi, gc0, nch, sct, qtmp):
        p_ = str(qi % 2)
        hTq, cvq, ropq, scy, QTq = (TQ[n][p_] for n in ("hTq", "cvq", "ropq", "scy", "QTq"))
        N = nch * 128
        is_ctx = gc0 < 2
        for ci in range(nch):
            S.dma("sp", hTq[:, :, ci * 128:(ci + 1) * 128], hT_d[gc0 + ci], reads=["hT_d"], writes=[hTq])
            S.dma("sp", cvq[:, :, ci * 128:(ci + 1) * 128], cvc_d[gc0 + ci], reads=["cvc_d"], writes=[cvq])
        if not is_ctx:
            t0 = (gc0 - 2) * 128
            S.dma("sp", ropq[:, :, 0:N], rope_in[:, :, t0:t0 + N], writes=[ropq])
        for m in range(4):
            pB, pZ = pb(), pb()
            for kc in range(8):
                mm(pB[:, 0:N], WQ(kc, Q_SCB + m * 128, Q_SCB + (m + 1) * 128), hTq[:, kc, 0:N], kc == 0, kc == 7, [W_Q, hTq], [pB])
            for kc in range(8):
                mm(pZ[:, 0:N], WQ(kc, Q_SCZ + m * 128, Q_SCZ + (m + 1) * 128), hTq[:, kc, 0:N], kc == 0, kc == 7, [W_Q, hTq], [pZ])
            S.op("act", lambda e, pZ=pZ: e.activation(out=sct[:, 0:N], in_=pZ[:, 0:N], func=AF.Silu), [pZ], [sct])
            S.op("dve", lambda e, pB=pB, m=m: e.tensor_tensor(out=cvq[:, m, 0:N], in0=pB[:, 0:N], in1=cvq[:, m, 0:N], op=ALU.mult),
                 [pB, cvq], [cvq])
            S.op("pool", lambda e, m=m: e.tensor_tensor(out=scy[:, m, 0:N], in0=cvq[:, m, 0:N], in1=sct[:, 0:N], op=ALU.mult),
                 [cvq, sct], [scy])
        for ci in range(nch):
            S.dma("sp", yc_d[gc0 + ci, :, 8:12, :], scy[:, :, ci * 128:(ci + 1) * 128], reads=[scy], writes=["yc_d"])
        for j in range(4):
            pq = pb()
            for kc in range(8):
                mm(pq[:, 0:N], WQ(kc, Q_Q + j * 128, Q_Q + (j + 1) * 128), hTq[:, kc, 0:N], kc == 0, kc == 7, [W_Q, hTq], [pq])
            if is_ctx:
                S.op("act", lambda e, pq=pq, j=j: e.copy(out=QTq[:, j, 0:N], in_=pq[:, 0:N]), [pq], [QTq])
            else:
                pqs = pb()
                for kc in range(8):
                    mm(pqs[:, 0:N], WQ(kc, Q_QSW + j * 128, Q_QSW + (j + 1) * 128), hTq[:, kc, 0:N], kc == 0, kc == 7,
                       [W_Q, hTq], [pqs])
                S.op("dve", lambda e, pq=pq: e.tensor_tensor(out=qtmp[:, 0:N], in0=pq[:, 0:N], in1=ropq[:, 0, 0:N], op=ALU.mult),
                     [pq, ropq], [qtmp])
                S.op("dve", lambda e, pqs=pqs: e.tensor_tensor(out=qtmp[:, 512:512 + N], in0=pqs[:, 0:N], in1=ropq[:, 1, 0:N],
                                                               op=ALU.mult), [pqs, ropq], [qtmp])
                S.op("pool", lambda e, j=j: e.tensor_tensor(out=QTq[:, j, 0:N], in0=qtmp[:, 0:N], in1=qtmp[:, 512:512 + N],
                                                            op=ALU.add), [qtmp], [QTq])
        for ci in range(nch):
            S.dma("sp", qt_d[gc0 + ci], QTq[:, :, ci * 128:(ci + 1) * 128], reads=[QTq], writes=["qt_d"])
        for g in range(2):
            zaq = TQ["zaq"][str(g)]
            for j in range(4):
                hq = g * 4 + j
                pza = pb()
                for kc in range(8):
                    mm(pza[0:64, 0:N], WQ(kc, Q_ZA + hq * 64, Q_ZA + (hq + 1) * 64), hTq[:, kc, 0:N], kc == 0, kc == 7,
                       [W_Q, hTq], [pza])
                S.op("act", lambda e, pza=pza, j=j, zaq=zaq: e.activation(out=zaq[:, j, 0:N], in_=pza[0:64, 0:N], func=AF.Silu),
                     [pza], [zaq])
            for ci in range(nch):
                S.dma("sp", za_d[gc0 + ci, :, g * 4:(g + 1) * 4, :], zaq[:, :, ci * 128:(ci + 1) * 128], reads=[zaq],
                      writes=["za_d"])

    WC = lambda kc, c0, c1: W[:, kc * NCC + c0: kc * NCC + c1]
    WO0 = W3OFF
    WOA = WO0 + 12 * 1024

    def passC(l):
        last = (l == DEPTH - 1)
        rb = [TC["rb0"], TC["rb1"]]
        Lm = [TC["Lm0"], TC["Lm1"]]
        (cbm, hilo, dres, S16f) = (TC[n] for n in ("cbm", "hilo", "dres", "S16"))
        if not pending.pop(("C", l), False):
            take(pieces_C(l, TC), 999)
        pf = pieces_3(l, TC)
        pending[("3", l)] = True
        S.op("pool", lambda e: e.memset(TC["S"][:], 0.0), [], [TC["S"]])
        S.op("pool", lambda e: e.memset(TC["S16"][:], 0.0), [], [TC["S16"]])

        def tiles(gc):
            p = str(gc % NBUF_C)
            return {n: TC[n][p] for n in ("xs", "btm", "dtda", "hT", "btct", "sbin", "KTb", "Vb", "sz", "Ex",
                                          "M_f", "M_b", "xdt_f", "xdt_b", "QT", "szA")}

        def kblocks(gc):
            if gc < 2:
                return [(0, None), (1, None)]
            n = gc - 2
            kbs = []
            if n > 0:
                kbs.append((gc - 1, "LI"))
            kbs.append((gc, None))
            if n < 31:
                kbs.append((gc + 1, "UI"))
            return kbs + [(0, None), (1, None)]

        def stageF(gc):
            t = tiles(gc)
            xs, btm, dtda, hT, btct, sbin, KTb, Vb, sz, Ex, QT, szA = (t[n] for n in (
                "xs", "btm", "dtda", "hT", "btct", "sbin", "KTb", "Vb", "sz", "Ex", "QT", "szA"))
            Mm = {"f": t["M_f"], "b": t["M_b"]}
            xdt = {"f": t["xdt_f"], "b": t["xdt_b"]}
            is_ctx = gc < 2
            S.dma("sp", xs[:], xs_d[gc], reads=["xs_d"], writes=[xs])
            S.dma("sp", btm[:], btm_d[gc], reads=["btm_d"], writes=[btm])
            S.dma("sp", dtda[:], dtda_d[gc], reads=["dtda_d"], writes=[dtda])
            if is_ctx and last:
                return
            S.dma("sp", hT[:], hT_d[gc], reads=["hT_d"], writes=[hT])
            S.dma("sp", btct[:, 0:2, :], bt_d[gc], reads=["bt_d"], writes=[btct])
            S.dma("sp", btct[:, 2:4, :], ct_d[gc], reads=["ct_d"], writes=[btct])
            S.dma("sp", sbin[:], sb_d[gc], reads=["sb_d"], writes=[sbin])
            S.dma("sp", QT[:], qt_d[gc], reads=["qt_d"], writes=[QT])
            S.dma("sp", szA[:].rearrange("p g j t -> p (g j) t"), za_d[gc], reads=["za_d"], writes=[szA])
            kbs = kblocks(gc)
            for i, (kg, _) in enumerate(kbs):
                S.dma("sp", KTb[:, i, :], kt_d[:, kg * 128:(kg + 1) * 128], reads=["kt_d"], writes=[KTb])
                S.dma("sp", Vb[:, i, :], v_d[kg], reads=["v_d"], writes=[Vb])
            S.op("dve", lambda e: e.tensor_copy(out=hilo[:, 0:32], in_=dtda[:, 32:64]), [dtda], [hilo])
            S.op("dve", lambda e: e.tensor_tensor(out=dres[:], in0=dtda[:, 32:64], in1=hilo[:, 0:32], op=ALU.subtract),
                 [dtda, hilo], [dres])
            S.op("dve", lambda e: e.tensor_copy(out=hilo[:, 32:64], in_=dres[:]), [dres], [hilo])
            pz = [pb(), pb()]
            for h2 in range(2):
                for kc in range(8):
                    mm(pz[h2][:, 0:512], hT[:, kc, :], WC(kc, C_Z + h2 * 512, C_Z + (h2 + 1) * 512), kc == 0, kc == 7,
                       [W_C, hT], [pz[h2]])
                S.op("act", lambda e, h2=h2: e.activation(out=sz[:, h2 * 512:(h2 + 1) * 512], in_=pz[h2][:, 0:512],
                                                           func=AF.Silu), [pz[h2]], [sz])
            p = pb()
            mm(p[:, 32:48], cm_f["UI"][:], dtda[:, 32:48], True, True, [cm_f["UI"], dtda], [p])
            mm(p[:, 48:64], cm_f["LI"][:], dtda[:, 48:64], True, True, [cm_f["LI"], dtda], [p])
            S.op("act", lambda e: e.activation(out=Ex[:, 32:64], in_=p[:, 32:64], func=AF.Exp), [p], [Ex])
            pcb = pb()
            for g in range(2):
                mm(pcb[:, g * 128:(g + 1) * 128], btct[:, g, :], btct[:, 2 + g, :], True, True, [btct], [pcb])
            for di, mk in enumerate(("UI", "LI")):
                S.op("dve", lambda e, di=di, mk=mk: e.tensor_tensor(
                    out=cbm[:, di, :, :], in0=pcb[:, 0:256].rearrange("p (g l) -> p g l", g=2),
                    in1=cm_f[mk][:].unsqueeze(1).to_broadcast([128, 2, 128]), op=ALU.mult), [pcb, cm_f[mk]], [cbm])
            li = 0
            for di, (dk, lk, mk) in enumerate((("f", "SU", "UI"), ("b", "SL", "LI"))):
                for part in range(2):
                    eng = "pool" if part == 0 else "dve"
                    S.op(eng, lambda e, part=part, di=di, mk=mk: e.tensor_tensor(
                        out=rb[part][:],
                        in0=hilo[:, part * 32 + di * 16: part * 32 + di * 16 + 16].unsqueeze(2).to_broadcast([128, 16, 128]),
                        in1=cm_b[mk][:].unsqueeze(1).to_broadcast([128, 16, 128]), op=ALU.mult),
                        [hilo, cm_b[mk]], [rb[part]])
                for q4 in range(4):
                    pD = pb()
                    for part in range(2):
                        mm(pD[:, 0:512], cm_b[lk][:], rb[part][:, q4 * 4:(q4 + 1) * 4, :].rearrange("p h l -> p (h l)"),
                           part == 0, part == 1, [cm_b[lk], rb[part]], [pD])
                    Lq = Lm[li % 2]
                    li += 1
                    S.op("act", lambda e, pD=pD, Lq=Lq: e.activation(out=Lq[:].rearrange("p h l -> p (h l)"),
                                                                      in_=pD[:, 0:512], func=AF.Exp), [pD], [Lq])
                    g = q4 // 2
                    eng = "pool"
                    S.op(eng, lambda e, dk=dk, di=di, q4=q4, g=g, Lq=Lq: e.tensor_tensor(
                        out=Mm[dk][:, q4 * 4:(q4 + 1) * 4, :], in0=Lq[:],
                        in1=cbm[:, di, g, :].unsqueeze(1).to_broadcast([128, 4, 128]), op=ALU.mult),
                        [Lq, cbm], [Mm[dk]])
                S.op("dve", lambda e, dk=dk, di=di: e.tensor_tensor(
                    out=xdt[dk][:].rearrange("p (h q) -> p h q", h=16), in0=xs[:].rearrange("p (h q) -> p h q", h=16),
                    in1=dtda[:, di * 16:(di + 1) * 16].unsqueeze(2).to_broadcast([128, 16, 64]), op=ALU.mult),
                    [xs, dtda], [xdt[dk]])

        def stageB(gc):
            t = tiles(gc)
            xs, btm, dtda, btct, sbin, KTb, Vb, sz, Ex, QT, szA = (t[n] for n in (
                "xs", "btm", "dtda", "btct", "sbin", "KTb", "Vb", "sz", "Ex", "QT", "szA"))
            Mm = {"f": t["M_f"], "b": t["M_b"]}
            xdt = {"f": t["xdt_f"], "b": t["xdt_b"]}
            bp = str(gc % 2)
            tmpA, tmpB, ysb, ysT, OG, rden, st8 = (TC[n][bp] for n in ("tmpA", "tmpB", "ysb", "ysT", "OG", "rden", "stB"))
            PT = [TC["PT0"][bp], TC["PT1"][bp]]
            TS = dict(TC)
            TS.update({"xs": xs, "btm": btm, "dtda": dtda, "rhsb": TC["rhsb"][bp], "wsm": TC["wsm"][bp], "ExS": TC["ExS"][bp]})
            is_ctx = gc < 2
            if is_ctx and last:
                state_update(TS, "SU", (32, 48), (0, 16), gc)
                return
            kbs = kblocks(gc)
            for g in range(2):
                gs = slice(g * 64, (g + 1) * 64)
                po, pd = pb(), pb()
                pscs = [pb(), pb()]
                nk = len(kbs)
                for i, (kg, mk) in enumerate(kbs):
                    psc = pscs[i % 2]
                    mm(psc[:, 0:512], KTb[gs, i, :], QT[gs, :, :].rearrange("p j t -> p (j t)"), True, mk is None, [KTb, QT], [psc])
                    if mk is not None:
                        mm(psc[:, 0:512], ident_b[:], negm[mk][:].rearrange("p j t -> p (j t)"), False, True,
                           [ident_b, negm[mk]], [psc])
                    pt = PT[i % 2]
                    S.op("act", lambda e, psc=psc, pt=pt: e.activation(out=pt[:], in_=psc[:, 0:512], func=AF.Exp, scale=0.125),
                         [psc], [pt])
                    mm(po[0:64, 0:512], Vb[:, i, gs], pt[:], i == 0, i == nk - 1, [Vb, pt], [po])
                    mm(pd[0:64, 0:512], cm_b["ones"][:, 0:64], pt[:], i == 0, i == nk - 1, [cm_b["ones"], pt], [pd])
                S.op("dve", lambda e, g=g, pd=pd: e.tensor_tensor(
                    out=rden[:], in0=pd[0:64, 0:512].rearrange("p (j t) -> p j t", j=4),
                    in1=small[0:64, 80 + g * 4: 84 + g * 4].unsqueeze(2).to_broadcast([64, 4, 128]), op=ALU.add),
                    [pd, small], [rden])
                S.op("dve", lambda e: e.reciprocal(out=rden[:], in_=rden[:]), [rden], [rden])
                S.op("dve", lambda e, po=po: e.tensor_tensor(out=rden[:].rearrange("p j t -> p (j t)"), in0=po[0:64, 0:512],
                                                              in1=rden[:].rearrange("p j t -> p (j t)"), op=ALU.mult),
                     [po, rden], [rden])
                S.op("pool", lambda e, g=g: e.tensor_tensor(out=OG[:, g, :, :], in0=rden[:], in1=szA[:, g, :, :], op=ALU.mult),
                     [rden, szA], [OG])
            S.dma("sp", og_d[gc], OG[:].rearrange("p g j t -> p (g j) t"), reads=[OG], writes=["og_d"])
            py = [pb(), pb()]
            for h in range(16):
                o = py[h // 8][:, (h % 8) * 64:(h % 8 + 1) * 64]
                mm(o, Mm["f"][:, h, :], xdt["f"][:, h * 64:(h + 1) * 64], True, False, [Mm["f"], xdt["f"]], [py[h // 8]])
                mm(o, Mm["b"][:, h, :], xdt["b"][:, h * 64:(h + 1) * 64], False, True, [Mm["b"], xdt["b"]], [py[h // 8]])
            pof = [pb(), pb()]
            for g in range(2):
                mm(pof[g][:, 0:512], btct[:, 2 + g, :], S16f[:, g * 512:(g + 1) * 512], True, True, [btct, S16f], [pof[g]])
            for g in range(2):
                S.op("dve", lambda e, g=g: e.tensor_tensor(
                    out=tmpA[:, g * 512:(g + 1) * 512].rearrange("p (h q) -> p h q", h=8),
                    in0=pof[g][:, 0:512].rearrange("p (h q) -> p h q", h=8),
                    in1=Ex[:, 32 + g * 8: 32 + (g + 1) * 8].unsqueeze(2).to_broadcast([128, 8, 64]), op=ALU.mult),
                    [pof[g], Ex], [tmpA])
                S.op("dve", lambda e, g=g: e.tensor_tensor(out=tmpA[:, g * 512:(g + 1) * 512], in0=tmpA[:, g * 512:(g + 1) * 512],
                                                            in1=py[g][:, 0:512], op=ALU.add), [tmpA, py[g]], [tmpA])
            pob = [pb(), pb()]
            for g in range(2):
                mm(pob[g][:, 0:512], btct[:, 2 + g, :], sbin[:, g * 512:(g + 1) * 512], True, True, [btct, sbin], [pob[g]])
            for g in range(2):
                S.op("dve", lambda e, g=g: e.tensor_tensor(
                    out=tmpB[:, g * 512:(g + 1) * 512].rearrange("p (h q) -> p h q", h=8),
                    in0=pob[g][:, 0:512].rearrange("p (h q) -> p h q", h=8),
                    in1=Ex[:, 48 + g * 8: 48 + (g + 1) * 8].unsqueeze(2).to_broadcast([128, 8, 64]), op=ALU.mult),
                    [pob[g], Ex], [tmpB])
            S.op("pool", lambda e: e.tensor_tensor(out=tmpA[:], in0=tmpA[:], in1=tmpB[:], op=ALU.add), [tmpA, tmpB], [tmpA])
            S.op("pool", lambda e: e.tensor_tensor(
                out=tmpB[:].rearrange("p (h q) -> p h q", h=16), in0=xs[:].rearrange("p (h q) -> p h q", h=16),
                in1=small[:, 64:80].unsqueeze(2).to_broadcast([128, 16, 64]), op=ALU.mult), [xs, small], [tmpB])
            S.op("pool", lambda e: e.tensor_tensor(out=tmpA[:], in0=tmpA[:], in1=tmpB[:], op=ALU.add), [tmpA, tmpB], [tmpA])
            S.op("dve", lambda e: e.tensor_tensor(out=tmpA[:], in0=tmpA[:], in1=sz[:], op=ALU.mult), [tmpA, sz], [tmpA])
            S.op("act", lambda e: e.activation(out=tmpB[:], in_=tmpA[:], func=AF.Square), [tmpA], [tmpB])
            S.op("dve", lambda e: e.reduce_sum(out=st8[:, 2:4], in_=tmpB[:].rearrange("p (g q) -> p g q", g=2), axis=AX.X),
                 [tmpB], [st8])
            S.op("dve", lambda e: e.tensor_scalar(out=st8[:, 2:4], in0=st8[:, 2:4], scalar1=1.0 / 512, scalar2=EPS,
                                                   op0=ALU.mult, op1=ALU.add), [st8], [st8])
            S.op("act", lambda e: e.activation(out=st8[:, 2:4], in_=st8[:, 2:4], func=AF.Sqrt), [st8], [st8])
            S.op("dve", lambda e: e.reciprocal(out=st8[:, 2:4], in_=st8[:, 2:4]), [st8], [st8])
            for g in range(2):
                S.op("dve", lambda e, g=g: e.scalar_tensor_tensor(
                    out=ysb[:, g * 512:(g + 1) * 512], in0=tmpA[:, g * 512:(g + 1) * 512], scalar=st8[:, 2 + g:3 + g],
                    in1=snw_b[:, g * 512:(g + 1) * 512], op0=ALU.mult, op1=ALU.mult), [tmpA, st8, snw_b], [ysb])
            for m in range(8):
                tr(psT[:, m * 128:(m + 1) * 128], ysb[:, m * 128:(m + 1) * 128], [ysb], [psT])
            S.op("act", lambda e: e.copy(out=ysT[:].rearrange("p k t -> p (k t)"), in_=psT[:]), [psT], [ysT])
            S.dma("sp", yc_d[gc, :, 0:8, :], ysT[:], reads=[ysT], writes=["yc_d"])
            state_update(TS, "SU", (32, 48), (0, 16), gc)

        S.begin("passC")
        for gc in range(NCH):
            S.itn = gc
            take(pf, 1)
            bsel[0] = (0, 3)
            stageF(gc)
            bsel[0] = (3, 7)
            stageB(gc)
        bsel[0] = None
        take(pf, 999)
        S.end()
        S.emit_scheduled("passC", overlap=OVL_C)

    def passC3(l):
        last = (l == DEPTH - 1)
        if not pending.pop(("3", l), False):
            take(pieces_3(l, T3), 999)
        pf = pieces_A(l + 1, T3) if l + 1 < DEPTH else []
        if pf:
            pending[("A", l + 1)] = True
        S.begin("p3")
        for gc in range(NCH):
            S.itn = gc
            take(pf, 1)
            is_ctx = gc < 2
            if is_ctx and last:
                continue
            p = str(gc % 2)
            yc, og, xin, xnew, tmpB, stt = (T3[n][p] for n in ("yc", "og", "xin", "xnew", "tmpB", "st"))
            if is_ctx:
                src = (ctx_in if l == 0 else ctx1_d)[gc * 128:(gc + 1) * 128, :]
            else:
                src = (x_in if l == 0 else x1_d)[(gc - 2) * 128:(gc - 1) * 128, :]
            S.dma("sp", yc[:], yc_d[gc], reads=["yc_d"], writes=[yc])
            ogv = og_d[gc].rearrange("d (t two) k -> two d t k", two=2)
            S.dma("sp", og[0:64, :, :], ogv[0], reads=["og_d"], writes=[og])
            S.dma("sp", og[64:128, :, :], ogv[1], reads=["og_d"], writes=[og])
            S.dma("sp", xin[:], src, reads=["x1_d", "ctx1_d"], writes=[xin])
            pout = [pb(), pb()]
            g_t = aux if is_ctx else g_l
            for h2 in range(2):
                ns = slice(h2 * 512, (h2 + 1) * 512)
                steps = []
                for t in range(12):
                    steps.append((yc[:, t, :], W[:, WO0 + t * 1024 + h2 * 512: WO0 + t * 1024 + (h2 + 1) * 512], [yc, W_3]))
                for t in range(4):
                    steps.append((og[:, t, :], W[:, WO0 + (12 + t) * 1024 + h2 * 512: WO0 + (12 + t) * 1024 + (h2 + 1) * 512], [og, W_3]))
                for i, (lt, rh, rd) in enumerate(steps):
                    mm(pout[h2][:, 0:512], lt, rh, i == 0, i == len(steps) - 1, rd, [pout[h2]])
                S.op("dve", lambda e, h2=h2, ns=ns, g_t=g_t, xnew=xnew, pout=pout: e.tensor_tensor(
                    out=xnew[:, ns], in0=pout[h2][:, 0:512], in1=g_t[:, ns], op=ALU.mult), [pout[h2], g_t], [xnew])
            S.op("pool", lambda e, xnew=xnew, xin=xin: e.tensor_tensor(out=xnew[:], in0=xnew[:], in1=xin[:], op=ALU.add),
                 [xnew, xin], [xnew])
            if not last:
                if is_ctx:
                    S.dma("sp", ctx1_d[gc * 128:(gc + 1) * 128, :], xnew[:], reads=[xnew], writes=["ctx1_d"])
                else:
                    S.dma("sp", x1_d[(gc - 2) * 128:(gc - 1) * 128, :], xnew[:], reads=[xnew], writes=["x1_d"])
            else:
                rms_rstd(xnew, tmpB, D, st8=stt)
                S.op("dve", lambda e, xnew=xnew, tmpB=tmpB, stt=stt: e.scalar_tensor_tensor(
                    out=tmpB[:], in0=xnew[:], scalar=stt[:, 0:1], in1=aux[:], op0=ALU.mult, op1=ALU.mult),
                    [xnew, stt, aux], [tmpB])
                S.dma("sp", out_d[(gc - 2) * 128:(gc - 1) * 128, :], tmpB[:], reads=[tmpB], writes=["out_d"])
        take(pf, 999)
        S.end()
        S.emit_scheduled("p3", overlap=OVL_C)

    build_consts()
    for l in range(n_layers):
        S.barrier()
        layer_setup(l)
        if stop_after == (l, "setup"):
            break
        pass0(l)
        if stop_after == (l, "p0"):
            break
        S.barrier()
        passA(l)
        if stop_after == (l, "pA"):
            break
        S.barrier()
        passQ(l)
        S.barrier()
        passC(l)
        S.barrier()
        passC3(l)
    S.wait_all("sp")
    build_program.stats = (S.ninst, S.nwaits, getattr(S, "sim_time", 0.0))
    build_program.sim_log = getattr(S, "sim_log", [])
    return nc


def _rope_table():
    f32 = np.float32
    tok = np.arange(T_LAT)
    rows = (tok // 64).astype(f32)
    cols = (tok % 64).astype(f32)
    inv = (f32(10000.0) ** (-(np.arange(0, 32, 2).astype(f32)) / f32(32))).astype(f32)
    ang = np.concatenate([rows[:, None] * inv[None, :], cols[:, None] * inv[None, :]], axis=-1).astype(f32)
    cos, sin = np.cos(ang).astype(f32), np.sin(ang).astype(f32)
    tab = np.zeros((128, 2, T_LAT), f32)
    for p in range(128):
        d = p % 64
        a, r = divmod(d, 32)
        j, i = divmod(r, 16)
        tab[p, 0] = cos[:, a * 16 + i]
        tab[p, 1] = sin[:, a * 16 + i] * (-1.0 if j == 0 else 1.0)
    return tab


def _prep_shared(inp):
    f = lambda a: np.ascontiguousarray(np.asarray(a, dtype=np.float32))
    cA, cC, cQ = np.array(_cols_A()), np.array(_cols_C()), np.array(_cols_Q())
    w_in = np.asarray(inp["w_in"], dtype=np.float32)
    w_out = np.asarray(inp["w_out"], dtype=np.float32)
    sh = {
        "norm_w": f(inp["norm_w"]).reshape(DEPTH, 1, D),
        "w_mod": f(inp["w_mod"]),
        "b_mod": f(inp["b_mod"]).reshape(DEPTH, 1, 3 * D),
        "wA": f(w_in[:, :, cA]),
        "wC": f(w_in[:, :, cC]),
        "wQ": f(w_in[:, :, cQ]),
        "wo_all": f(w_out),
        "convw": f(np.asarray(inp["ssd_conv_w"]).reshape(DEPTH, 3, 12, 128).transpose(0, 3, 2, 1)),
        "convb": f(np.asarray(inp["ssd_conv_b"]).reshape(DEPTH, 12, 128).transpose(0, 2, 1)),
        "scw": f(np.asarray(inp["sc_conv_w"]).reshape(DEPTH, 3, 4, 128).transpose(0, 3, 2, 1)),
        "dt_bias": f(inp["ssd_dt_bias"]).reshape(DEPTH, 1, 32),
        "a_log": f(inp["ssd_a_log"]).reshape(DEPTH, 1, 32),
        "ssd_d": f(inp["ssd_d"]).reshape(DEPTH, 1, 16),
        "ssd_norm_w": f(inp["ssd_norm_w"]).reshape(DEPTH, 1, D),
        "sink": f(inp["attn_sink"]).reshape(DEPTH, 1, 8),
        "final_norm_w": f(inp["final_norm_w"]).reshape(1, D),
        "ropecs": _rope_table(),
    }
    return sh


def _in_maps(inp, n_cores=8):
    sh = _prep_shared(inp)
    x = np.asarray(inp["x"], dtype=np.float32)
    c = np.asarray(inp["c"], dtype=np.float32)
    ctx = np.asarray(inp["ctx"], dtype=np.float32)
    c_ctx = np.asarray(inp["c_ctx"], dtype=np.float32)
    maps = []
    for core in range(n_cores):
        b = core % 4
        cc = np.stack([c[b].reshape(8, 128).T, c_ctx.reshape(8, 128).T], axis=-1)
        m = dict(sh)
        m["x"] = np.ascontiguousarray(x[b])
        m["ctx"] = np.ascontiguousarray(ctx[b])
        m["cc"] = np.ascontiguousarray(cc.astype(np.float32))
        maps.append(m)
    return maps


def kernel(x, c, ctx, c_ctx, norm_w, w_mod, b_mod, w_in, ssd_conv_w, ssd_conv_b, ssd_dt_bias, ssd_a_log, ssd_d,
           ssd_norm_w, sc_conv_w, attn_sink, w_out, final_norm_w):
    inp = dict(x=x, c=c, ctx=ctx, c_ctx=c_ctx, norm_w=norm_w, w_mod=w_mod, b_mod=b_mod, w_in=w_in,
               ssd_conv_w=ssd_conv_w, ssd_conv_b=ssd_conv_b, ssd_dt_bias=ssd_dt_bias, ssd_a_log=ssd_a_log,
               ssd_d=ssd_d, ssd_norm_w=ssd_norm_w, sc_conv_w=sc_conv_w, attn_sink=attn_sink, w_out=w_out,
               final_norm_w=final_norm_w)
    nc = build_program()
    maps = _in_maps(inp, 8)
    res = run_bass_kernel_spmd(nc, maps, core_ids=list(range(8)))
    out = np.stack([np.asarray(res.results[b]["out"], dtype=np.float32).reshape(T_LAT, D) for b in range(4)], axis=0)
    return out
```

```python
import numpy as np
import concourse.bass as bass
import concourse.mybir as mybir
from concourse.bass_utils import run_bass_kernel_spmd

F32 = mybir.dt.float32
BF16 = mybir.dt.bfloat16
AF = mybir.ActivationFunctionType
ALU = mybir.AluOpType
AX = mybir.AxisListType

D = 1024
T_LAT = 4096
T_CTX = 256
NCH = 34
T_ALL = NCH * 128
DEPTH = 2
EPS = 1e-6

O_Z = 0
O_XS = 1024
O_B = 2048
O_C = 2304
O_DT = 2560
O_SCV = 2592
O_SCC = 3104
O_SCB = 3616
O_SCZ = 4128
O_Q = 4640
O_K = 5152
O_V = 5280
O_ZA = 5408


def _rope_partner(d):
    a, r = divmod(d, 32)
    j, i = divmod(r, 16)
    return a * 32 + (1 - j) * 16 + i


def _cols_A():
    cols = list(range(O_XS, O_XS + 1536))
    cols += list(range(O_SCV, O_SCV + 512))
    cols += list(range(O_SCC, O_SCC + 512))
    cols += list(range(O_K, O_K + 128))
    cols += [O_K + g * 64 + _rope_partner(d) for g in range(2) for d in range(64)]
    cols += list(range(O_V, O_V + 128))
    cols += list(range(O_DT, O_DT + 32))
    return cols


A_XBC, A_SCV, A_SCC, A_K, A_KSW, A_V, A_DT, NA = 0, 1536, 2048, 2560, 2688, 2816, 2944, 2976


def _cols_C():
    return list(range(O_Z, O_Z + 1024))


def _cols_Q():
    cols = list(range(O_SCB, O_SCB + 512))
    cols += list(range(O_SCZ, O_SCZ + 512))
    qt = []
    qs = []
    for j in range(4):
        for hq in (j, 4 + j):
            qt += [O_Q + hq * 64 + d for d in range(64)]
            qs += [O_Q + hq * 64 + _rope_partner(d) for d in range(64)]
    cols += qt + qs
    cols += list(range(O_ZA, O_ZA + 512))
    return cols


C_Z, NCC = 0, 1024
Q_SCB, Q_SCZ, Q_Q, Q_QSW, Q_ZA, NQ = 0, 512, 1024, 1536, 2048, 2560


class Tl:
    def __init__(self, t, k):
        self.t = t
        self.k = k

    def __getitem__(self, idx):
        return self.t[idx]


class Sched:
    def __init__(self, nc, n_dma_sems=12):
        self.nc = nc
        self.engs = {"pe": nc.tensor, "act": nc.scalar, "dve": nc.vector,
                     "pool": nc.gpsimd, "sp": nc.sync}
        self.sem = {}
        self.cnt = {}
        for e in self.engs:
            self.sem[e] = nc.alloc_semaphore("s_" + e)
            self.cnt[e] = 0
        self.dring = {}
        for q in ("sp", "act", "pool"):
            self.dring[q] = [[nc.alloc_semaphore("d_%s%d" % (q, i)), 0] for i in range(n_dma_sems)]
        self.dpos = {q: 0 for q in self.dring}
        self.seen = {e: {} for e in self.engs}
        self.state = {}
        self.ninst = 0
        self.nwaits = 0
        self.cur = None
        self.stages = {}
        self.itn = 0

    def begin(self, name):
        self.cur = self.stages.setdefault(name, [])

    def end(self):
        self.cur = None

    def emit(self, name):
        assert self.cur is None
        for item in self.stages.pop(name, []):
            self._emit_item(item)

    def emit_scheduled(self, name, overlap=3.0):
        assert self.cur is None
        items = self.stages.pop(name, [])
        n = len(items)
        if n == 0:
            return

        class _Probe:
            def __init__(self):
                self.nm, self.a, self.k = None, (), {}

            def __getattr__(self, nm):
                def f(*a, **k):
                    self.nm, self.a, self.k = nm, a, k
                    return self
                return f

        def cost_of(it):
            if it[0] == "dma":
                return 0.1, 2.2
            pr = _Probe()
            try:
                it[2](pr)
                out = pr.k.get("out", pr.a[0] if pr.a else None)
                shp = tuple(out.shape)
                nel = 1
                for d_ in shp[1:]:
                    nel *= int(d_)
            except Exception:
                nel = 512
            eng = it[1]
            if eng == "pe":
                c = 0.12 if pr.nm == "transpose" else 0.04 + nel / 1400.0
            elif eng == "act":
                c = 0.2 + nel / 1150.0
            elif eng == "dve":
                c = 0.15 + nel / 680.0
            else:
                c = 0.2 + nel / 580.0
            return c, c

        last_w = {}
        readers = {}
        preds = [set() for _ in range(n)]
        for i, it in enumerate(items):
            if it[0] == "dma":
                reads, writes = it[4], it[5]
            else:
                reads, writes = it[3], it[4]
            for k in reads:
                w = last_w.get(k)
                if w is not None:
                    preds[i].add(w)
            for k in writes:
                w = last_w.get(k)
                if w is not None:
                    preds[i].add(w)
                for r in readers.get(k, ()):
                    preds[i].add(r)
            for k in reads:
                readers.setdefault(k, []).append(i)
            for k in writes:
                last_w[k] = i
                readers[k] = []
        succs = [[] for _ in range(n)]
        indeg = [0] * n
        for i in range(n):
            preds[i].discard(i)
            indeg[i] = len(preds[i])
            for p_ in preds[i]:
                succs[p_].append(i)
        engs = [it[1] for it in items]
        costs = [cost_of(it) for it in items]
        ready_t = [0.0] * n
        free = {}
        ready = set(i for i in range(n) if indeg[i] == 0)
        window = int(overlap * 1200)
        done = 0
        lowest = 0
        emitted = [False] * n
        while ready:
            best, bkey = None, None
            for i in ready:
                if i - lowest > window:
                    continue
                st = max(ready_t[i], free.get(engs[i], 0.0))
                key = (st, i)
                if bkey is None or key < bkey:
                    best, bkey = i, key
            if best is None:
                best = min(ready)
                bkey = (max(ready_t[best], free.get(engs[best], 0.0)), best)
            i = best
            ready.discard(i)
            st = bkey[0]
            busy, lat = costs[i]
            free[engs[i]] = st + busy
            fin = st + lat
            it = items[i]
            if it[0] == "op":
                self.op(*it[1:5])
            else:
                self.dma(*it[1:6], **it[6])
            emitted[i] = True
            while lowest < n and emitted[lowest]:
                lowest += 1
            done += 1
            for j in succs[i]:
                hop = 0.05 if engs[j] == engs[i] else 0.45
                if fin + hop > ready_t[j]:
                    ready_t[j] = fin + hop
                indeg[j] -= 1
                if indeg[j] == 0:
                    ready.add(j)
        assert done == n, (done, n)
        self.sim_time = getattr(self, "sim_time", 0.0) + max(free.values())
        busy = {}
        for i in range(n):
            busy[engs[i]] = busy.get(engs[i], 0.0) + costs[i][0]
        self.sim_log = getattr(self, "sim_log", [])
        self.sim_log.append((str(name), round(max(free.values()), 1), {k_: round(v_, 1) for k_, v_ in busy.items()}))

    def _emit_item(self, item):
        if item[0] == "op":
            self.op(*item[1:5])
        else:
            self.dma(*item[1:6], **item[6])

    def emit_merged(self, name_a, name_b):
        assert self.cur is None
        la = self.stages.pop(name_a, [])
        lb = self.stages.pop(name_b, [])
        i = j = 0
        while i < len(la) or j < len(lb):
            if j >= len(lb) or (i < len(la) and i * len(lb) <= j * len(la)):
                self._emit_item(la[i])
                i += 1
            else:
                self._emit_item(lb[j])
                j += 1

    @staticmethod
    def _keys(lst):
        return [x.k if isinstance(x, Tl) else x for x in lst]

    def _need(self, eng, ev):
        sem, val, src = ev
        if src == "pe" and eng == "pe":
            return
        cur = self.seen[eng].get(sem.name, 0)
        if cur >= val:
            return
        self.seen[eng][sem.name] = val
        self.engs[eng].wait_ge(sem, val)
        self.nwaits += 1

    def _deps(self, eng, reads, writes):
        for k in reads:
            st = self.state.get(k)
            if st and st[0] is not None:
                self._need(eng, st[0])
        for k in writes:
            st = self.state.get(k)
            if st:
                if st[0] is not None:
                    self._need(eng, st[0])
                for ev in st[1].values():
                    self._need(eng, ev)

    def _record(self, ev, reads, writes):
        for k in reads:
            st = self.state.setdefault(k, [None, {}])
            old = st[1].get(ev[0].name)
            if old is None or old[1] < ev[1]:
                st[1][ev[0].name] = ev
        for k in writes:
            self.state[k] = [ev, {}]

    def op(self, eng, fn, reads=(), writes=()):
        reads = self._keys(reads)
        writes = self._keys(writes)
        if self.cur is not None:
            self.cur.append(("op", eng, fn, reads, writes, self.itn))
            return
        self._deps(eng, reads, writes)
        ins = fn(self.engs[eng])
        self.cnt[eng] += 1
        ins.then_inc(self.sem[eng], 1)
        ev = (self.sem[eng], self.cnt[eng], eng)
        self._record(ev, reads, writes)
        self.ninst += 1

    def dma(self, q, out, in_, reads=(), writes=(), **kw):
        reads = self._keys(reads)
        writes = self._keys(writes)
        if self.cur is not None:
            self.cur.append(("dma", q, out, in_, reads, writes, kw, self.itn))
            return
        ring = self.dring[q]
        slot = ring[self.dpos[q] % len(ring)]
        self.dpos[q] += 1
        sem, tot = slot
        if tot > 0:
            self._need(q, (sem, tot, None))
        self._deps(q, reads, writes)
        ins = self.engs[q].dma_start(out=out, in_=in_, **kw)
        slot[1] = tot + 16
        ins.then_inc(sem, 16)
        ev = (sem, slot[1], None)
        self._record(ev, reads, writes)
        self.ninst += 1

    def barrier(self):
        evs = [(self.sem[e], self.cnt[e], e) for e in self.engs if self.cnt[e] > 0]
        for q in self.dring:
            for sem, tot in self.dring[q]:
                if tot > 0:
                    evs.append((sem, tot, None))
        for e in self.engs:
            for ev in evs:
                self._need(e, ev)

    def wait_all(self, eng="sp"):
        for k, st in list(self.state.items()):
            if st[0] is not None:
                self._need(eng, st[0])
            for ev in st[1].values():
                self._need(eng, ev)


OVL_C = 3.0


def build_program(debug_out=None, n_layers=DEPTH, stop_after=None):
    nc = bass.Bass("TRN2", target_bir_lowering=False)
    S = Sched(nc)

    def din(name, shape, dt=F32):
        return nc.dram_tensor(name, list(shape), dt, kind="ExternalInput").ap()

    dbg = set(debug_out or [])

    def dscr(name, shape, dt=F32):
        kind = "ExternalOutput" if name in dbg else "Internal"
        return nc.dram_tensor(name, list(shape), dt, kind=kind).ap()

    x_in = din("x", [T_LAT, D])
    ctx_in = din("ctx", [T_CTX, D])
    cc_in = din("cc", [128, 8, 2])
    normw_in = din("norm_w", [DEPTH, 1, D])
    wmod_in = din("w_mod", [DEPTH, D, 3 * D])
    bmod_in = din("b_mod", [DEPTH, 1, 3 * D])
    wA_in = din("wA", [DEPTH, D, NA])
    wC_in = din("wC", [DEPTH, D, NCC])
    wQ_in = din("wQ", [DEPTH, D, NQ])
    wom_in = din("wo_all", [DEPTH, 2048, D])
    convw_in = din("convw", [DEPTH, 128, 12, 3])
    convb_in = din("convb", [DEPTH, 128, 12])
    scw_in = din("scw", [DEPTH, 128, 4, 3])
    dtb_in = din("dt_bias", [DEPTH, 1, 32])
    alog_in = din("a_log", [DEPTH, 1, 32])
    dsk_in = din("ssd_d", [DEPTH, 1, 16])
    snw_in = din("ssd_norm_w", [DEPTH, 1, D])
    sink_in = din("sink", [DEPTH, 1, 8])
    fnw_in = din("final_norm_w", [1, D])
    rope_in = din("ropecs", [128, 2, T_LAT])
    out_d = nc.dram_tensor("out", [T_LAT, D], F32, kind="ExternalOutput").ap()

    x1_d = dscr("x1", [T_LAT, D])
    ctx1_d = dscr("ctx1", [T_CTX, D])
    hT_d = dscr("hT_all", [NCH, 128, 8, 128], BF16)
    xs_d = dscr("xs_all", [NCH, 128, 1024], BF16)
    btm_d = dscr("btm_all", [NCH, 128, 256], BF16)
    bt_d = dscr("bt_all", [NCH, 128, 2, 128], BF16)
    ct_d = dscr("ct_all", [NCH, 128, 2, 128], BF16)
    dtda_d = dscr("dtda_all", [NCH, 128, 64])
    sb_d = dscr("sb_all", [NCH, 128, 1024], BF16)
    cvc_d = dscr("cvc_all", [NCH, 128, 4, 128])
    kt_d = dscr("kt_all", [128, T_ALL], BF16)
    v_d = dscr("v_all", [NCH, 128, 128], BF16)
    mod_d = dscr("mod_scr", [DEPTH, 2, 3 * D])
    yc_d = dscr("yc_all", [NCH, 128, 12, 128], BF16)
    og_d = dscr("og_all", [NCH, 64, 8, 128], BF16)
    qt_d = dscr("qt_all", [NCH, 128, 4, 128], BF16)
    za_d = dscr("za_all", [NCH, 64, 8, 128])
    dbg_ys = dscr("dbg_ys", [NCH, 128, 1024], BF16) if "dbg_ys" in dbg else None
    dbg_sc = dscr("dbg_sc", [NCH, 128, 4, 128], BF16) if "dbg_sc" in dbg else None
    dbg_og = dscr("dbg_og", [NCH, 64, 2, 4, 128], BF16) if "dbg_og" in dbg else None

    SB_BASE, SB_END = 16640, 229376
    DTB = {F32: 4, BF16: 2}
    ptr = {"persist": SB_BASE}
    lim = {}

    def _alloc(space, name, shape, dt):
        n = 1
        for d_ in shape[1:]:
            n *= d_
        nbytes = (n * DTB[dt] + 63) // 64 * 64
        off = ptr[space]
        ptr[space] = off + nbytes
        assert ptr[space] <= lim.get(space, SB_END), (space, name, ptr[space])
        uname = "%s_%s" % (space, name)
        return Tl(nc.alloc_sbuf_tensor_at(uname, list(shape), dt, offset=off), uname)

    def sb(name, shape, dt=F32):
        return _alloc("persist", name, shape, dt)

    W = sb("W", [128, 49152], BF16)
    ident_b = sb("ident_b", [128, 128], BF16)
    cm_f = {n: sb("cm_" + n, [128, 128], F32) for n in ("UI", "LI", "SU", "SL", "ones")}
    cm_b = {n: sb("cb_" + n, [128, 128], BF16) for n in ("UI", "LI", "SU", "SL", "ones")}
    negm = {n: sb("negm_" + n, [128, 4, 128], BF16) for n in ("UI", "LI")}
    g_l = sb("g_l", [128, D])
    snw_b = sb("snw_b", [128, D])
    aux = sb("aux", [128, D])
    small = sb("small", [128, 128])
    convw = sb("convw", [128, 12, 3])
    convb = sb("convb", [128, 12])
    scw = sb("scw", [128, 4, 3])
    cc = sb("cc", [128, 8, 2])
    st8 = sb("st8", [128, 8])
    PH_BASE = ptr["persist"]

    def phase_tiles(space, specs):
        ptr[space] = PH_BASE
        d_ = {}
        for (name, shape, dt) in specs:
            if isinstance(name, tuple):
                d_[name[0]] = {k_: _alloc(space, "%s_%s" % (name[0], k_), shape, dt) for k_ in name[1]}
            else:
                d_[name] = _alloc(space, name, shape, dt)
        return d_

    dbl = lambda n: (n, ("0", "1"))
    T0 = phase_tiles("p0", [
        ("A_l", [128, D], F32), ("sh_l", [128, D], F32), ("A_c", [128, D], F32), ("sh_c", [128, D], F32),
        ("tmpA", [128, D], F32), ("stg0", [128, 1024], F32), ("stg1", [128, 1024], F32),
        ("modsb", [2, 3 * D], F32), ("bmod2", [2, 3 * D], F32),
        (dbl("xin"), [128, D], F32), (dbl("sq"), [128, D], F32), (dbl("tmpB"), [128, D], F32), (dbl("hb"), [128, D], BF16),
        (dbl("hT"), [128, 8, 128], BF16), (dbl("st"), [128, 8], F32)])
    TA = phase_tiles("pA", [
        ("stg0", [128, 1024], F32), ("stg1", [128, 1024], F32), (dbl("hTw"), [128, 8, 258], BF16),
        (dbl("xbcT"), [128, 12, 256], BF16), (dbl("cv"), [128, 258], F32), (dbl("cv2"), [128, 258], F32),
        (dbl("cvc"), [128, 4, 256], F32),
        (dbl("ropeT"), [128, 2, 256], F32), (dbl("ktile"), [128, 256], BF16), (dbl("xs"), [128, 1024], BF16),
        (dbl("btm"), [128, 256], BF16),
        (dbl("dtda"), [128, 64], F32), (dbl("dtr"), [128, 32], F32), ("ExS", [128, 64], F32), ("wsm", [128, 32], F32),
        (dbl("vtm"), [128, 128], BF16), ("rhsb", [128, 1024], BF16), ("S", [128, 1024], F32), ("S16", [128, 1024], BF16)])
    dbl = lambda n: (n, ("0", "1"))
    TC = phase_tiles("pC", [
        ("stg0", [128, 1024], F32), ("stg1", [128, 1024], F32),
        (dbl("xs"), [128, 1024], BF16), (dbl("btm"), [128, 256], BF16), (dbl("dtda"), [128, 64], F32),
        (dbl("hT"), [128, 8, 128], BF16), (dbl("btct"), [128, 4, 128], BF16), (dbl("sbin"), [128, 1024], BF16),
        (dbl("KTb"), [128, 5, 128], BF16),
        (dbl("Vb"), [128, 5, 128], BF16), (dbl("sz"), [128, 1024], F32), (dbl("Ex"), [128, 64], F32),
        (dbl("M_f"), [128, 16, 128], BF16), (dbl("M_b"), [128, 16, 128], BF16), (dbl("xdt_f"), [128, 1024], BF16),
        (dbl("xdt_b"), [128, 1024], BF16), (dbl("QT"), [128, 4, 128], BF16), (dbl("szA"), [64, 2, 4, 128], F32),
        ("OG", [64, 2, 4, 128], BF16), ("rden", [64, 4, 128], F32),
        ("hilo", [128, 64], BF16), ("dres", [128, 32], F32), ("wsm", [128, 32], F32), ("ExS", [128, 64], F32)])
    ptr["pCb"] = SB_BASE + 8 * NCC * 2
    lim["pCb"] = SB_BASE + 65536
    _pcb_base = ptr["pCb"]
    TCb = {}
    for (name, shape, dt) in [
            ("rb0", [128, 16, 128], BF16), ("rb1", [128, 16, 128], BF16), ("Lm0", [128, 4, 128], F32), ("Lm1", [128, 4, 128], F32),
            ("tmpA", [128, D], F32), ("tmpB", [128, D], F32), ("cbm", [128, 2, 2, 128], F32),
            ("ysb", [128, 1024], BF16), ("ysT", [128, 8, 128], BF16), ("rhsb", [128, 1024], BF16), ("S", [128, 1024], F32),
            ("S16", [128, 1024], BF16), ("PT0", [128, 512], BF16), ("PT1", [128, 512], BF16)]:
        TCb[name] = _alloc("pCb", name, shape, dt)
    TC.update(TCb)
    NBUF_C = 2
    for (name, shape, dt, sp_) in [
            ("tmpA", [128, D], F32, "pCb"), ("tmpB", [128, D], F32, "pCb"), ("ysb", [128, 1024], BF16, "pC"),
            ("ysT", [128, 8, 128], BF16, "pC"), ("PT0", [128, 512], BF16, "pC"), ("PT1", [128, 512], BF16, "pC"),
            ("rhsb", [128, 1024], BF16, "pC")]:
        TC[name] = {"0": TC[name], "1": _alloc(sp_, name + "_b", shape, dt)}
    for (name, shape, dt) in [("OG", [64, 2, 4, 128], BF16), ("rden", [64, 4, 128], F32), ("wsm", [128, 32], F32),
                              ("ExS", [128, 64], F32), ("stB", [128, 8], F32)]:
        first = TC[name] if name in TC else _alloc("pC", name + "_a", shape, dt)
        TC[name] = {"0": first, "1": _alloc("pC", name + "_b", shape, dt)}
    TQ = phase_tiles("pQ", [
        ("stg0", [128, 1024], F32), ("stg1", [128, 1024], F32), ("sct", [128, 512], F32), ("qtmp", [128, 1024], F32),
        (dbl("hTq"), [128, 8, 512], BF16), (dbl("cvq"), [128, 4, 512], F32), (dbl("ropq"), [128, 2, 512], F32),
        (dbl("scy"), [128, 4, 512], BF16), (dbl("QTq"), [128, 4, 512], BF16), (dbl("zaq"), [64, 4, 512], F32)])
    T3 = phase_tiles("p3", [
        ("stg0", [128, 1024], F32), ("stg1", [128, 1024], F32),
        (dbl("yc"), [128, 12, 128], BF16), (dbl("og"), [128, 4, 128], BF16), (dbl("xin"), [128, D], F32),
        (dbl("xnew"), [128, D], F32), (dbl("tmpB"), [128, D], F32), (dbl("st"), [128, 8], F32)])
    print("SBUF bytes: persist %d  p0 %d  pA %d  pC %d pCb %d/%d p3 %d pQ %d (end %d)" % (PH_BASE, ptr["p0"], ptr["pA"], ptr["pC"], ptr["pCb"], lim["pCb"], ptr["p3"], ptr["pQ"], SB_END))

    def ps(name, shape, dt=F32):
        return Tl(nc.alloc_psum_tensor(name, list(shape), dt), name)

    psT = ps("psT", [128, 1024], BF16)
    banks = [ps("bk%d" % i, [128, 512]) for i in range(7)]
    bpos = [0]

    bsel = [None]
    bsub = {}

    def pb():
        if bsel[0] is None:
            b = banks[bpos[0] % len(banks)]
            bpos[0] += 1
            return b
        lo, hi = bsel[0]
        c = bsub.get(bsel[0], 0)
        bsub[bsel[0]] = c + 1
        return banks[lo + c % (hi - lo)]

    def mm(out_ap, lhsT, rhs, start, stop, reads, writes):
        S.op("pe", lambda e: e.matmul(out_ap, lhsT=lhsT, rhs=rhs, start=start, stop=stop), reads, writes)

    def tr(out_ap, in_ap, reads, writes):
        S.op("pe", lambda e: e.transpose(out_ap, in_ap, ident_b[:]), list(reads) + [ident_b], writes)

    def bc_row(dst_tile, dst_ap, src_row_ap, n=128):
        S.dma("sp", dst_ap, src_row_ap.partition_broadcast(n), writes=[dst_tile])

    cast_rr = [0]
    W_A, W_Q, W_C, W_3 = (Tl(W.t, "W_A"), Tl(W.t, "W_Q"), Tl(W.t, "W_C"), Tl(W.t, "W_3"))
    QOFF = 8 * NA
    W3OFF = 32768

    def weight_pieces(TT, key, dst_off, src, ncols, kparts):
        pieces = []
        for k in range(kparts):
            c0 = 0
            while c0 < ncols:
                cw = min(1024, ncols - c0)

                def piece(k=k, c0=c0, cw=cw):
                    st = TT["stg%d" % (cast_rr[0] % 2)]
                    S.dma("sp", st[:, 0:cw], src[k * 128:(k + 1) * 128, c0:c0 + cw], writes=[st])
                    o = dst_off + k * ncols + c0
                    eng = ("act", "dve", "pool")[cast_rr[0] % 3]
                    if eng == "act":
                        S.op("act", lambda e: e.copy(out=W[:, o:o + cw], in_=st[:, 0:cw]), [st], [key])
                    else:
                        S.op(eng, lambda e: e.tensor_copy(out=W[:, o:o + cw], in_=st[:, 0:cw]), [st], [key])
                    cast_rr[0] += 1
                pieces.append(piece)
                c0 += cw
        return pieces

    def pieces_A(l, TT):
        return weight_pieces(TT, W_A, 0, wA_in[l], NA, 8)

    def pieces_Q(l, TT):
        return weight_pieces(TT, W_Q, QOFF, wQ_in[l], NQ, 8)

    def pieces_C(l, TT):
        return weight_pieces(TT, W_C, 0, wC_in[l], NCC, 8)

    def pieces_3(l, TT):
        ps_ = []
        for t in range(16):
            ps_ += weight_pieces(TT, W_3, W3OFF + t * 1024, wom_in[l, t * 128:(t + 1) * 128, :], 1024, 1)
        return ps_

    pending = {}

    def take(pieces, n):
        for _ in range(min(n, len(pieces))):
            pieces.pop(0)()

    def build_consts():
        def sel(t, pattern, cm, op):
            S.op("pool", lambda e: e.memset(t[:], 1.0), [], [t])
            S.op("pool", lambda e: e.affine_select(out=t[:], in_=t[:], pattern=pattern, compare_op=op,
                                                   fill=0.0, base=0, channel_multiplier=cm), [t], [t])
        sel(cm_f["UI"], [[1, 128]], -1, ALU.is_ge)
        sel(cm_f["LI"], [[-1, 128]], 1, ALU.is_ge)
        sel(cm_f["SU"], [[-1, 128]], 1, ALU.is_gt)
        sel(cm_f["SL"], [[1, 128]], -1, ALU.is_gt)
        S.op("pool", lambda e: e.memset(cm_f["ones"][:], 1.0), [], [cm_f["ones"]])
        for n in cm_f:
            S.op("dve", lambda e, n=n: e.tensor_copy(out=cm_b[n][:], in_=cm_f[n][:]), [cm_f[n]], [cm_b[n]])
        S.op("pool", lambda e: e.memset(ident_b[:], 1.0), [], [ident_b])
        S.op("pool", lambda e: e.affine_select(out=ident_b[:], in_=ident_b[:], pattern=[[-1, 128]],
                                               compare_op=ALU.is_equal, fill=0.0, base=0, channel_multiplier=1),
             [ident_b], [ident_b])
        for n in ("UI", "LI"):
            S.op("dve", lambda e, n=n: e.tensor_scalar(
                out=negm[n][:], in0=cm_f[n][:].unsqueeze(1).to_broadcast([128, 4, 128]), scalar1=-1.0, scalar2=2.4e5,
                op0=ALU.add, op1=ALU.mult), [cm_f[n]], [negm[n]])
        S.dma("sp", cc[:], cc_in, writes=[cc])
        S.op("act", lambda e: e.activation(out=cc[:], in_=cc[:], func=AF.Silu), [cc], [cc])

    mod_done = set()

    def mod_compute(l):
        modsb, bmod2 = T0["modsb"], T0["bmod2"]
        accs = [pb() for _ in range(6)]
        i = 0
        for kc in range(8):
            for third in range(3):
                st = T0["stg%d" % (i % 2)]
                i += 1
                S.dma("sp", st[:, 0:1024], wmod_in[l, kc * 128:(kc + 1) * 128, third * 1024:(third + 1) * 1024],
                      writes=[st])
                for j in range(2):
                    a_ = accs[third * 2 + j]
                    mm(a_[0:2, 0:512], cc[:, kc, :], st[:, j * 512:(j + 1) * 512], kc == 0, kc == 7, [cc, st], [a_])
        S.dma("sp", bmod2[:], bmod_in[l].partition_broadcast(2), writes=[bmod2])
        for j in range(6):
            S.op("dve", lambda e, j=j: e.tensor_tensor(out=modsb[:, j * 512:(j + 1) * 512], in0=accs[j][0:2, 0:512],
                                                        in1=bmod2[:, j * 512:(j + 1) * 512], op=ALU.add),
                 [accs[j], bmod2], [modsb])
        S.dma("sp", mod_d[l], modsb[:], reads=[modsb], writes=[("mod_d", l)])
        mod_done.add(l)

    def layer_setup(l):
        tmpA = T0["tmpA"]
        if l not in mod_done:
            mod_compute(l)
        for (row, sh_t, A_t, g_t) in ((0, T0["sh_l"], T0["A_l"], g_l), (1, T0["sh_c"], T0["A_c"], aux)):
            S.dma("sp", sh_t[:], mod_d[l, row:row + 1, 0:D].partition_broadcast(128), reads=[("mod_d", l)], writes=[sh_t])
            S.dma("sp", A_t[:], mod_d[l, row:row + 1, D:2 * D].partition_broadcast(128), reads=[("mod_d", l)], writes=[A_t])
            if row == 0 or l < DEPTH - 1:
                S.dma("sp", g_t[:], mod_d[l, row:row + 1, 2 * D:3 * D].partition_broadcast(128), reads=[("mod_d", l)],
                      writes=[g_t])
        if l == DEPTH - 1:
            bc_row(aux, aux[:], fnw_in)
        bc_row(tmpA, tmpA[:], normw_in[l])
        for A_t in (T0["A_l"], T0["A_c"]):
            S.op("dve", lambda e, A_t=A_t: e.scalar_tensor_tensor(out=A_t[:], in0=A_t[:], scalar=1.0, in1=tmpA[:],
                                                                    op0=ALU.add, op1=ALU.mult), [A_t, tmpA], [A_t])
        bc_row(snw_b, snw_b[:], snw_in[l])
        bc_row(small, small[:, 0:32], dtb_in[l])
        bc_row(small, small[:, 32:64], alog_in[l])
        bc_row(small, small[:, 64:80], dsk_in[l])
        bc_row(small, small[:, 80:88], sink_in[l])
        S.op("act", lambda e: e.activation(out=small[:, 32:64], in_=small[:, 32:64], func=AF.Exp), [small], [small])
        S.op("dve", lambda e: e.tensor_scalar_mul(out=small[:, 32:64], in0=small[:, 32:64], scalar1=-1.0), [small], [small])
        S.op("act", lambda e: e.activation(out=small[:, 80:88], in_=small[:, 80:88], func=AF.Exp), [small], [small])
        S.dma("sp", convw[:], convw_in[l], writes=[convw])
        S.dma("sp", convb[:], convb_in[l], writes=[convb])
        S.dma("sp", scw[:], scw_in[l], writes=[scw])

    def rms_rstd(src_tile, scratch, width, st8=st8):
        S.op("act", lambda e: e.activation(out=scratch[:, 0:width], in_=src_tile[:, 0:width], func=AF.Square),
             [src_tile], [scratch])
        S.op("dve", lambda e: e.reduce_sum(out=st8[:, 0:1], in_=scratch[:, 0:width], axis=AX.X), [scratch], [st8])
        S.op("dve", lambda e: e.tensor_scalar(out=st8[:, 0:1], in0=st8[:, 0:1], scalar1=1.0 / width, scalar2=EPS,
                                               op0=ALU.mult, op1=ALU.add), [st8], [st8])
        S.op("act", lambda e: e.activation(out=st8[:, 0:1], in_=st8[:, 0:1], func=AF.Sqrt), [st8], [st8])
        S.op("dve", lambda e: e.reciprocal(out=st8[:, 0:1], in_=st8[:, 0:1]), [st8], [st8])

    def pass0(l):
        pf = []
        if ("A", l) not in pending:
            pf = pieces_A(l, T0)
            pending[("A", l)] = True
        S.begin("p0")
        for gc in range(NCH):
            S.itn = gc
            if gc == 4 and l + 1 < DEPTH and (l + 1) not in mod_done:
                mod_compute(l + 1)
            take(pf, 1)
            p = str(gc % 2)
            xin, sq, tmpB, hb, hT, stt = (T0[n][p] for n in ("xin", "sq", "tmpB", "hb", "hT", "st"))
            if gc < 2:
                src = (ctx_in if l == 0 else ctx1_d)[gc * 128:(gc + 1) * 128, :]
                A_t, sh_t = T0["A_c"], T0["sh_c"]
            else:
                src = (x_in if l == 0 else x1_d)[(gc - 2) * 128:(gc - 1) * 128, :]
                A_t, sh_t = T0["A_l"], T0["sh_l"]
            S.dma("sp", xin[:], src, reads=["x1_d", "ctx1_d"], writes=[xin])
            rms_rstd(xin, sq, D, st8=stt)
            S.op("dve", lambda e, A_t=A_t, xin=xin, tmpB=tmpB, stt=stt: e.scalar_tensor_tensor(
                out=tmpB[:], in0=xin[:], scalar=stt[:, 0:1], in1=A_t[:], op0=ALU.mult, op1=ALU.mult),
                [xin, stt, A_t], [tmpB])
            S.op("pool", lambda e, sh_t=sh_t, hb=hb, tmpB=tmpB: e.tensor_tensor(out=hb[:], in0=tmpB[:], in1=sh_t[:], op=ALU.add),
                 [tmpB, sh_t], [hb])
            for kc in range(8):
                tr(psT[:, kc * 128:(kc + 1) * 128], hb[:, kc * 128:(kc + 1) * 128], [hb], [psT])
            S.op("act", lambda e, hT=hT: e.copy(out=hT[:].rearrange("p k t -> p (k t)"), in_=psT[:]), [psT], [hT])
            S.dma("sp", hT_d[gc], hT[:], reads=[hT], writes=["hT_d"])
        take(pf, 999)
        S.end()
        S.emit_scheduled("p0", overlap=OVL_C)

    def state_update(TT, pmat_key, da_cols, dt_cols, gc, store_d=None):
        Sd, Sd16, xs, btm, dtda, Ex, wsm, rhsb = (TT[n] for n in ("S", "S16", "xs", "btm", "dtda", "ExS", "wsm", "rhsb"))
        p = pb()
        mm(p[:, 0:16], cm_f[pmat_key][:], dtda[:, da_cols[0]:da_cols[1]], True, True, [cm_f[pmat_key], dtda], [p])
        mm(p[:, 16:32], cm_f["ones"][:], dtda[:, da_cols[0]:da_cols[1]], True, True, [cm_f["ones"], dtda], [p])
        S.op("act", lambda e: e.activation(out=Ex[:, 0:32], in_=p[:, 0:32], func=AF.Exp), [p], [Ex])
        S.op("dve", lambda e: e.tensor_tensor(out=wsm[:, 0:16], in0=dtda[:, dt_cols[0]:dt_cols[1]], in1=Ex[:, 0:16],
                                               op=ALU.mult), [dtda, Ex], [wsm])
        S.op("dve", lambda e: e.tensor_tensor(out=rhsb[:].rearrange("p (h q) -> p h q", h=16),
                                               in0=xs[:].rearrange("p (h q) -> p h q", h=16),
                                               in1=wsm[:, 0:16].unsqueeze(2).to_broadcast([128, 16, 64]), op=ALU.mult),
             [xs, wsm], [rhsb])
        if store_d is not None:
            S.dma("sp", store_d[gc], Sd16[:], reads=[Sd16], writes=["sb_d"])
        pa, pb2 = pb(), pb()
        mm(pa[:, 0:512], btm[:, 0:128], rhsb[:, 0:512], True, True, [btm, rhsb], [pa])
        mm(pb2[:, 0:512], btm[:, 128:256], rhsb[:, 512:1024], True, True, [btm, rhsb], [pb2])
        S.op("pool", lambda e: e.tensor_tensor(out=Sd[:].rearrange("p (h q) -> p h q", h=16),
                                                in0=Sd[:].rearrange("p (h q) -> p h q", h=16),
                                                in1=Ex[:, 16:32].unsqueeze(2).to_broadcast([128, 16, 64]), op=ALU.mult),
             [Sd, Ex], [Sd])
        S.op("dve", lambda e: e.tensor_tensor(out=Sd[:, 0:512], in0=Sd[:, 0:512], in1=pa[:, 0:512], op=ALU.add),
             [Sd, pa], [Sd])
        S.op("dve", lambda e: e.tensor_tensor(out=Sd[:, 512:1024], in0=Sd[:, 512:1024], in1=pb2[:, 0:512], op=ALU.add),
             [Sd, pb2], [Sd])
        S.op("act", lambda e: e.copy(out=Sd16[:], in_=Sd[:]), [Sd], [Sd16])

    WA = lambda kc, c0, c1: W[:, kc * NA + c0: kc * NA + c1]

    def passA(l):
        if not pending.pop(("A", l), False):
            take(pieces_A(l, TA), 999)
        pf = pieces_Q(l, TA)
        pending[("Q", l)] = True
        S.op("pool", lambda e: e.memset(TA["S"][:], 0.0), [], [TA["S"]])
        S.op("pool", lambda e: e.memset(TA["S16"][:], 0.0), [], [TA["S16"]])
        order = [0] + [2 + 2 * j for j in range(15, -1, -1)]
        S.begin("pA")
        for oi, gc0 in enumerate(order):
            S.itn = oi
            take(pf, 2)
            passA_super(oi, gc0)
        take(pf, 999)
        S.end()
        S.emit_scheduled("pA", overlap=OVL_C)

    def passA_super(oi, gc0):
        if True:
            sp_ = str(oi % 2)
            (hTw, xbcT, cv, cv2, cvc, ropeT, ktile) = (TA[n][sp_] for n in ("hTw", "xbcT", "cv", "cv2", "cvc", "ropeT", "ktile"))
            is_ctx = gc0 < 2
            for c2 in range(2):
                S.dma("sp", hTw[:, :, 1 + c2 * 128: 1 + (c2 + 1) * 128], hT_d[gc0 + c2], reads=["hT_d"], writes=[hTw])
            if is_ctx or gc0 == 2:
                S.op("pool", lambda e: e.memset(hTw[:, :, 0:1], 0.0), [], [hTw])
            else:
                S.dma("sp", hTw[:, :, 0:1], hT_d[gc0 - 1, :, :, 127:128], reads=["hT_d"], writes=[hTw],
                      allow_slow_non_contiguous=True)
            if is_ctx or gc0 == NCH - 2:
                S.op("pool", lambda e: e.memset(hTw[:, :, 257:258], 0.0), [], [hTw])
            else:
                S.dma("sp", hTw[:, :, 257:258], hT_d[gc0 + 2, :, :, 0:1], reads=["hT_d"], writes=[hTw],
                      allow_slow_non_contiguous=True)
            for m in range(12):
                p = pb()
                for kc in range(8):
                    mm(p[:, 0:258], WA(kc, A_XBC + m * 128, A_XBC + (m + 1) * 128), hTw[:, kc, :], kc == 0, kc == 7,
                       [W_A, hTw], [p])
                S.op("dve", lambda e, p=p, m=m: e.tensor_scalar_mul(out=cv[:, 0:256], in0=p[:, 0:256],
                                                                     scalar1=convw[:, m, 0:1]), [p, convw], [cv])
                S.op("dve", lambda e, p=p, m=m: e.scalar_tensor_tensor(out=cv[:, 0:256], in0=p[:, 1:257],
                                                                        scalar=convw[:, m, 1:2], in1=cv[:, 0:256],
                                                                        op0=ALU.mult, op1=ALU.add), [p, convw, cv], [cv])
                S.op("dve", lambda e, p=p, m=m: e.scalar_tensor_tensor(out=cv[:, 0:256], in0=p[:, 2:258],
                                                                        scalar=convw[:, m, 2:3], in1=cv[:, 0:256],
                                                                        op0=ALU.mult, op1=ALU.add), [p, convw, cv], [cv])
                S.op("act", lambda e, m=m: e.activation(out=xbcT[:, m, :], in_=cv[:, 0:256], func=AF.Silu,
                                                         bias=convb[:, m:m + 1], scale=1.0), [cv, convb], [xbcT])
            for m in range(4):
                pv, pc = pb(), pb()
                for kc in range(8):
                    mm(pv[:, 0:258], WA(kc, A_SCV + m * 128, A_SCV + (m + 1) * 128), hTw[:, kc, :], kc == 0, kc == 7,
                       [W_A, hTw], [pv])
                for kc in range(8):
                    mm(pc[:, 0:258], WA(kc, A_SCC + m * 128, A_SCC + (m + 1) * 128), hTw[:, kc, :], kc == 0, kc == 7,
                       [W_A, hTw], [pc])
                S.op("act", lambda e, pv=pv: e.copy(out=cv2[:], in_=pv[:, 0:258]), [pv], [cv2])
                S.op("dve", lambda e, pc=pc: e.tensor_tensor(out=cv2[:], in0=pc[:, 0:258], in1=cv2[:], op=ALU.mult),
                     [pc, cv2], [cv2])
                S.op("dve", lambda e, m=m: e.tensor_scalar_mul(out=cvc[:, m, :], in0=cv2[:, 0:256],
                                                                 scalar1=scw[:, m, 0:1]), [cv2, scw], [cvc])
                S.op("dve", lambda e, m=m: e.scalar_tensor_tensor(out=cvc[:, m, :], in0=cv2[:, 1:257],
                                                                    scalar=scw[:, m, 1:2], in1=cvc[:, m, :],
                                                                    op0=ALU.mult, op1=ALU.add), [cv2, scw, cvc], [cvc])
                S.op("dve", lambda e, m=m: e.scalar_tensor_tensor(out=cvc[:, m, :], in0=cv2[:, 2:258],
                                                                    scalar=scw[:, m, 2:3], in1=cvc[:, m, :],
                                                                    op0=ALU.mult, op1=ALU.add), [cv2, scw, cvc], [cvc])
            for c2 in range(2):
                S.dma("sp", cvc_d[gc0 + c2], cvc[:, :, c2 * 128:(c2 + 1) * 128], reads=[cvc], writes=["cvc_d"])
            pk, pks = pb(), pb()
            for kc in range(8):
                mm(pk[:, 0:258], WA(kc, A_K, A_K + 128), hTw[:, kc, :], kc == 0, kc == 7, [W_A, hTw], [pk])
            if is_ctx:
                S.op("act", lambda e: e.copy(out=ktile[:], in_=pk[:, 1:257]), [pk], [ktile])
            else:
                for kc in range(8):
                    mm(pks[:, 0:258], WA(kc, A_KSW, A_KSW + 128), hTw[:, kc, :], kc == 0, kc == 7, [W_A, hTw], [pks])
                t0 = (gc0 - 2) * 128
                S.dma("sp", ropeT[:], rope_in[:, :, t0:t0 + 256], writes=[ropeT])
                S.op("dve", lambda e: e.tensor_tensor(out=cv[:, 0:256], in0=pk[:, 1:257], in1=ropeT[:, 0, :], op=ALU.mult),
                     [pk, ropeT], [cv])
                S.op("dve", lambda e: e.tensor_tensor(out=cv2[:, 0:256], in0=pks[:, 1:257], in1=ropeT[:, 1, :], op=ALU.mult),
                     [pks, ropeT], [cv2])
                S.op("pool", lambda e: e.tensor_tensor(out=ktile[:], in0=cv[:, 0:256], in1=cv2[:, 0:256], op=ALU.add),
                     [cv, cv2], [ktile])
            S.dma("sp", kt_d[:, gc0 * 128:(gc0 + 2) * 128], ktile[:], reads=[ktile], writes=["kt_d"])
            for c2 in (1, 0):
                passA_chunk(gc0, c2, hTw, xbcT)

    def passA_chunk(gc0, c2, hTw, xbcT):
        if True:
            if True:
                gc = gc0 + c2
                cp_ = str(gc % 2)
                xs, btm, dtda, dtr, vtm = (TA[n][cp_] for n in ("xs", "btm", "dtda", "dtr", "vtm"))
                TS = dict(TA)
                TS.update({"xs": xs, "btm": btm, "dtda": dtda})
                tsl = slice(1 + c2 * 128, 1 + (c2 + 1) * 128)
                csl = slice(c2 * 128, (c2 + 1) * 128)
                p = pb()
                for kc in range(8):
                    mm(p[:, 0:128], hTw[:, kc, tsl], WA(kc, A_V, A_V + 128), kc == 0, kc == 7, [W_A, hTw], [p])
                S.op("act", lambda e, p=p: e.copy(out=vtm[:], in_=p[:, 0:128]), [p], [vtm])
                S.dma("sp", v_d[gc], vtm[:], reads=[vtm], writes=["v_d"])
                p = pb()
                for kc in range(8):
                    mm(p[:, 0:32], hTw[:, kc, tsl], WA(kc, A_DT, A_DT + 32), kc == 0, kc == 7, [W_A, hTw], [p])
                S.op("dve", lambda e, p=p: e.tensor_tensor(out=dtr[:], in0=p[:, 0:32], in1=small[:, 0:32], op=ALU.add),
                     [p, small], [dtr])
                S.op("act", lambda e: e.activation(out=dtr[:], in_=dtr[:], func=AF.Exp), [dtr], [dtr])
                S.op("act", lambda e: e.activation(out=dtda[:, 0:32], in_=dtr[:], func=AF.Ln, bias=1.0, scale=1.0),
                     [dtr], [dtda])
                S.op("dve", lambda e: e.tensor_tensor(out=dtda[:, 32:64], in0=dtda[:, 0:32], in1=small[:, 32:64],
                                                       op=ALU.mult), [dtda, small], [dtda])
                S.dma("sp", dtda_d[gc], dtda[:], reads=[dtda], writes=["dtda_d"])
                for m in range(8):
                    tr(psT[:, m * 128:(m + 1) * 128], xbcT[:, m, csl], [xbcT], [psT])
                S.op("act", lambda e: e.copy(out=xs[:], in_=psT[:]), [psT], [xs])
                S.dma("sp", xs_d[gc], xs[:], reads=[xs], writes=["xs_d"])
                for m in range(2):
                    tr(psT[:, m * 128:(m + 1) * 128], xbcT[:, 8 + m, csl], [xbcT], [psT])
                S.op("dve", lambda e: e.tensor_copy(out=btm[:], in_=psT[:, 0:256]), [psT], [btm])
                S.dma("sp", btm_d[gc], btm[:], reads=[btm], writes=["btm_d"])
                S.dma("sp", bt_d[gc], xbcT[:, 8:10, csl], reads=[xbcT], writes=["bt_d"])
                S.dma("sp", ct_d[gc], xbcT[:, 10:12, csl], reads=[xbcT], writes=["ct_d"])
                state_update(TS, "SL", (48, 64), (16, 32), gc, store_d=sb_d)

    WQ = lambda kc, c0, c1: W[:, QOFF + kc * NQ + c0: QOFF + kc * NQ + c1]

    def passQ(l):
        if not pending.pop(("Q", l), False):
            take(pieces_Q(l, TQ), 999)
        pf = pieces_C(l, TQ)
        pending[("C", l)] = True
        sct, qtmp = TQ["sct"], TQ["qtmp"]
        quads = [(0, 2)] + [(2 + 4 * k, 4) for k in range(8)]
        S.begin("pQ")
        for qi, (gc0, nch) in enumerate(quads):
            S.itn = qi
            take(pf, 1)
            passQ_quad(qi, gc0, nch, sct, qtmp)
        take(pf, 999)
        S.end()
        S.emit_scheduled("pQ", overlap=OVL_C)

    def passQ_quad(qi, gc0, nch, sct, qtmp):
        p_ = str(qi % 2)
        hTq, cvq, ropq, scy, QTq = (TQ[n][p_] for n in ("hTq", "cvq", "ropq", "scy", "QTq"))
        N = nch * 128
        is_ctx = gc0 < 2
        for ci in range(nch):
            S.dma("sp", hTq[:, :, ci * 128:(ci + 1) * 128], hT_d[gc0 + ci], reads=["hT_d"], writes=[hTq])
            S.dma("sp", cvq[:, :, ci * 128:(ci + 1) * 128], cvc_d[gc0 + ci], reads=["cvc_d"], writes=[cvq])
        if not is_ctx:
            t0 = (gc0 - 2) * 128
            S.dma("sp", ropq[:, :, 0:N], rope_in[:, :, t0:t0 + N], writes=[ropq])
        for m in range(4):
            pB, pZ = pb(), pb()
            for kc in range(8):
                mm(pB[:, 0:N], WQ(kc, Q_SCB + m * 128, Q_SCB + (m + 1) * 128), hTq[:, kc, 0:N], kc == 0, kc == 7, [W_Q, hTq], [pB])
            for kc in range(8):
                mm(pZ[:, 0:N], WQ(kc, Q_SCZ + m * 128, Q_SCZ + (m + 1) * 128), hTq[:, kc, 0:N], kc == 0, kc == 7, [W_Q, hTq], [pZ])
            S.op("act", lambda e, pZ=pZ: e.activation(out=sct[:, 0:N], in_=pZ[:, 0:N], func=AF.Silu), [pZ], [sct])
            S.op("dve", lambda e, pB=pB, m=m: e.tensor_tensor(out=cvq[:, m, 0:N], in0=pB[:, 0:N], in1=cvq[:, m, 0:N], op=ALU.mult),
                 [pB, cvq], [cvq])
            S.op("pool", lambda e, m=m: e.tensor_tensor(out=scy[:, m, 0:N], in0=cvq[:, m, 0:N], in1=sct[:, 0:N], op=ALU.mult),
                 [cvq, sct], [scy])
        for ci in range(nch):
            S.dma("sp", yc_d[gc0 + ci, :, 8:12, :], scy[:, :, ci * 128:(ci + 1) * 128], reads=[scy], writes=["yc_d"])
        for j in range(4):
            pq = pb()
            for kc in range(8):
                mm(pq[:, 0:N], WQ(kc, Q_Q + j * 128, Q_Q + (j + 1) * 128), hTq[:, kc, 0:N], kc == 0, kc == 7, [W_Q, hTq], [pq])
            if is_ctx:
                S.op("act", lambda e, pq=pq, j=j: e.copy(out=QTq[:, j, 0:N], in_=pq[:, 0:N]), [pq], [QTq])
            else:
                pqs = pb()
                for kc in range(8):
                    mm(pqs[:, 0:N], WQ(kc, Q_QSW + j * 128, Q_QSW + (j + 1) * 128), hTq[:, kc, 0:N], kc == 0, kc == 7,
                       [W_Q, hTq], [pqs])
                S.op("dve", lambda e, pq=pq: e.tensor_tensor(out=qtmp[:, 0:N], in0=pq[:, 0:N], in1=ropq[:, 0, 0:N], op=ALU.mult),
                     [pq, ropq], [qtmp])
                S.op("dve", lambda e, pqs=pqs: e.tensor_tensor(out=qtmp[:, 512:512 + N], in0=pqs[:, 0:N], in1=ropq[:, 1, 0:N],
                                                               op=ALU.mult), [pqs, ropq], [qtmp])
                S.op("pool", lambda e, j=j: e.tensor_tensor(out=QTq[:, j, 0:N], in0=qtmp[:, 0:N], in1=qtmp[:, 512:512 + N],
                                                            op=ALU.add), [qtmp], [QTq])
        for ci in range(nch):
            S.dma("sp", qt_d[gc0 + ci], QTq[:, :, ci * 128:(ci + 1) * 128], reads=[QTq], writes=["qt_d"])
        for g in range(2):
            zaq = TQ["zaq"][str(g)]
            for j in range(4):
                hq = g * 4 + j
                pza = pb()
                for kc in range(8):
                    mm(pza[0:64, 0:N], WQ(kc, Q_ZA + hq * 64, Q_ZA + (hq + 1) * 64), hTq[:, kc, 0:N], kc == 0, kc == 7,
                       [W_Q, hTq], [pza])
                S.op("act", lambda e, pza=pza, j=j, zaq=zaq: e.activation(out=zaq[:, j, 0:N], in_=pza[0:64, 0:N], func=AF.Silu),
                     [pza], [zaq])
            for ci in range(nch):
                S.dma("sp", za_d[gc0 + ci, :, g * 4:(g + 1) * 4, :], zaq[:, :, ci * 128:(ci + 1) * 128], reads=[zaq],
                      writes=["za_d"])

    WC = lambda kc, c0, c1: W[:, kc * NCC + c0: kc * NCC + c1]
    WO0 = W3OFF
    WOA = WO0 + 12 * 1024

    def passC(l):
        last = (l == DEPTH - 1)
        rb = [TC["rb0"], TC["rb1"]]
        Lm = [TC["Lm0"], TC["Lm1"]]
        (cbm, hilo, dres, S16f) = (TC[n] for n in ("cbm", "hilo", "dres", "S16"))
        if not pending.pop(("C", l), False):
            take(pieces_C(l, TC), 999)
        pf = pieces_3(l, TC)
        pending[("3", l)] = True
        S.op("pool", lambda e: e.memset(TC["S"][:], 0.0), [], [TC["S"]])
        S.op("pool", lambda e: e.memset(TC["S16"][:], 0.0), [], [TC["S16"]])

        def tiles(gc):
            p = str(gc % NBUF_C)
            return {n: TC[n][p] for n in ("xs", "btm", "dtda", "hT", "btct", "sbin", "KTb", "Vb", "sz", "Ex",
                                          "M_f", "M_b", "xdt_f", "xdt_b", "QT", "szA")}

        def kblocks(gc):
            if gc < 2:
                return [(0, None), (1, None)]
            n = gc - 2
            kbs = []
            if n > 0:
                kbs.append((gc - 1, "LI"))
            kbs.append((gc, None))
            if n < 31:
                kbs.append((gc + 1, "UI"))
            return kbs + [(0, None), (1, None)]

        def stageF(gc):
            t = tiles(gc)
            xs, btm, dtda, hT, btct, sbin, KTb, Vb, sz, Ex, QT, szA = (t[n] for n in (
                "xs", "btm", "dtda", "hT", "btct", "sbin", "KTb", "Vb", "sz", "Ex", "QT", "szA"))
            Mm = {"f": t["M_f"], "b": t["M_b"]}
            xdt = {"f": t["xdt_f"], "b": t["xdt_b"]}
            is_ctx = gc < 2
            S.dma("sp", xs[:], xs_d[gc], reads=["xs_d"], writes=[xs])
            S.dma("sp", btm[:], btm_d[gc], reads=["btm_d"], writes=[btm])
            S.dma("sp", dtda[:], dtda_d[gc], reads=["dtda_d"], writes=[dtda])
            if is_ctx and last:
                return
            S.dma("sp", hT[:], hT_d[gc], reads=["hT_d"], writes=[hT])
            S.dma("sp", btct[:, 0:2, :], bt_d[gc], reads=["bt_d"], writes=[btct])
            S.dma("sp", btct[:, 2:4, :], ct_d[gc], reads=["ct_d"], writes=[btct])
            S.dma("sp", sbin[:], sb_d[gc], reads=["sb_d"], writes=[sbin])
            S.dma("sp", QT[:], qt_d[gc], reads=["qt_d"], writes=[QT])
            S.dma("sp", szA[:].rearrange("p g j t -> p (g j) t"), za_d[gc], reads=["za_d"], writes=[szA])
            kbs = kblocks(gc)
            for i, (kg, _) in enumerate(kbs):
                S.dma("sp", KTb[:, i, :], kt_d[:, kg * 128:(kg + 1) * 128], reads=["kt_d"], writes=[KTb])
                S.dma("sp", Vb[:, i, :], v_d[kg], reads=["v_d"], writes=[Vb])
            S.op("dve", lambda e: e.tensor_copy(out=hilo[:, 0:32], in_=dtda[:, 32:64]), [dtda], [hilo])
            S.op("dve", lambda e: e.tensor_tensor(out=dres[:], in0=dtda[:, 32:64], in1=hilo[:, 0:32], op=ALU.subtract),
                 [dtda, hilo], [dres])
            S.op("dve", lambda e: e.tensor_copy(out=hilo[:, 32:64], in_=dres[:]), [dres], [hilo])
            pz = [pb(), pb()]
            for h2 in range(2):
                for kc in range(8):
                    mm(pz[h2][:, 0:512], hT[:, kc, :], WC(kc, C_Z + h2 * 512, C_Z + (h2 + 1) * 512), kc == 0, kc == 7,
                       [W_C, hT], [pz[h2]])
                S.op("act", lambda e, h2=h2: e.activation(out=sz[:, h2 * 512:(h2 + 1) * 512], in_=pz[h2][:, 0:512],
                                                           func=AF.Silu), [pz[h2]], [sz])
            p = pb()
            mm(p[:, 32:48], cm_f["UI"][:], dtda[:, 32:48], True, True, [cm_f["UI"], dtda], [p])
            mm(p[:, 48:64], cm_f["LI"][:], dtda[:, 48:64], True, True, [cm_f["LI"], dtda], [p])
            S.op("act", lambda e: e.activation(out=Ex[:, 32:64], in_=p[:, 32:64], func=AF.Exp), [p], [Ex])
            pcb = pb()
            for g in range(2):
                mm(pcb[:, g * 128:(g + 1) * 128], btct[:, g, :], btct[:, 2 + g, :], True, True, [btct], [pcb])
            for di, mk in enumerate(("UI", "LI")):
                S.op("dve", lambda e, di=di, mk=mk: e.tensor_tensor(
                    out=cbm[:, di, :, :], in0=pcb[:, 0:256].rearrange("p (g l) -> p g l", g=2),
                    in1=cm_f[mk][:].unsqueeze(1).to_broadcast([128, 2, 128]), op=ALU.mult), [pcb, cm_f[mk]], [cbm])
            li = 0
            for di, (dk, lk, mk) in enumerate((("f", "SU", "UI"), ("b", "SL", "LI"))):
                for part in range(2):
                    eng = "pool" if part == 0 else "dve"
                    S.op(eng, lambda e, part=part, di=di, mk=mk: e.tensor_tensor(
                        out=rb[part][:],
                        in0=hilo[:, part * 32 + di * 16: part * 32 + di * 16 + 16].unsqueeze(2).to_broadcast([128, 16, 128]),
                        in1=cm_b[mk][:].unsqueeze(1).to_broadcast([128, 16, 128]), op=ALU.mult),
                        [hilo, cm_b[mk]], [rb[part]])
                for q4 in range(4):
                    pD = pb()
                    for part in range(2):
                        mm(pD[:, 0:512], cm_b[lk][:], rb[part][:, q4 * 4:(q4 + 1) * 4, :].rearrange("p h l -> p (h l)"),
                           part == 0, part == 1, [cm_b[lk], rb[part]], [pD])
                    Lq = Lm[li % 2]
                    li += 1
                    S.op("act", lambda e, pD=pD, Lq=Lq: e.activation(out=Lq[:].rearrange("p h l -> p (h l)"),
                                                                      in_=pD[:, 0:512], func=AF.Exp), [pD], [Lq])
                    g = q4 // 2
                    eng = "pool"
                    S.op(eng, lambda e, dk=dk, di=di, q4=q4, g=g, Lq=Lq: e.tensor_tensor(
                        out=Mm[dk][:, q4 * 4:(q4 + 1) * 4, :], in0=Lq[:],
                        in1=cbm[:, di, g, :].unsqueeze(1).to_broadcast([128, 4, 128]), op=ALU.mult),
                        [Lq, cbm], [Mm[dk]])
                S.op("dve", lambda e, dk=dk, di=di: e.tensor_tensor(
                    out=xdt[dk][:].rearrange("p (h q) -> p h q", h=16), in0=xs[:].rearrange("p (h q) -> p h q", h=16),
                    in1=dtda[:, di * 16:(di + 1) * 16].unsqueeze(2).to_broadcast([128, 16, 64]), op=ALU.mult),
                    [xs, dtda], [xdt[dk]])

        def stageB(gc):
            t = tiles(gc)
            xs, btm, dtda, btct, sbin, KTb, Vb, sz, Ex, QT, szA = (t[n] for n in (
                "xs", "btm", "dtda", "btct", "sbin", "KTb", "Vb", "sz", "Ex", "QT", "szA"))
            Mm = {"f": t["M_f"], "b": t["M_b"]}
            xdt = {"f": t["xdt_f"], "b": t["xdt_b"]}
            bp = str(gc % 2)
            tmpA, tmpB, ysb, ysT, OG, rden, st8 = (TC[n][bp] for n in ("tmpA", "tmpB", "ysb", "ysT", "OG", "rden", "stB"))
            PT = [TC["PT0"][bp], TC["PT1"][bp]]
            TS = dict(TC)
            TS.update({"xs": xs, "btm": btm, "dtda": dtda, "rhsb": TC["rhsb"][bp], "wsm": TC["wsm"][bp], "ExS": TC["ExS"][bp]})
            is_ctx = gc < 2
            if is_ctx and last:
                state_update(TS, "SU", (32, 48), (0, 16), gc)
                return
            kbs = kblocks(gc)
            for g in range(2):
                gs = slice(g * 64, (g + 1) * 64)
                po, pd = pb(), pb()
                pscs = [pb(), pb()]
                nk = len(kbs)
                for i, (kg, mk) in enumerate(kbs):
                    psc = pscs[i % 2]
                    mm(psc[:, 0:512], KTb[gs, i, :], QT[gs, :, :].rearrange("p j t -> p (j t)"), True, mk is None, [KTb, QT], [psc])
                    if mk is not None:
                        mm(psc[:, 0:512], ident_b[:], negm[mk][:].rearrange("p j t -> p (j t)"), False, True,
                           [ident_b, negm[mk]], [psc])
                    pt = PT[i % 2]
                    S.op("act", lambda e, psc=psc, pt=pt: e.activation(out=pt[:], in_=psc[:, 0:512], func=AF.Exp, scale=0.125),
                         [psc], [pt])
                    mm(po[0:64, 0:512], Vb[:, i, gs], pt[:], i == 0, i == nk - 1, [Vb, pt], [po])
                    mm(pd[0:64, 0:512], cm_b["ones"][:, 0:64], pt[:], i == 0, i == nk - 1, [cm_b["ones"], pt], [pd])
                S.op("dve", lambda e, g=g, pd=pd: e.tensor_tensor(
                    out=rden[:], in0=pd[0:64, 0:512].rearrange("p (j t) -> p j t", j=4),
                    in1=small[0:64, 80 + g * 4: 84 + g * 4].unsqueeze(2).to_broadcast([64, 4, 128]), op=ALU.add),
                    [pd, small], [rden])
                S.op("dve", lambda e: e.reciprocal(out=rden[:], in_=rden[:]), [rden], [rden])
                S.op("dve", lambda e, po=po: e.tensor_tensor(out=rden[:].rearrange("p j t -> p (j t)"), in0=po[0:64, 0:512],
                                                              in1=rden[:].rearrange("p j t -> p (j t)"), op=ALU.mult),
                     [po, rden], [rden])
                S.op("pool", lambda e, g=g: e.tensor_tensor(out=OG[:, g, :, :], in0=rden[:], in1=szA[:, g, :, :], op=ALU.mult),
                     [rden, szA], [OG])
            S.dma("sp", og_d[gc], OG[:].rearrange("p g j t -> p (g j) t"), reads=[OG], writes=["og_d"])
            py = [pb(), pb()]
            for h in range(16):
                o = py[h // 8][:, (h % 8) * 64:(h % 8 + 1) * 64]
                mm(o, Mm["f"][:, h, :], xdt["f"][:, h * 64:(h + 1) * 64], True, False, [Mm["f"], xdt["f"]], [py[h // 8]])
                mm(o, Mm["b"][:, h, :], xdt["b"][:, h * 64:(h + 1) * 64], False, True, [Mm["b"], xdt["b"]], [py[h // 8]])
            pof = [pb(), pb()]
            for g in range(2):
                mm(pof[g][:, 0:512], btct[:, 2 + g, :], S16f[:, g * 512:(g + 1) * 512], True, True, [btct, S16f], [pof[g]])
            for g in range(2):
                S.op("dve", lambda e, g=g: e.tensor_tensor(
                    out=tmpA[:, g * 512:(g + 1) * 512].rearrange("p (h q) -> p h q", h=8),
                    in0=pof[g][:, 0:512].rearrange("p (h q) -> p h q", h=8),
                    in1=Ex[:, 32 + g * 8: 32 + (g + 1) * 8].unsqueeze(2).to_broadcast([128, 8, 64]), op=ALU.mult),
                    [pof[g], Ex], [tmpA])
                S.op("dve", lambda e, g=g: e.tensor_tensor(out=tmpA[:, g * 512:(g + 1) * 512], in0=tmpA[:, g * 512:(g + 1) * 512],
                                                            in1=py[g][:, 0:512], op=ALU.add), [tmpA, py[g]], [tmpA])
            pob = [pb(), pb()]
            for g in range(2):
                mm(pob[g][:, 0:512], btct[:, 2 + g, :], sbin[:, g * 512:(g + 1) * 512], True, True, [btct, sbin], [pob[g]])
            for g in range(2):
                S.op("dve", lambda e, g=g: e.tensor_tensor(
                    out=tmpB[:, g * 512:(g + 1) * 512].rearrange("p (h q) -> p h q", h=8),
                    in0=pob[g][:, 0:512].rearrange("p (h q) -> p h q", h=8),
                    in1=Ex[:, 48 + g * 8: 48 + (g + 1) * 8].unsqueeze(2).to_broadcast([128, 8, 64]), op=ALU.mult),
                    [pob[g], Ex], [tmpB])
            S.op("pool", lambda e: e.tensor_tensor(out=tmpA[:], in0=tmpA[:], in1=tmpB[:], op=ALU.add), [tmpA, tmpB], [tmpA])
            S.op("pool", lambda e: e.tensor_tensor(
                out=tmpB[:].rearrange("p (h q) -> p h q", h=16), in0=xs[:].rearrange("p (h q) -> p h q", h=16),
                in1=small[:, 64:80].unsqueeze(2).to_broadcast([128, 16, 64]), op=ALU.mult), [xs, small], [tmpB])
            S.op("pool", lambda e: e.tensor_tensor(out=tmpA[:], in0=tmpA[:], in1=tmpB[:], op=ALU.add), [tmpA, tmpB], [tmpA])
            S.op("dve", lambda e: e.tensor_tensor(out=tmpA[:], in0=tmpA[:], in1=sz[:], op=ALU.mult), [tmpA, sz], [tmpA])
            S.op("act", lambda e: e.activation(out=tmpB[:], in_=tmpA[:], func=AF.Square), [tmpA], [tmpB])
            S.op("dve", lambda e: e.reduce_sum(out=st8[:, 2:4], in_=tmpB[:].rearrange("p (g q) -> p g q", g=2), axis=AX.X),
                 [tmpB], [st8])
            S.op("dve", lambda e: e.tensor_scalar(out=st8[:, 2:4], in0=st8[:, 2:4], scalar1=1.0 / 512, scalar2=EPS,
                                                   op0=ALU.mult, op1=ALU.add), [st8], [st8])
            S.op("act", lambda e: e.activation(out=st8[:, 2:4], in_=st8[:, 2:4], func=AF.Sqrt), [st8], [st8])
            S.op("dve", lambda e: e.reciprocal(out=st8[:, 2:4], in_=st8[:, 2:4]), [st8], [st8])
            for g in range(2):
                S.op("dve", lambda e, g=g: e.scalar_tensor_tensor(
                    out=ysb[:, g * 512:(g + 1) * 512], in0=tmpA[:, g * 512:(g + 1) * 512], scalar=st8[:, 2 + g:3 + g],
                    in1=snw_b[:, g * 512:(g + 1) * 512], op0=ALU.mult, op1=ALU.mult), [tmpA, st8, snw_b], [ysb])
            for m in range(8):
                tr(psT[:, m * 128:(m + 1) * 128], ysb[:, m * 128:(m + 1) * 128], [ysb], [psT])
            S.op("act", lambda e: e.copy(out=ysT[:].rearrange("p k t -> p (k t)"), in_=psT[:]), [psT], [ysT])
            S.dma("sp", yc_d[gc, :, 0:8, :], ysT[:], reads=[ysT], writes=["yc_d"])
            state_update(TS, "SU", (32, 48), (0, 16), gc)

        S.begin("passC")
        for gc in range(NCH):
            S.itn = gc
            take(pf, 1)
            bsel[0] = (0, 3)
            stageF(gc)
            bsel[0] = (3, 7)
            stageB(gc)
        bsel[0] = None
        take(pf, 999)
        S.end()
        S.emit_scheduled("passC", overlap=OVL_C)

    def passC3(l):
        last = (l == DEPTH - 1)
        if not pending.pop(("3", l), False):
            take(pieces_3(l, T3), 999)
        pf = pieces_A(l + 1, T3) if l + 1 < DEPTH else []
        if pf:
            pending[("A", l + 1)] = True
        S.begin("p3")
        for gc in range(NCH):
            S.itn = gc
            take(pf, 1)
            is_ctx = gc < 2
            if is_ctx and last:
                continue
            p = str(gc % 2)
            yc, og, xin, xnew, tmpB, stt = (T3[n][p] for n in ("yc", "og", "xin", "xnew", "tmpB", "st"))
            if is_ctx:
                src = (ctx_in if l == 0 else ctx1_d)[gc * 128:(gc + 1) * 128, :]
            else:
                src = (x_in if l == 0 else x1_d)[(gc - 2) * 128:(gc - 1) * 128, :]
            S.dma("sp", yc[:], yc_d[gc], reads=["yc_d"], writes=[yc])
            ogv = og_d[gc].rearrange("d (t two) k -> two d t k", two=2)
            S.dma("sp", og[0:64, :, :], ogv[0], reads=["og_d"], writes=[og])
            S.dma("sp", og[64:128, :, :], ogv[1], reads=["og_d"], writes=[og])
            S.dma("sp", xin[:], src, reads=["x1_d", "ctx1_d"], writes=[xin])
            pout = [pb(), pb()]
            g_t = aux if is_ctx else g_l
            for h2 in range(2):
                ns = slice(h2 * 512, (h2 + 1) * 512)
                steps = []
                for t in range(12):
                    steps.append((yc[:, t, :], W[:, WO0 + t * 1024 + h2 * 512: WO0 + t * 1024 + (h2 + 1) * 512], [yc, W_3]))
                for t in range(4):
                    steps.append((og[:, t, :], W[:, WO0 + (12 + t) * 1024 + h2 * 512: WO0 + (12 + t) * 1024 + (h2 + 1) * 512], [og, W_3]))
                for i, (lt, rh, rd) in enumerate(steps):
                    mm(pout[h2][:, 0:512], lt, rh, i == 0, i == len(steps) - 1, rd, [pout[h2]])
                S.op("dve", lambda e, h2=h2, ns=ns, g_t=g_t, xnew=xnew, pout=pout: e.tensor_tensor(
                    out=xnew[:, ns], in0=pout[h2][:, 0:512], in1=g_t[:, ns], op=ALU.mult), [pout[h2], g_t], [xnew])
            S.op("pool", lambda e, xnew=xnew, xin=xin: e.tensor_tensor(out=xnew[:], in0=xnew[:], in1=xin[:], op=ALU.add),
                 [xnew, xin], [xnew])
            if not last:
                if is_ctx:
                    S.dma("sp", ctx1_d[gc * 128:(gc + 1) * 128, :], xnew[:], reads=[xnew], writes=["ctx1_d"])
                else:
                    S.dma("sp", x1_d[(gc - 2) * 128:(gc - 1) * 128, :], xnew[:], reads=[xnew], writes=["x1_d"])
            else:
                rms_rstd(xnew, tmpB, D, st8=stt)
                S.op("dve", lambda e, xnew=xnew, tmpB=tmpB, stt=stt: e.scalar_tensor_tensor(
                    out=tmpB[:], in0=xnew[:], scalar=stt[:, 0:1], in1=aux[:], op0=ALU.mult, op1=ALU.mult),
                    [xnew, stt, aux], [tmpB])
                S.dma("sp", out_d[(gc - 2) * 128:(gc - 1) * 128, :], tmpB[:], reads=[tmpB], writes=["out_d"])
        take(pf, 999)
        S.end()
        S.emit_scheduled("p3", overlap=OVL_C)

    build_consts()
    for l in range(n_layers):
        S.barrier()
        layer_setup(l)
        if stop_after == (l, "setup"):
            break
        pass0(l)
        if stop_after == (l, "p0"):
            break
        S.barrier()
        passA(l)
        if stop_after == (l, "pA"):
            break
        S.barrier()
        passQ(l)
        S.barrier()
        passC(l)
        S.barrier()
        passC3(l)
    S.wait_all("sp")
    build_program.stats = (S.ninst, S.nwaits, getattr(S, "sim_time", 0.0))
    build_program.sim_log = getattr(S, "sim_log", [])
    return nc


def _rope_table():
    f32 = np.float32
    tok = np.arange(T_LAT)
    rows = (tok // 64).astype(f32)
    cols = (tok % 64).astype(f32)
    inv = (f32(10000.0) ** (-(np.arange(0, 32, 2).astype(f32)) / f32(32))).astype(f32)
    ang = np.concatenate([rows[:, None] * inv[None, :], cols[:, None] * inv[None, :]], axis=-1).astype(f32)
    cos, sin = np.cos(ang).astype(f32), np.sin(ang).astype(f32)
    tab = np.zeros((128, 2, T_LAT), f32)
    for p in range(128):
        d = p % 64
        a, r = divmod(d, 32)
        j, i = divmod(r, 16)
        tab[p, 0] = cos[:, a * 16 + i]
        tab[p, 1] = sin[:, a * 16 + i] * (-1.0 if j == 0 else 1.0)
    return tab


def _prep_shared(inp):
    f = lambda a: np.ascontiguousarray(np.asarray(a, dtype=np.float32))
    cA, cC, cQ = np.array(_cols_A()), np.array(_cols_C()), np.array(_cols_Q())
    w_in = np.asarray(inp["w_in"], dtype=np.float32)
    w_out = np.asarray(inp["w_out"], dtype=np.float32)
    sh = {
        "norm_w": f(inp["norm_w"]).reshape(DEPTH, 1, D),
        "w_mod": f(inp["w_mod"]),
        "b_mod": f(inp["b_mod"]).reshape(DEPTH, 1, 3 * D),
        "wA": f(w_in[:, :, cA]),
        "wC": f(w_in[:, :, cC]),
        "wQ": f(w_in[:, :, cQ]),
        "wo_all": f(w_out),
        "convw": f(np.asarray(inp["ssd_conv_w"]).reshape(DEPTH, 3, 12, 128).transpose(0, 3, 2, 1)),
        "convb": f(np.asarray(inp["ssd_conv_b"]).reshape(DEPTH, 12, 128).transpose(0, 2, 1)),
        "scw": f(np.asarray(inp["sc_conv_w"]).reshape(DEPTH, 3, 4, 128).transpose(0, 3, 2, 1)),
        "dt_bias": f(inp["ssd_dt_bias"]).reshape(DEPTH, 1, 32),
        "a_log": f(inp["ssd_a_log"]).reshape(DEPTH, 1, 32),
        "ssd_d": f(inp["ssd_d"]).reshape(DEPTH, 1, 16),
        "ssd_norm_w": f(inp["ssd_norm_w"]).reshape(DEPTH, 1, D),
        "sink": f(inp["attn_sink"]).reshape(DEPTH, 1, 8),
        "final_norm_w": f(inp["final_norm_w"]).reshape(1, D),
        "ropecs": _rope_table(),
    }
    return sh


def _in_maps(inp, n_cores=8):
    sh = _prep_shared(inp)
    x = np.asarray(inp["x"], dtype=np.float32)
    c = np.asarray(inp["c"], dtype=np.float32)
    ctx = np.asarray(inp["ctx"], dtype=np.float32)
    c_ctx = np.asarray(inp["c_ctx"], dtype=np.float32)
    maps = []
    for core in range(n_cores):
        b = core % 4
        cc = np.stack([c[b].reshape(8, 128).T, c_ctx.reshape(8, 128).T], axis=-1)
        m = dict(sh)
        m["x"] = np.ascontiguousarray(x[b])
        m["ctx"] = np.ascontiguousarray(ctx[b])
        m["cc"] = np.ascontiguousarray(cc.astype(np.float32))
        maps.append(m)
    return maps


def kernel(x, c, ctx, c_ctx, norm_w, w_mod, b_mod, w_in, ssd_conv_w, ssd_conv_b, ssd_dt_bias, ssd_a_log, ssd_d,
           ssd_norm_w, sc_conv_w, attn_sink, w_out, final_norm_w):
    inp = dict(x=x, c=c, ctx=ctx, c_ctx=c_ctx, norm_w=norm_w, w_mod=w_mod, b_mod=b_mod, w_in=w_in,
               ssd_conv_w=ssd_conv_w, ssd_conv_b=ssd_conv_b, ssd_dt_bias=ssd_dt_bias, ssd_a_log=ssd_a_log,
               ssd_d=ssd_d, ssd_norm_w=ssd_norm_w, sc_conv_w=sc_conv_w, attn_sink=attn_sink, w_out=w_out,
               final_norm_w=final_norm_w)
    nc = build_program()
    maps = _in_maps(inp, 8)
    res = run_bass_kernel_spmd(nc, maps, core_ids=list(range(8)))
    out = np.stack([np.asarray(res.results[b]["out"], dtype=np.float32).reshape(T_LAT, D) for b in range(4)], axis=0)
    return out
```
